# Optimizing a Trainium2 kernel written in Bass

```python
import math
import jax, jax.numpy as jnp
from jax import lax
import numpy as np

D_MODEL = 1024
BATCH = 16
SEQ = 4096
DEPTH = 2

N_MIXERS = 2
N_RET = (DEPTH + 1) // 2
N_S5 = DEPTH // 2
RET_HEADS = 4
RET_DK = D_MODEL // RET_HEADS
RET_DV = 2 * RET_DK
RET_QK = RET_HEADS * RET_DK
RET_V = RET_HEADS * RET_DV
RET_PROJ = 2 * RET_QK + 2 * RET_V
RET_CHUNK = 128
ROPE_BASE = 10000.0
S5_GROUP = 16
S5_GROUPS = D_MODEL // S5_GROUP
S5_STATE = 64
DT_MIN = 1e-3
DT_MAX = 1e-1
D_FF = 2816
CONV_W = 3
EPS = 1e-6

kernel_name = "hybrid_retention_s5_convffn_adaln"


def rms_norm(x):
    xf = x.astype(jnp.float32)
    return (xf * lax.rsqrt(jnp.mean(xf * xf, axis=-1, keepdims=True) + EPS)).astype(x.dtype)


def rotary(t, pos):
    half = t.shape[-1] // 2
    inv_freq = jnp.power(ROPE_BASE, -jnp.arange(half, dtype=jnp.float32) / half)
    ang = pos.astype(jnp.float32)[..., None] * inv_freq
    cos = jnp.cos(ang)[:, :, None, :]
    sin = jnp.sin(ang)[:, :, None, :]
    t1 = t[..., :half].astype(jnp.float32)
    t2 = t[..., half:].astype(jnp.float32)
    return jnp.concatenate([t1 * cos - t2 * sin, t1 * sin + t2 * cos], axis=-1).astype(t.dtype)


def retention_chunkwise(q, k, v):
    bsz, L = q.shape[0], q.shape[1]
    nc = L // RET_CHUNK
    log_gamma = jnp.log1p(-jnp.exp2(-5.0 - jnp.arange(RET_HEADS, dtype=jnp.float32)))
    idx = jnp.arange(RET_CHUNK, dtype=jnp.float32)
    diff = idx[:, None] - idx[None, :]
    intra = jnp.where(diff[None] >= 0,
                      jnp.exp(log_gamma[:, None, None] * jnp.maximum(diff, 0.0)[None]), 0.0)
    cross = jnp.exp(log_gamma[:, None] * (idx + 1.0))
    to_state = jnp.exp(log_gamma[:, None] * (RET_CHUNK - 1.0 - idx))
    chunk_decay = jnp.exp(log_gamma * RET_CHUNK)

    def to_chunks(t):
        return t.reshape(bsz, nc, RET_CHUNK, RET_HEADS, t.shape[-1]).transpose(1, 0, 3, 2, 4)

    def step(state, qkv):
        qc, kc, vc = qkv
        s = jnp.einsum('bhnd,bhmd->bhnm', qc, kc) * intra[None]
        o = (jnp.einsum('bhnm,bhmv->bhnv', s, vc)
             + jnp.einsum('bhnd,bhdv->bhnv', qc, state) * cross[None, :, :, None])
        state = (chunk_decay[None, :, None, None] * state
                 + jnp.einsum('bhmd,bhmv->bhdv', kc * to_state[None, :, :, None], vc))
        return state, o

    state0 = jnp.zeros((bsz, RET_HEADS, RET_DK, RET_DV), jnp.float32)
    _, o = lax.scan(step, state0, (to_chunks(q), to_chunks(k), to_chunks(v)))
    return o.transpose(1, 0, 3, 2, 4).reshape(bsz, L, RET_HEADS, RET_DV)


def retention_mixer(h, pos, w_in, w_out):
    bsz, L, _ = h.shape
    proj = h @ w_in
    q, k, v, g = jnp.split(proj, [RET_QK, 2 * RET_QK, 2 * RET_QK + RET_V], axis=-1)
    q = rotary(q.reshape(bsz, L, RET_HEADS, RET_DK), pos)
    k = rotary(k.reshape(bsz, L, RET_HEADS, RET_DK), pos) * (RET_DK ** -0.5)
    v = v.reshape(bsz, L, RET_HEADS, RET_DV)
    o = retention_chunkwise(q.astype(jnp.float32), k.astype(jnp.float32), v.astype(jnp.float32))
    mu = jnp.mean(o, axis=-1, keepdims=True)
    oc = o - mu
    o = oc * lax.rsqrt(jnp.mean(oc * oc, axis=-1, keepdims=True) + EPS)
    o = o.reshape(bsz, L, RET_V).astype(h.dtype)
    return (jax.nn.silu(g) * o) @ w_out


def s5_mixer(h, w_in, lam_re, lam_im, log_dt, b_re, b_im, c_re, c_im, d_skip, w_glu):
    bsz, L, _ = h.shape
    u = (h @ w_in).astype(jnp.float32)
    ug = u.reshape(bsz, L, S5_GROUPS, S5_GROUP)
    dt = jnp.exp(log_dt.astype(jnp.float32))[:, None]
    lr = lam_re.astype(jnp.float32)
    li = lam_im.astype(jnp.float32)
    mag = jnp.exp(lr * dt)
    ar = mag * jnp.cos(li * dt)
    ai = mag * jnp.sin(li * dt)
    nr = ar - 1.0
    den = lr * lr + li * li
    fr = (nr * lr + ai * li) / den
    fi = (ai * lr - nr * li) / den
    br = b_re.astype(jnp.float32)
    bi = b_im.astype(jnp.float32)
    bbr = fr[..., None] * br - fi[..., None] * bi
    bbi = fr[..., None] * bi + fi[..., None] * br
    bu_r = jnp.einsum('blgi,gpi->lbgp', ug, bbr)
    bu_i = jnp.einsum('blgi,gpi->lbgp', ug, bbi)
    a_r = jnp.broadcast_to(ar[None, None], (L, 1, S5_GROUPS, S5_STATE))
    a_i = jnp.broadcast_to(ai[None, None], (L, 1, S5_GROUPS, S5_STATE))

    def combine(e1, e2):
        a1r, a1i, b1r, b1i = e1
        a2r, a2i, b2r, b2i = e2
        return (a2r * a1r - a2i * a1i,
                a2r * a1i + a2i * a1r,
                a2r * b1r - a2i * b1i + b2r,
                a2r * b1i + a2i * b1r + b2i)

    _, _, s_r, s_i = lax.associative_scan(combine, (a_r, a_i, bu_r, bu_i), axis=0)
    y = (jnp.einsum('lbgp,gip->blgi', s_r, c_re.astype(jnp.float32))
         - jnp.einsum('lbgp,gip->blgi', s_i, c_im.astype(jnp.float32)))
    y = y.reshape(bsz, L, D_MODEL) + d_skip.astype(jnp.float32) * u
    y = jax.nn.gelu(y).astype(h.dtype)
    ya, yb = jnp.split(y @ w_glu, 2, axis=-1)
    return ya * jax.nn.sigmoid(yb)


def conv_ffn(h, w_up, conv_w, conv_b, w_down):
    val, gate = jnp.split(h @ w_up, 2, axis=-1)
    gate = lax.conv_general_dilated(gate, conv_w.astype(gate.dtype), window_strides=(1,),
                                    padding=[(CONV_W - 1, 0)],
                                    dimension_numbers=('NWC', 'WIO', 'NWC'),
                                    feature_group_count=D_FF) + conv_b
    return (jax.nn.silu(gate) * val) @ w_down


def setup_inputs(seed: int = 0) -> dict:
    key = jax.random.key(seed)
    ks = jax.random.split(key, 24)
    f32 = jnp.float32
    nrm = lambda k, shape, s: jax.random.normal(k, shape, f32) * s
    x = nrm(ks[0], (BATCH, SEQ, D_MODEL), 1.0)
    c = nrm(ks[1], (BATCH, D_MODEL), 1.0)
    offset = jax.random.randint(ks[2], (BATCH, 1), 0, 1024, dtype=jnp.int32)
    pos = offset + jnp.arange(SEQ, dtype=jnp.int32)[None, :]
    ada_w = nrm(ks[3], (DEPTH, D_MODEL, 6 * D_MODEL), D_MODEL ** -0.5)
    ada_b = nrm(ks[4], (DEPTH, 6 * D_MODEL), 0.02)
    ret_w_in = nrm(ks[5], (N_RET, D_MODEL, RET_PROJ), D_MODEL ** -0.5)
    ret_w_out = nrm(ks[6], (N_RET, RET_V, D_MODEL), RET_V ** -0.5)
    s5_w_in = nrm(ks[7], (N_S5, D_MODEL, D_MODEL), D_MODEL ** -0.5)
    s5_lam_re = -0.5 * jnp.exp(nrm(ks[8], (N_S5, S5_GROUPS, S5_STATE), 0.02))
    s5_lam_im = (math.pi * jnp.arange(S5_STATE, dtype=f32))[None, None, :] + nrm(ks[9], (N_S5, S5_GROUPS, S5_STATE), 0.01)
    s5_log_dt = jax.random.uniform(ks[10], (N_S5, S5_GROUPS), f32, math.log(DT_MIN), math.log(DT_MAX))
    s5_b_re = nrm(ks[11], (N_S5, S5_GROUPS, S5_STATE, S5_GROUP), (2 * S5_GROUP) ** -0.5)
    s5_b_im = nrm(ks[12], (N_S5, S5_GROUPS, S5_STATE, S5_GROUP), (2 * S5_GROUP) ** -0.5)
    s5_c_re = nrm(ks[13], (N_S5, S5_GROUPS, S5_GROUP, S5_STATE), S5_STATE ** -0.5)
    s5_c_im = nrm(ks[14], (N_S5, S5_GROUPS, S5_GROUP, S5_STATE), S5_STATE ** -0.5)
    s5_d = nrm(ks[15], (N_S5, D_MODEL), 1.0)
    s5_w_glu = nrm(ks[16], (N_S5, D_MODEL, 2 * D_MODEL), D_MODEL ** -0.5)
    ffn_w_up = nrm(ks[17], (DEPTH, D_MODEL, 2 * D_FF), D_MODEL ** -0.5)
    ffn_conv_w = nrm(ks[18], (DEPTH, CONV_W, 1, D_FF), CONV_W ** -0.5)
    ffn_conv_b = nrm(ks[19], (DEPTH, D_FF), 0.02)
    ffn_w_down = nrm(ks[20], (DEPTH, D_FF, D_MODEL), D_FF ** -0.5)
    final_norm_g = 1.0 + nrm(ks[21], (D_MODEL,), 0.02)
    return {"x": x, "c": c, "pos": pos, "ada_w": ada_w, "ada_b": ada_b,
            "ret_w_in": ret_w_in, "ret_w_out": ret_w_out,
            "s5_w_in": s5_w_in, "s5_lam_re": s5_lam_re, "s5_lam_im": s5_lam_im,
            "s5_log_dt": s5_log_dt, "s5_b_re": s5_b_re, "s5_b_im": s5_b_im,
            "s5_c_re": s5_c_re, "s5_c_im": s5_c_im, "s5_d": s5_d, "s5_w_glu": s5_w_glu,
            "ffn_w_up": ffn_w_up, "ffn_conv_w": ffn_conv_w, "ffn_conv_b": ffn_conv_b,
            "ffn_w_down": ffn_w_down, "final_norm_g": final_norm_g}


def reference(x, c, pos, ada_w, ada_b, ret_w_in, ret_w_out, s5_w_in, s5_lam_re, s5_lam_im,
              s5_log_dt, s5_b_re, s5_b_im, s5_c_re, s5_c_im, s5_d, s5_w_glu,
              ffn_w_up, ffn_conv_w, ffn_conv_b, ffn_w_down, final_norm_g):
    cond = jax.nn.silu(c)
    for i in range(DEPTH):
        mod = cond @ ada_w[i] + ada_b[i]
        sh1, sc1, g1, sh2, sc2, g2 = jnp.split(mod[:, None, :], 6, axis=-1)
        h = rms_norm(x) * (1.0 + sc1) + sh1
        j = i // N_MIXERS
        if i % N_MIXERS == 0:
            y = retention_mixer(h, pos, ret_w_in[j], ret_w_out[j])
        else:
            y = s5_mixer(h, s5_w_in[j], s5_lam_re[j], s5_lam_im[j], s5_log_dt[j],
                         s5_b_re[j], s5_b_im[j], s5_c_re[j], s5_c_im[j], s5_d[j], s5_w_glu[j])
        x = x + g1 * y
        h = rms_norm(x) * (1.0 + sc2) + sh2
        x = x + g2 * conv_ffn(h, ffn_w_up[i], ffn_conv_w[i], ffn_conv_b[i], ffn_w_down[i])
    return rms_norm(x) * final_norm_g
```

```python
import math
import os
import numpy as np
import concourse.bass as bass
import concourse.mybir as mybir
from concourse.bass_utils import run_bass_kernel_spmd

F32 = mybir.dt.float32
BF16 = mybir.dt.bfloat16
I32 = mybir.dt.int32
AF = mybir.ActivationFunctionType
ALU = mybir.AluOpType

D = 1024
DFF = 2816
NJ = 22
H = 4
EPS = 1e-6
TWO_PI = float(2 * np.pi)
PI = float(np.pi)

ENGS = ("pe", "act", "dve", "pool", "sp")


class _Op:
    __slots__ = ("eng", "fn", "deps", "needs_inc", "sem", "val", "is_dma")

    def __init__(self, eng, fn, is_dma):
        self.eng = eng
        self.fn = fn
        self.deps = []
        self.needs_inc = is_dma
        self.sem = None
        self.val = 0
        self.is_dma = is_dma


class Prog:
    def __init__(self, nc, n_dma_sems=24):
        self.nc = nc
        self.ops = {e: [] for e in ENGS}
        self.last_w = {}
        self.readers = {}
        self.n_dma_sems = n_dma_sems
        self.dma_rr = 0
        self.dma_rr_sw = 0
        self.dma_last = {}
        self.dma_cnt = {}

    def add(self, eng, fn, reads=(), writes=(), is_dma=False):
        op = _Op(eng, fn, is_dma)
        deps = {}
        for k in reads:
            w = self.last_w.get(k)
            if w is not None:
                deps[id(w)] = w
        for k in writes:
            w = self.last_w.get(k)
            if w is not None:
                deps[id(w)] = w
            for r in self.readers.get(k, {}).values():
                deps[id(r)] = r
        for k in reads:
            self.readers.setdefault(k, {})[(eng, is_dma)] = op if not is_dma else op
            if is_dma:
                self.readers[k][(eng, id(op))] = op
        for k in writes:
            self.last_w[k] = op
            self.readers[k] = {}
        if is_dma:
            if eng == "pool":
                si = self.n_dma_sems - 8 + (self.dma_rr_sw % 8)
                self.dma_rr_sw += 1
            else:
                si = self.dma_rr % (self.n_dma_sems - 8)
                self.dma_rr += 1
            prev = self.dma_last.get(si)
            if prev is not None:
                deps[id(prev)] = prev
            self.dma_last[si] = op
            self.dma_cnt[si] = self.dma_cnt.get(si, 0) + 16
            op.sem = ("dma", si)
            op.val = self.dma_cnt[si]
        for d in deps.values():
            if d is op:
                continue
            if d.eng == "pe" and eng == "pe" and not d.is_dma and not is_dma:
                continue
            op.deps.append(d)
            d.needs_inc = True
        self.ops[eng].append(op)
        return op

    def emit(self):
        nc = self.nc
        sems = {}
        for e in ENGS:
            sems[("eng", e)] = nc.alloc_semaphore(f"s_{e}")
        for i in range(self.n_dma_sems):
            sems[("dma", i)] = nc.alloc_semaphore(f"s_dma{i}")
        for e in ENGS:
            c = 0
            for op in self.ops[e]:
                if op.is_dma:
                    continue
                if op.needs_inc:
                    c += 1
                    op.sem = ("eng", e)
                    op.val = c
            if os.environ.get("KDEBUG"):
                print("engine", e, "ops", len(self.ops[e]), "incs", c, flush=True)
        if os.environ.get("KDEBUG"):
            print("dma sem counts", self.dma_cnt, flush=True)
        engmap = {"pe": "tensor", "act": "scalar", "dve": "vector", "pool": "gpsimd", "sp": "sync"}
        with nc.Block() as block:
            for e in ENGS:
                ops = self.ops[e]
                if not ops:
                    continue

                def body(eng, ops=ops):
                    waited = {}
                    for op in ops:
                        for d in op.deps:
                            if waited.get(d.sem, 0) >= d.val:
                                continue
                            eng.wait_ge(sems[d.sem], d.val)
                            waited[d.sem] = d.val
                        ins = op.fn(eng)
                        if op.needs_inc:
                            ins.then_inc(sems[op.sem], 16 if op.is_dma else 1)
                    for op in ops:
                        if op.is_dma and waited.get(op.sem, 0) < op.val:
                            eng.wait_ge(sems[op.sem], op.val)
                            waited[op.sem] = op.val

                getattr(block, engmap[e])(body)


SLAB = 4096
def slab_plan():
    plan = []
    for h in range(H):
        for cg in ("qk", "v", "g"):
            plan.append(("win", h, cg))
    for mp in range(4):
        plan.append(("wout", mp))
    for jp in range(NJ // 2):
        plan.append(("up0", jp))
    for m in range(8):
        plan.append(("dn0", m))
    for half in range(2):
        plan.append(("s5in", half))
    for nm in ("Br", "Bi", "Cr", "Cin"):
        plan.append(("s5" + nm,))
    for mp in range(4):
        plan.append(("glu", mp))
    for jp in range(NJ // 2):
        plan.append(("up1", jp))
    for m in range(8):
        plan.append(("dn1", m))
    return plan


PLAN = slab_plan()
NSLAB = len(PLAN)
SLAB_IDX = {k: i for i, k in enumerate(PLAN)}
CIN_IDX = SLAB_IDX[("s5Cin",)]


def host_slabs(inp):
    W = np.zeros((NSLAB, 128, SLAB), np.float32)
    w_in = inp["ret_w_in"][0].reshape(8, 128, 6144)
    for h in range(H):
        cols = {
            "qk": np.concatenate([w_in[:, :, h * 256:(h + 1) * 256], w_in[:, :, 1024 + h * 256:1024 + (h + 1) * 256]], axis=2),
            "v": w_in[:, :, 2048 + h * 512:2048 + (h + 1) * 512],
            "g": w_in[:, :, 4096 + h * 512:4096 + (h + 1) * 512],
        }
        for cg in ("qk", "v", "g"):
            W[SLAB_IDX[("win", h, cg)]] = cols[cg].transpose(1, 0, 2).reshape(128, SLAB)
    w_out = inp["ret_w_out"][0].reshape(16, 128, 1024)
    for mp in range(4):
        W[SLAB_IDX[("wout", mp)]] = w_out[:, :, mp * 256:(mp + 1) * 256].transpose(1, 0, 2).reshape(128, SLAB)
    for l in range(2):
        w_up = inp["ffn_w_up"][l].reshape(8, 128, 2 * DFF)
        for jp in range(NJ // 2):
            blk = np.zeros((128, 2, 8, 256), np.float32)
            for jj in range(2):
                j = jp * 2 + jj
                blk[:, jj, :, 0:128] = w_up[:, :, j * 128:(j + 1) * 128].transpose(1, 0, 2)
                blk[:, jj, :, 128:256] = w_up[:, :, DFF + j * 128:DFF + (j + 1) * 128].transpose(1, 0, 2)
            W[SLAB_IDX[(f"up{l}", jp)]] = blk.reshape(128, SLAB)
        w_dn = inp["ffn_w_down"][l].reshape(NJ, 128, 1024)
        for m in range(8):
            W[SLAB_IDX[(f"dn{l}", m)], :, 0:NJ * 128] = w_dn[:, :, m * 128:(m + 1) * 128].transpose(1, 0, 2).reshape(128, NJ * 128)
    s5in = inp["s5_w_in"][0].reshape(8, 128, 1024)
    for half in range(2):
        W[SLAB_IDX[("s5in", half)]] = s5in[:, :, half * 512:(half + 1) * 512].transpose(1, 0, 2).reshape(128, SLAB)
    for nm, key in (("Br", "s5_b_re"), ("Bi", "s5_b_im")):
        B = inp[key][0]
        blk = np.zeros((128, 32, 128), np.float32)
        for ct in range(32):
            for gp in range(2):
                g = 2 * ct + gp
                r0 = 32 * (ct % 4) + 16 * gp
                blk[r0:r0 + 16, ct, gp * 64:(gp + 1) * 64] = B[g].T
        W[SLAB_IDX[("s5" + nm,)]] = blk.reshape(128, SLAB)
    for nm, key in (("Cr", "s5_c_re"), ("Cin", "s5_c_im")):
        C = inp[key][0]
        blk = np.zeros((128, 32, 128), np.float32)
        for ct in range(32):
            for gp in range(2):
                g = 2 * ct + gp
                c0 = 32 * (ct % 4) + 16 * gp
                blk[gp * 64:(gp + 1) * 64, ct, c0:c0 + 16] = C[g].T
        W[SLAB_IDX[("s5" + nm,)]] = blk.reshape(128, SLAB)
    wg = inp["s5_w_glu"][0].reshape(8, 128, 2048)
    for mp in range(4):
        blk = np.zeros((128, 8, 2, 256), np.float32)
        for mm in range(2):
            m = mp * 2 + mm
            blk[:, :, mm, 0:128] = wg[:, :, m * 128:(m + 1) * 128].transpose(1, 0, 2)
            blk[:, :, mm, 128:256] = wg[:, :, 1024 + m * 128:1024 + (m + 1) * 128].transpose(1, 0, 2)
        W[SLAB_IDX[("glu", mp)]] = blk.reshape(128, SLAB)
    return W


def host_consts():
    half = 128
    inv_freq = np.power(np.float32(10000.0), -np.arange(half, dtype=np.float32) / np.float32(half)).astype(np.float32)
    c = {}
    c["ident"] = np.eye(128, dtype=np.float32)
    c["ones"] = np.ones((128, 128), np.float32)
    m = np.arange(128)
    c["mask01"] = (m[None, :] >= m[:, None]).astype(np.float32)
    c["invf"] = np.broadcast_to(inv_freq[None, :], (128, 128)).copy()
    lg = np.log1p(-np.exp2(-5.0 - np.arange(H, dtype=np.float64)))
    p1 = (np.arange(128, dtype=np.float64) + 1.0)[:, None]
    dq = np.exp(lg[None, :] * p1)
    dk = np.exp(-lg[None, :] * p1) * (256.0 ** -0.5)
    c["dqk"] = np.concatenate([dq, dk], axis=1).astype(np.float32)
    c["iota1"] = np.broadcast_to((np.arange(512, dtype=np.float32) + 1.0)[None, :], (128, 512)).copy()
    return c, [float(np.exp(lg[h] * 128.0)) for h in range(H)]


_, GCH = host_consts()


def build_program(NS, L, stop_after=None, gelu_mode=None):
    if gelu_mode is None:
        gelu_mode = os.environ.get("K_GELU", "act")
    NCH = L // 512
    NTT = L // 128
    nc = bass.Bass("TRN2", target_bir_lowering=False)
    P = Prog(nc)

    def din(name, shape, dt=F32):
        return nc.dram_tensor(name, list(shape), dt, kind="ExternalInput").ap()

    xT_in = din("xT", [NS, D, L])
    cT_in = din("cT", [128, 8, NS])
    pos_in = din("pos", [NS, 128, NTT], I32)
    adaw_in = din("ada_w", [2, 128, 8, 6144])
    adab_in = din("ada_b", [128, 2, 48])
    wslab_in = din("wslab", [NSLAB, 128, SLAB])
    s5p_in = din("s5p", [128, 3, 32])
    s5d_in = din("s5d", [128, 8])
    convw_in = din("conv_w", [128, 2, NJ, 3])
    convb_in = din("conv_b", [128, 2, NJ])
    fng_in = din("fng", [128, 8])
    ident_in = din("ident", [128, 128])
    ones_in = din("ones", [128, 128])
    mask_in = din("mask01", [128, 128])
    invf_in = din("invf", [128, 128])
    dqk_in = din("dqk", [128, 8])
    iota_in = din("iota1", [128, 512])
    outT = nc.dram_tensor("outT", [NS, D, L], F32, kind="ExternalOutput").ap()
    wb = nc.dram_tensor("wb", [NSLAB, 128, SLAB], BF16).ap()
    tabs = nc.dram_tensor("tabs", [32, 128, 4, 512], F32).ap()

    def sb(name, shape, dt=F32):
        return nc.alloc_sbuf_tensor("sb_" + name, list(shape), dt)

    banks = [nc.alloc_psum_tensor(f"pb{i}", [128, 512], F32) for i in range(8)]
    b7 = banks[7][:].bitcast(BF16)
    b1 = banks[1][:].bitcast(BF16)

    xT = sb("xT", [128, 8, 512])
    hT = sb("hT", [128, 8, 512], BF16)
    rstd = sb("rstd", [128, 512])
    sq = [sb(f"sq{i}", [128, 512], BF16) for i in range(2)]
    tmpf = [sb(f"tmpf{i}", [128, 512]) for i in range(2)]
    NB = 6
    ring = [sb(f"ring{i}", [128, SLAB], BF16) for i in range(NB)]
    modT = sb("modT", [128, 2, 48, NS])
    cond = sb("cond", [128, 8, NS], BF16)
    cin = sb("cin", [128, 8, NS])
    adab = sb("adab", [128, 2, 48])
    ident = sb("ident", [128, 128], BF16)
    ones = sb("ones", [128, 128], BF16)
    mask01 = sb("mask01", [128, 128])
    invf = sb("invf", [128, 128])
    dqk = sb("dqk", [128, 8])
    posi = sb("posi", [128, NS, NTT], I32)
    posf = sb("posf", [128, NS, NTT])
    convw = sb("convw", [128, 2, NJ, 3])
    convb = sb("convb", [128, 2, NJ])
    fng = sb("fng", [128, 8])
    s5d = sb("s5d", [128, 8])
    s5p = sb("s5p", [128, 3, 32])
    rho = sb("rho", [128, 32])
    rABCD = sb("rABCD", [128, 4, 2, 128])
    ang = rABCD
    cs = sb("cs", [128, 4, 2, 128])
    mm4 = sb("mm4", [128, 4, 512])
    m1 = mm4[:, 0, :]; m2 = mm4[:, 1, :]; m3 = mm4[:, 2, :]; m4 = mm4[:, 3, :]
    oraw = mm4[:].rearrange("p a c -> p (a c)").bitcast(BF16).rearrange("p (a c) -> p a c", a=8)
    angm = mm4[:, 0:2, :]
    angi = mm4[:, 2:4, :].bitcast(I32)
    rA = rABCD[:, 0]; rB = rABCD[:, 1]; rC = rABCD[:, 2]; rD = rABCD[:, 3]
    RK = ["rA", "rB", "rC", "rD"]
    rot = sb("rot", [128, 2, 2, 128])
    qk_tm = [sb(f"qk_tm{i}", [128, 2, 256], BF16) for i in range(2)]
    qkT = [sb(f"qkT{i}", [128, 4, 128], BF16) for i in range(2)]
    v_sb = [sb(f"v_sb{i}", [128, 512], BF16) for i in range(2)]
    g_sb4 = sb("g_sb4", [128, 4, 512], BF16)
    st6x = sb("st6x", [128, 2, 4, 6]); mv4 = sb("mv4", [128, 2, 4, 2])
    gn = sb("gn", [128, 2, 6, 4])
    sT_sb = sb("sT_sb", [128, 128], BF16)
    Rst = sb("Rst", [128, H, 2, 512])
    Sbs = [sb(f"Sb{i}", [128, 2, 512], BF16) for i in range(2)]
    st6 = sb("st6", [128, 6]); mv = sb("mv", [128, 4])
    yn = sb("yn", [128, 512], BF16)
    y_tm = sb("y_tm", [128, 512], BF16)
    aT = sb("aT", [128, NJ, 512], BF16)
    yT = aT[:, 0:16, :]
    gbuf = [sb(f"gbuf{i}", [128, 514]) for i in range(2)]
    acc = [sb(f"acc{i}", [128, 512]) for i in range(2)]
    sil = [sb(f"sil{i}", [128, 512]) for i in range(2)]
    tails = sb("tails", [128, 2, NJ, 2])
    uT_bf = aT[:, 0:8, :]
    tab = [sb(f"tab{i}", [128, 4, 512]) for i in range(2)]
    wri = sb("wri", [128, 2, 512])
    zri = sb("zri", [128, 2, 512])
    s_bf = [sb(f"s_bf{i}", [128, 4, 512], BF16) for i in range(2)]
    cwk = sb("cwk", [128, 4])
    carry = sb("carry", [128, 32, 2])
    yg = hT

    PL = os.environ.get("K_POOL", "dve")

    def MM(out, lhsT, rhs, start, stop, r, w):
        P.add("pe", lambda e: e.matmul(out, lhsT, rhs, start=start, stop=stop), reads=r, writes=w)

    def TR(out, in_, r, w):
        P.add("pe", lambda e: e.transpose(out, in_, ident[:]), reads=list(r) + ["ident"], writes=w)

    def ACT(out, in_, func, r, w, bias=0.0, scale=1.0):
        P.add("act", lambda e: e.activation(out=out, in_=in_, func=func, bias=bias, scale=scale), reads=r, writes=w)

    def TT(eng, out, in0, in1, op, r, w):
        P.add(eng, lambda e: e.tensor_tensor(out=out, in0=in0, in1=in1, op=op), reads=r, writes=w)

    def TS(eng, out, in0, s1, s2, op0, op1, r, w):
        if s2 is None:
            P.add(eng, lambda e: e.tensor_scalar(out=out, in0=in0, scalar1=s1, scalar2=None, op0=op0), reads=r, writes=w)
        else:
            P.add(eng, lambda e: e.tensor_scalar(out=out, in0=in0, scalar1=s1, scalar2=s2, op0=op0, op1=op1), reads=r, writes=w)

    def STT(eng, out, in0, scalar, in1, op0, op1, r, w):
        P.add(eng, lambda e: e.scalar_tensor_tensor(out=out, in0=in0, scalar=scalar, in1=in1, op0=op0, op1=op1), reads=r, writes=w)

    def CP(eng, out, in_, r, w):
        P.add(eng, lambda e: e.tensor_copy(out=out, in_=in_), reads=r, writes=w)

    def DMA(eng, out, in_, r, w):
        P.add(eng, lambda e: e.dma_start(out=out, in_=in_), reads=r, writes=w, is_dma=True)

    def range_reduce_sin(eng, x, xi, xm, out, kx, ki, km, kout):
        TS(eng, xi, x, float(1.0 / TWO_PI), None, ALU.mult, None, kx, ki)
        STT(eng, x, xi, -TWO_PI, x, ALU.mult, ALU.add, ki + kx, kx)
        TS(eng, xm, x, PI, -TWO_PI, ALU.is_gt, ALU.mult, kx, km)
        TT(eng, x, x, xm, ALU.add, kx + km, kx)
        TS(eng, xm, x, -PI, TWO_PI, ALU.is_lt, ALU.mult, kx, km)
        TT(eng, x, x, xm, ALU.add, kx + km, kx)
        ACT(out, x, AF.Sin, kx, kout)

    for (dst, src, nm) in ((mask01, mask_in, "mask01"), (invf, invf_in, "invf"), (dqk, dqk_in, "dqk"),
                           (adab, adab_in, "adab"), (cin, cT_in, "cin"), (convw, convw_in, "convw"),
                           (convb, convb_in, "convb"), (fng, fng_in, "fng"), (s5d, s5d_in, "s5d"),
                           (s5p, s5p_in, "s5p"), (posi, pos_in.rearrange("s p n -> p s n"), "posi")):
        DMA("sp", dst[:], src, [], [nm])
    DMA("pool", ident[:], ident_in, [], ["ident"])
    DMA("pool", ones[:], ones_in, [], ["ones"])
    GRP = 4
    for i0 in range(0, NSLAB, GRP):
        i1 = min(NSLAB, i0 + GRP)
        idxs = [i for i in range(i0, i1) if i != CIN_IDX]
        runs = []
        for i in idxs:
            if runs and runs[-1][1] == i:
                runs[-1][1] = i + 1
            else:
                runs.append([i, i + 1])
        for a, b in runs:
            DMA("pool", wb[a:b].rearrange("s p f -> (s p) f"), wslab_in[a:b].rearrange("s p f -> (s p) f"), [], [f"wb{i}" for i in range(a, b)])
    cst = aT[:].rearrange("p j c -> p (j c)").bitcast(F32)[:, 0:SLAB]
    DMA("sp", cst, wslab_in[CIN_IDX], [], ["aT"])
    TS("dve", ring[0][:], cst, -1.0, None, ALU.mult, None, ["aT"], ["ring0"])
    DMA("sp", wb[CIN_IDX], ring[0][:], ["ring0"], [f"wb{CIN_IDX}"])

    CP("dve", posf[:], posi[:], ["posi"], ["posf"])
    ACT(cond[:], cin[:], AF.Silu, ["cin"], ["cond"])
    for l in range(2):
        for cgp in range(12):
            slot = ring[1 + (cgp % 2)]
            key = f"ring{1 + (cgp % 2)}"
            DMA("pool", slot[:].rearrange("p (k c) -> p k c", k=8), adaw_in[l][:, :, cgp * 512:(cgp + 1) * 512], [], [key])
            sv = slot[:].rearrange("p (k c) -> p k c", k=8)
            for mm in range(4):
                m = cgp * 4 + mm
                for k in range(8):
                    MM(banks[0][:, m * NS:(m + 1) * NS], sv[:, k, mm * 128:(mm + 1) * 128], cond[:, k, :], k == 0, k == 7,
                       [key, "cond"], ["bank0"])
        bv = banks[0][:, 0:48 * NS].rearrange("p (m s) -> p m s", s=NS)
        for s in range(NS):
            TT("dve", modT[:, l, :, s], bv[:, :, s], adab[:, l, :], ALU.add, ["bank0", "adab"], ["modT"])
    for l in range(2):
        for c0 in (8, 32):
            TS("dve", modT[:, l, c0:c0 + 8, :], modT[:, l, c0:c0 + 8, :], 1.0, None, ALU.add, None, ["modT"], ["modT"])

    def mod(l, which, k, s):
        base = {"sh1": 0, "sc1": 8, "g1": 16, "sh2": 24, "sc2": 32, "g2": 40}[which]
        return modT[:, l, base + k, s:s + 1]

    lr = s5p[:, 0, :]; li = s5p[:, 1, :]; ldt = s5p[:, 2, :]
    sw = sb("s5w", [128, 16, 32])
    swi = sb("s5wi", [128, 2, 32], I32)
    dtp = sw[:, 0, :]; xx = sw[:, 1, :]; th = sw[:, 2, :]
    ACT(dtp, ldt, AF.Exp, ["s5p"], ["s5w"])
    TT("dve", xx, lr, dtp, ALU.mult, ["s5p", "s5w"], ["s5w"])
    TT("dve", th, li, dtp, ALU.mult, ["s5p", "s5w"], ["s5w"])
    ACT(rho[:], xx, AF.Exp, ["s5w"], ["rho"])
    TS("dve", sw[:, 3, :], th, 1.0, None, ALU.mult, None, ["s5w"], ["s5w"])
    TS("dve", sw[:, 4, :], th, float(PI / 2), None, ALU.add, None, ["s5w"], ["s5w"])
    range_reduce_sin("dve", sw[:, 3:5, :], swi[:, :, :], sw[:, 5:7, :], sw[:, 7:9, :], ["s5w"], ["s5wi"], ["s5w"], ["s5w"])
    sn0 = sw[:, 7, :]; cs0 = sw[:, 8, :]
    ar = sw[:, 9, :]; ai = sw[:, 10, :]
    TT("dve", ar, rho[:], cs0, ALU.mult, ["rho", "s5w"], ["s5w"])
    TT("dve", ai, rho[:], sn0, ALU.mult, ["rho", "s5w"], ["s5w"])
    nr = sw[:, 11, :]
    TS("dve", nr, ar, -1.0, None, ALU.add, None, ["s5w"], ["s5w"])
    den = sw[:, 12, :]; t0 = sw[:, 13, :]
    TT("dve", den, lr, lr, ALU.mult, ["s5p", "s5w"], ["s5w"])
    TT("dve", t0, li, li, ALU.mult, ["s5p", "s5w"], ["s5w"])
    TT("dve", den, den, t0, ALU.add, ["s5w"], ["s5w"])
    P.add("dve", lambda e: e.reciprocal(out=den, in_=den), reads=["s5w"], writes=["s5w"])
    fcoef = sb("fcoef", [128, 3, 32])
    TT("dve", t0, nr, lr, ALU.mult, ["s5w", "s5p"], ["s5w"])
    TT("dve", sw[:, 14, :], ai, li, ALU.mult, ["s5w", "s5p"], ["s5w"])
    TT("dve", t0, t0, sw[:, 14, :], ALU.add, ["s5w"], ["s5w"])
    TT("dve", fcoef[:, 0, :], t0, den, ALU.mult, ["s5w"], ["fcoef"])
    TT("dve", t0, ai, lr, ALU.mult, ["s5w", "s5p"], ["s5w"])
    TT("dve", sw[:, 14, :], nr, li, ALU.mult, ["s5w", "s5p"], ["s5w"])
    TT("dve", t0, t0, sw[:, 14, :], ALU.subtract, ["s5w"], ["s5w"])
    TT("dve", fcoef[:, 1, :], t0, den, ALU.mult, ["s5w"], ["fcoef"])
    TS("dve", fcoef[:, 2, :], fcoef[:, 0, :], -1.0, None, ALU.mult, None, ["fcoef"], ["fcoef"])
    iota1 = rstd
    DMA("sp", iota1[:], iota_in, [], ["rstd"])
    phx = zri; phi_ = angi
    phm = s_bf[0][:].rearrange("p a c -> p (a c)").bitcast(F32).rearrange("p (a c) -> p a c", a=2)
    thv = sb("thv", [128, 32])
    CP("dve", thv[:], th, ["s5w"], ["thv"])
    for ct in range(32):
        tb = tab[ct % 2]
        tk = f"tab{ct % 2}"
        TS("dve", phx[:, 0, :], iota1[:], thv[:, ct:ct + 1], None, ALU.mult, None, ["rstd", "thv"], ["zri"])
        TS("dve", phx[:, 1, :], phx[:, 0, :], float(PI / 2), None, ALU.add, None, ["zri"], ["zri"])
        range_reduce_sin("dve", phx[:], phi_, phm, wri[:], ["zri"], ["m3", "m4"], ["s_bf0"], ["wri"])
        CP(PL, tb[:, 3, :], wri[:, 0, :], ["wri"], [tk])
        CP(PL, tb[:, 2, :], wri[:, 1, :], ["wri"], [tk])
        TS("dve", m1, wri[:, 1, :], fcoef[:, 0, ct:ct + 1], None, ALU.mult, None, ["wri", "fcoef"], ["m1"])
        STT("dve", tb[:, 0, :], wri[:, 0, :], fcoef[:, 1, ct:ct + 1], m1, ALU.mult, ALU.add, ["wri", "fcoef", "m1"], [tk])
        TS("dve", m2, wri[:, 1, :], fcoef[:, 1, ct:ct + 1], None, ALU.mult, None, ["wri", "fcoef"], ["m2"])
        STT("dve", tb[:, 1, :], wri[:, 0, :], fcoef[:, 2, ct:ct + 1], m2, ALU.mult, ALU.add, ["wri", "fcoef", "m2"], [tk])
        DMA("sp", tabs[ct], tb[:], [tk], [f"tabs{ct}"])

    rs = {"n": 0, "loaded": 0, "total": NS * NCH * NSLAB}

    def slab_cols(key):
        return NJ * 128 if key[0] in ("dn0", "dn1") else SLAB

    def ring_next(expect, live=1):
        n = rs["n"]
        assert PLAN[n % NSLAB] == expect, (PLAN[n % NSLAB], expect)
        lim = min(rs["total"], n + NB - live + 1)
        while rs["loaded"] < lim:
            j = rs["loaded"]
            si = j % NSLAB
            nc_ = slab_cols(PLAN[si])
            DMA("sp", ring[j % NB][:, 0:nc_], wb[si][:, 0:nc_], [f"wb{si}"], [f"ring{j % NB}"])
            rs["loaded"] += 1
        rs["n"] = n + 1
        return ring[n % NB], f"ring{n % NB}"

    def norm_mod(l, which_sc, which_sh, s):
        for k in range(8):
            if k % 2 == 0:
                ACT(sq[0][:], xT[:, k, :], AF.Square, ["xT"], ["sq0"])
            else:
                TT("dve", sq[1][:], xT[:, k, :], xT[:, k, :], ALU.mult, ["xT"], ["sq1"])
            MM(banks[5][:], ones[:], sq[k % 2][:], k == 0, k == 7, ["ones", f"sq{k % 2}"], ["bank5"])
        ACT(rstd[:], banks[5][:], AF.Sqrt, ["bank5"], ["rstd"], bias=EPS, scale=1.0 / D)
        P.add("dve", lambda e: e.reciprocal(out=rstd[:], in_=rstd[:]), reads=["rstd"], writes=["rstd"])
        for k in range(8):
            t = tmpf[k % 2]
            TT("dve", t[:], xT[:, k, :], rstd[:], ALU.mult, ["xT", "rstd"], [f"tmpf{k % 2}"])
            ACT(hT[:, k, :], t[:], AF.Identity, [f"tmpf{k % 2}", "modT"], ["hT"],
                bias=mod(l, which_sh, k, s), scale=mod(l, which_sc, k, s))

    def ffn(l, s):
        norm_mod(l, "sc2", "sh2", s)
        for jp in range(NJ // 2):
            slot, key = ring_next((f"up{l}", jp))
            wv = slot[:].rearrange("p (j k c) -> p j k c", j=2, k=8)
            for jj in range(2):
                j = jp * 2 + jj
                pv = banks[(j % 3) * 2]; pg = banks[(j % 3) * 2 + 1]
                kv = f"bank{(j % 3) * 2}"; kg = f"bank{(j % 3) * 2 + 1}"
                for k in range(8):
                    MM(pv[:], wv[:, jj, k, 0:128], hT[:, k, :], k == 0, k == 7, [key, "hT"], [kv])
                for k in range(8):
                    MM(pg[:], wv[:, jj, k, 128:256], hT[:, k, :], k == 0, k == 7, [key, "hT"], [kg])
                gb = gbuf[j % 2]; gk = f"gbuf{j % 2}"
                ac = acc[j % 2]; ak = f"acc{j % 2}"
                sl = sil[j % 2]; sk = f"sil{j % 2}"
                ACT(gb[:, 0:2], tails[:, l, j, :], AF.Copy, ["tails"], [gk])
                ACT(gb[:, 2:514], pg[:], AF.Copy, [kg], [gk])
                ACT(ac[:], pg[:], AF.Identity, [kg, "convw", "convb"], [ak], bias=convb[:, l, j:j + 1], scale=convw[:, l, j, 2:3])
                ACT(tails[:, l, j, :], gb[:, 512:514], AF.Copy, [gk], ["tails"])
                STT("dve", ac[:], gb[:, 1:513], convw[:, l, j, 1:2], ac[:], ALU.mult, ALU.add, [gk, "convw", ak], [ak])
                STT("dve", ac[:], gb[:, 0:512], convw[:, l, j, 0:1], ac[:], ALU.mult, ALU.add, [gk, "convw", ak], [ak])
                ACT(sl[:], ac[:], AF.Silu, [ak], [sk])
                TT("dve", aT[:, j, :], sl[:], pv[:], ALU.mult, [sk, kv], ["aT"])
        for m in range(8):
            slot, key = ring_next((f"dn{l}", m))
            wv = slot[:, 0:NJ * 128].rearrange("p (j c) -> p j c", j=NJ)
            pb = banks[6 + (m % 2)]; pk = f"bank{6 + (m % 2)}" if m % 2 == 0 else "bank7"
            for j in range(NJ):
                MM(pb[:], wv[:, j, :], aT[:, j, :], j == 0, j == NJ - 1, [key, "aT"], [pk])
            STT("dve", xT[:, m, :], pb[:], mod(l, "g2", m, s), xT[:, m, :], ALU.mult, ALU.add, [pk, "modT", "xT"], ["xT"])

    def retention(s, c):
        l = 0
        norm_mod(l, "sc1", "sh1", s)
        for nt in range(4):
            ntg = c * 4 + nt
            TS("dve", ang[:, nt, 0, :], invf[:], posf[:, s, ntg:ntg + 1], None, ALU.mult, None, ["invf", "posf"], RK)
            TS("dve", ang[:, nt, 1, :], ang[:, nt, 0, :], float(PI / 2), None, ALU.add, None, RK, RK)
        range_reduce_sin("dve", ang[:].rearrange("p a b c -> p (a b c)"), angi.rearrange("p a c -> p (a c)"),
                         angm.rearrange("p a c -> p (a c)"), cs[:].rearrange("p a b c -> p (a b c)"),
                         RK, ["m3", "m4"], ["m1", "m2"], ["cs"])
        slabs_h = {}

        def stage_A(h, i):
            if i == 0:
                slabs_h[h] = [ring_next(("win", h, cg), live=li + 1) for li, cg in enumerate(("qk", "v", "g"))]
            slabs = slabs_h[h]
            par = i % 2
            for ci in range(3):
                slot, key = slabs[ci]
                wv = slot[:].rearrange("p (k c) -> p k c", k=8)
                for k in range(8):
                    MM(banks[2 + ci][:], hT[:, k, i * 128:(i + 1) * 128], wv[:, k, :], k == 0, k == 7, ["hT", key], [f"bank{2 + ci}"])
            qv = banks[2][:].rearrange("p (a b c) -> p a b c", a=2, b=2)
            t1 = qv[:, :, 0, :]; t2 = qv[:, :, 1, :]
            cosb = cs[:, i, 1, :].unsqueeze(1).to_broadcast([128, 2, 128])
            sinb = cs[:, i, 0, :].unsqueeze(1).to_broadcast([128, 2, 128])
            TT("dve", rA, t1, cosb, ALU.mult, ["bank2", "cs"], ["rA"])
            TT("dve", rB, t2, sinb, ALU.mult, ["bank2", "cs"], ["rB"])
            TT("dve", rC, t1, sinb, ALU.mult, ["bank2", "cs"], ["rC"])
            TT("dve", rD, t2, cosb, ALU.mult, ["bank2", "cs"], ["rD"])
            TT("dve", rot[:, :, 0, :], rA, rB, ALU.subtract, ["rA", "rB"], ["rot"])
            TT("dve", rot[:, :, 1, :], rC, rD, ALU.add, ["rC", "rD"], ["rot"])
            qt = qk_tm[par]; qtk = f"qk_tm{par}"
            ACT(qt[:, 0, :], rot[:, 0, :, :].rearrange("p b c -> p (b c)"), AF.Identity, ["rot", "dqk"], [qtk], scale=dqk[:, h:h + 1])
            ACT(qt[:, 1, :], rot[:, 1, :, :].rearrange("p b c -> p (b c)"), AF.Identity, ["rot", "dqk"], [qtk], scale=dqk[:, 4 + h:5 + h])
            ACT(v_sb[par][:], banks[3][:], AF.Copy, ["bank3"], [f"v_sb{par}"])
            gbuf_h = g_sb4 if h % 2 == 0 else s_bf[1]
            ACT(gbuf_h[:, i, :], banks[4][:], AF.Silu, ["bank4"], [f"g{h % 2}_{i}"])
            for a in range(2):
                for dd in range(2):
                    TR(b7[:, (a * 2 + dd) * 128:(a * 2 + dd + 1) * 128], qt[:, a, dd * 128:(dd + 1) * 128], [qtk], ["bank7"])
            qT = qkT[par]; qTk = f"qkT{par}"
            CP("dve", qT[:].rearrange("p a c -> p (a c)"), b7[:, 0:512], ["bank7"], [qTk])

        def stage_B(h, i):
            par = i % 2
            qt = qk_tm[par]; qtk = f"qk_tm{par}"
            qT = qkT[par]; qTk = f"qkT{par}"
            Sin = Sbs[i % 2]; Sink = f"Sb{i % 2}"
            Sout = Sbs[(i + 1) % 2]; Soutk = f"Sb{(i + 1) % 2}"
            if i == 0:
                for dd in range(2):
                    ACT(Sin[:, dd, :], Rst[:, h, dd, :], AF.Copy, ["Rst"], [Sink], scale=GCH[h])
            for dd in range(2):
                MM(banks[5][:, 0:128], qT[:, 2 + dd, :], qT[:, dd, :], dd == 0, dd == 1, [qTk], ["bank5"])
            TT("dve", sT_sb[:], banks[5][:, 0:128], mask01[:], ALU.mult, ["bank5", "mask01"], ["sT_sb"])
            for dd in range(2):
                MM(banks[6][:], qT[:, dd, :], Sin[:, dd, :], dd == 0, False, [qTk, Sink], ["bank6"])
            MM(banks[6][:], sT_sb[:], v_sb[par][:], False, True, ["sT_sb", f"v_sb{par}"], ["bank6"])
            for dd in range(2):
                MM(banks[0][:], qt[:, 1, dd * 128:(dd + 1) * 128], v_sb[par][:], True, True, [qtk, f"v_sb{par}"], ["bank0"])
                STT("dve", Rst[:, h, dd, :], Rst[:, h, dd, :], GCH[h], banks[0][:], ALU.mult, ALU.add, ["Rst", "bank0"], ["Rst"])
                if i < 3:
                    ACT(Sout[:, dd, :], Rst[:, h, dd, :], AF.Copy, ["Rst"], [Soutk], scale=GCH[h])
            hp = h % 2
            P.add("dve", lambda e: e.bn_stats(out=st6x[:, hp, i, :], in_=banks[6][:]), reads=["bank6"], writes=[f"st6x{hp}"])
            ACT(oraw[:, hp * 4 + i, :], banks[6][:], AF.Copy, ["bank6"], [f"or{hp}_{i}"])

        def stage_Chead(h):
            hp = h % 2
            var = gn[:, hp, 2, :]; rs_ = gn[:, hp, 3, :]; nb = gn[:, hp, 4, :]
            for i in range(4):
                P.add("dve", lambda e, i=i: e.bn_aggr(out=mv4[:, hp, i, :], in_=st6x[:, hp, i, :]), reads=[f"st6x{hp}"], writes=[f"mv4{hp}"])
            ACT(var, mv4[:, hp, :, 1], AF.Sqrt, [f"mv4{hp}"], [f"gn{hp}"], bias=EPS, scale=1.0)
            P.add("dve", lambda e: e.reciprocal(out=rs_, in_=var), reads=[f"gn{hp}"], writes=[f"gn{hp}"])
            STT("dve", nb, mv4[:, hp, :, 0], -1.0, rs_, ALU.mult, ALU.mult, [f"gn{hp}", f"mv4{hp}"], [f"gn{hp}"])

        def stage_Ctile_pre(h, i):
            hp = h % 2
            gbuf_h = g_sb4 if hp == 0 else s_bf[1]
            ACT(yn[:], oraw[:, hp * 4 + i, :], AF.Identity, [f"or{hp}_{i}", f"gn{hp}"], ["yn"], bias=gn[:, hp, 4, i:i + 1], scale=gn[:, hp, 3, i:i + 1])
            TT("dve", y_tm[:], yn[:], gbuf_h[:, i, :], ALU.mult, ["yn", f"g{hp}_{i}"], ["y_tm"])

        def stage_Ctile_post(h, i):
            for vt in range(4):
                TR(b1[:, vt * 128:(vt + 1) * 128], y_tm[:, vt * 128:(vt + 1) * 128], ["y_tm"], ["bank1"])
            ACT(yT[:, h * 4:(h + 1) * 4, i * 128:(i + 1) * 128], b1[:, 0:512].rearrange("p (a c) -> p a c", a=4), AF.Copy, ["bank1"], ["aT"])

        order = [(h, i) for h in range(H) for i in range(4)]
        stage_A(*order[0])
        for idx, (h, i) in enumerate(order):
            if h > 0:
                stage_Ctile_pre(h - 1, i)
            if idx + 1 < len(order):
                stage_A(*order[idx + 1])
            if h > 0:
                stage_Ctile_post(h - 1, i)
            stage_B(h, i)
            if i == 3:
                stage_Chead(h)
        for i in range(4):
            stage_Ctile_pre(H - 1, i)
            stage_Ctile_post(H - 1, i)
        for mp in range(4):
            slot, key = ring_next(("wout", mp))
            wv = slot[:].rearrange("p (k c) -> p k c", k=16)
            for mm in range(2):
                m = mp * 2 + mm
                pb = banks[m % 2]; pk = f"bank{m % 2}"
                for kt in range(16):
                    MM(pb[:], wv[:, kt, mm * 128:(mm + 1) * 128], yT[:, kt, :], kt == 0, kt == 15, [key, "aT"], [pk])
                STT("dve", xT[:, m, :], pb[:], mod(l, "g1", m, s), xT[:, m, :], ALU.mult, ALU.add, [pk, "modT", "xT"], ["xT"])

    S5LV = int(os.environ.get("S5LV", "9"))

    def s5(s):
        l = 1
        norm_mod(l, "sc1", "sh1", s)
        for half in range(2):
            slot, key = ring_next(("s5in", half))
            wv = slot[:].rearrange("p (k c) -> p k c", k=8)
            for mm in range(4):
                m = half * 4 + mm
                pb = banks[m % 2]; pk = f"bank{m % 2}"
                for k in range(8):
                    MM(pb[:], wv[:, k, mm * 128:(mm + 1) * 128], hT[:, k, :], k == 0, k == 7, [key, "hT"], [pk])
                ACT(uT_bf[:, m, :], pb[:], AF.Copy, [pk], ["aT"])
        sBr, kBr = ring_next(("s5Br",), live=1); sBi, kBi = ring_next(("s5Bi",), live=2)
        sCr, kCr = ring_next(("s5Cr",), live=3); sCi, kCi = ring_next(("s5Cin",), live=4)
        vBr = sBr[:].rearrange("p (t c) -> p t c", t=32); vBi = sBi[:].rearrange("p (t c) -> p t c", t=32)
        vCr = sCr[:].rearrange("p (t c) -> p t c", t=32); vCi = sCi[:].rearrange("p (t c) -> p t c", t=32)
        def s5_B(ct):
            cht = ct // 4
            tb = tab[ct % 2]; tk = f"tab{ct % 2}"
            DMA("sp", tb[:], tabs[ct], [f"tabs{ct}"], [tk])
            pr = banks[2 + 2 * (ct % 2)]; pi = banks[3 + 2 * (ct % 2)]
            kr = f"bank{2 + 2 * (ct % 2)}"; ki = f"bank{3 + 2 * (ct % 2)}"
            MM(pr[:], vBr[:, ct, :], uT_bf[:, cht, :], True, True, [kBr, "aT"], [kr])
            MM(pi[:], vBi[:, ct, :], uT_bf[:, cht, :], True, True, [kBi, "aT"], [ki])

        s5_B(0)
        for ct in range(32):
            cht = ct // 4
            tb = tab[ct % 2]; tk = f"tab{ct % 2}"
            pr = banks[2 + 2 * (ct % 2)]; pi = banks[3 + 2 * (ct % 2)]
            kr = f"bank{2 + 2 * (ct % 2)}"; ki = f"bank{3 + 2 * (ct % 2)}"
            if ct + 1 < 32:
                s5_B(ct + 1)
            TT("dve", m1, pr[:], tb[:, 0, :], ALU.mult, [kr, tk], ["m1"])
            TT("dve", m2, pi[:], tb[:, 1, :], ALU.mult, [ki, tk], ["m2"])
            TT("dve", m3, pi[:], tb[:, 0, :], ALU.mult, [ki, tk], ["m3"])
            TT("dve", m4, pr[:], tb[:, 1, :], ALU.mult, [kr, tk], ["m4"])
            TT("dve", wri[:, 0, :], m1, m2, ALU.subtract, ["m1", "m2"], ["wri"])
            TT("dve", wri[:, 1, :], m3, m4, ALU.add, ["m3", "m4"], ["wri"])
            rb = rho[:, ct:ct + 1].to_broadcast([128, 512])
            for ri in range(2):
                P.add("dve", lambda e, ri=ri, rb=rb, ct=ct: e.tensor_tensor_scan(out=zri[:, ri, :], data0=rb, data1=wri[:, ri, :],
                                                                         initial=carry[:, ct, ri:ri + 1], op0=ALU.mult, op1=ALU.add),
                      reads=["rho", "wri", "carry"], writes=["zri"])
            pp = s_bf[ct % 2]; ppk = f"s_bf{ct % 2}"
            TT("dve", pp[:, 0, :], zri[:, 0, :], tb[:, 2, :], ALU.mult, ["zri", tk], [ppk])
            STT("dve", pp[:, 1, :], zri[:, 1, :], -1.0, tb[:, 3, :], ALU.mult, ALU.mult, ["zri", tk], [ppk])
            TT("dve", pp[:, 2, :], zri[:, 0, :], tb[:, 3, :], ALU.mult, ["zri", tk], [ppk])
            TT("dve", pp[:, 3, :], zri[:, 1, :], tb[:, 2, :], ALU.mult, ["zri", tk], [ppk])
            zr_l = zri[:, 0, 511:512]; zi_l = zri[:, 1, 511:512]
            cos_l = tb[:, 2, 511:512]; sin_l = tb[:, 3, 511:512]
            TS("dve", cwk[:, 0:1], zi_l, sin_l, None, ALU.mult, None, ["zri", tk], ["cwk"])
            TS("dve", cwk[:, 1:2], zi_l, cos_l, None, ALU.mult, None, ["zri", tk], ["cwk"])
            STT("dve", carry[:, ct, 0:1], zr_l, cos_l, cwk[:, 0:1], ALU.mult, ALU.subtract, ["zri", tk, "cwk"], ["carry"])
            STT("dve", carry[:, ct, 1:2], zr_l, sin_l, cwk[:, 1:2], ALU.mult, ALU.add, ["zri", tk, "cwk"], ["carry"])
            MM(banks[6][:], vCr[:, ct, :], pp[:, 0, :], ct % 4 == 0, False, [kCr, ppk], ["bank6"])
            MM(banks[6][:], vCr[:, ct, :], pp[:, 1, :], False, False, [kCr, ppk], ["bank6"])
            MM(banks[6][:], vCi[:, ct, :], pp[:, 2, :], False, False, [kCi, ppk], ["bank6"])
            MM(banks[6][:], vCi[:, ct, :], pp[:, 3, :], False, ct % 4 == 3, [kCi, ppk], ["bank6"])
            if ct % 4 == 3:
                t = tmpf[0]
                STT("dve", t[:], uT_bf[:, cht, :], s5d[:, cht:cht + 1], banks[6][:], ALU.mult, ALU.add, ["aT", "s5d", "bank6"], ["tmpf0"])
                ACT(yg[:, cht, :], t[:], AF.Gelu_apprx_tanh, ["tmpf0"], ["hT"])
        for mp in range(4):
            slot, key = ring_next(("glu", mp))
            wv = slot[:].rearrange("p (k m c) -> p k m c", k=8, m=2)
            for mm in range(2 if S5LV >= 6 else 0):
                m = mp * 2 + mm
                pa = banks[(m % 2) * 2]; pbb = banks[(m % 2) * 2 + 1]
                ka = f"bank{(m % 2) * 2}"; kb = f"bank{(m % 2) * 2 + 1}"
                for k in range(8):
                    MM(pa[:], wv[:, k, mm, 0:128], yg[:, k, :], k == 0, k == 7, [key, "hT"], [ka])
                for k in range(8):
                    MM(pbb[:], wv[:, k, mm, 128:256], yg[:, k, :], k == 0, k == 7, [key, "hT"], [kb])
                t = tmpf[m % 2]; tkk = f"tmpf{m % 2}"
                ACT(t[:], pbb[:], AF.Sigmoid, [kb], [tkk])
                TT("dve", t[:], pa[:], t[:], ALU.mult, [ka, tkk], [tkk])
                STT("dve", xT[:, m, :], t[:], mod(l, "g1", m, s), xT[:, m, :], ALU.mult, ALU.add, [tkk, "modT", "xT"], ["xT"])

    def final_norm_out(s, c):
        for k in range(8):
            ACT(sq[k % 2][:], xT[:, k, :], AF.Square, ["xT"], [f"sq{k % 2}"])
            MM(banks[5][:], ones[:], sq[k % 2][:], k == 0, k == 7, ["ones", f"sq{k % 2}"], ["bank5"])
        ACT(rstd[:], banks[5][:], AF.Sqrt, ["bank5"], ["rstd"], bias=EPS, scale=1.0 / D)
        P.add("dve", lambda e: e.reciprocal(out=rstd[:], in_=rstd[:]), reads=["rstd"], writes=["rstd"])
        for k in range(8):
            t = tmpf[k % 2]; tk = f"tmpf{k % 2}"
            STT("dve", t[:], xT[:, k, :], fng[:, k:k + 1], rstd[:], ALU.mult, ALU.mult, ["xT", "fng", "rstd"], [tk])
            DMA("sp", outT[s, k * 128:(k + 1) * 128, c * 512:(c + 1) * 512], t[:], [tk], [])

    for s in range(NS):
        P.add("dve", lambda e: e.memset(Rst[:], 0.0), reads=[], writes=["Rst"])
        P.add("dve", lambda e: e.memset(carry[:], 0.0), reads=[], writes=["carry"])
        P.add(PL, lambda e: e.memset(tails[:], 0.0), reads=[], writes=["tails"])
        for c in range(NCH):
            DMA("sp", xT[:], xT_in[s].rearrange("(k p) t -> p k t", p=128)[:, :, c * 512:(c + 1) * 512], [], ["xT"])
            retention(s, c)
            stages = [("ret", None), ("ffn0", lambda: ffn(0, s)), ("s5", lambda: s5(s)), ("ffn1", lambda: ffn(1, s))]
            done = False
            for nm, fn in stages:
                if fn is not None and not done:
                    fn()
                if stop_after == nm and not done:
                    done = True
                    DMA("sp", outT[s].rearrange("(k p) t -> p k t", p=128)[:, :, c * 512:(c + 1) * 512], xT[:], ["xT"], [])
                    while rs["n"] % NSLAB != 0:
                        ring_next(PLAN[rs["n"] % NSLAB])
            if not done:
                final_norm_out(s, c)
    P.emit()
    return nc


def prep_core_inputs(inp, seqs, W, consts):
    L = inp["x"].shape[1]
    NS = len(seqs)
    d = {}
    d["xT"] = np.ascontiguousarray(inp["x"][seqs].transpose(0, 2, 1))
    d["cT"] = np.ascontiguousarray(inp["c"][seqs].reshape(NS, 8, 128).transpose(2, 1, 0))
    d["pos"] = np.ascontiguousarray(inp["pos"][seqs].reshape(NS, L // 128, 128).transpose(0, 2, 1)).astype(np.int32)
    d["ada_w"] = np.ascontiguousarray(inp["ada_w"].reshape(2, 8, 128, 6144).transpose(0, 2, 1, 3))
    d["ada_b"] = np.ascontiguousarray(inp["ada_b"].reshape(2, 48, 128).transpose(2, 0, 1))
    d["wslab"] = W
    lam = np.stack([inp["s5_lam_re"][0].reshape(32, 128).T, inp["s5_lam_im"][0].reshape(32, 128).T,
                    np.repeat(inp["s5_log_dt"][0], 64).reshape(32, 128).T], axis=1)
    d["s5p"] = np.ascontiguousarray(lam.astype(np.float32))
    d["s5d"] = np.ascontiguousarray(inp["s5_d"][0].reshape(8, 128).T)
    d["conv_w"] = np.ascontiguousarray(inp["ffn_conv_w"][:, :, 0, :].reshape(2, 3, NJ, 128).transpose(3, 0, 2, 1))
    d["conv_b"] = np.ascontiguousarray(inp["ffn_conv_b"].reshape(2, NJ, 128).transpose(2, 0, 1))
    d["fng"] = np.ascontiguousarray(inp["final_norm_g"].reshape(8, 128).T)
    for k, v in consts.items():
        d[k] = v
    return {k: np.ascontiguousarray(v) for k, v in d.items()}


def run(inp, n_cores=8, stop_after=None, trace=False):
    inp = {k: np.asarray(v) for k, v in inp.items()}
    B, L, _ = inp["x"].shape
    NS = B // n_cores
    W = host_slabs(inp)
    consts, _ = host_consts()
    nc = build_program(NS, L, stop_after=stop_after)
    in_maps = [prep_core_inputs(inp, list(range(c * NS, (c + 1) * NS)), W, consts) for c in range(n_cores)]
    res = run_bass_kernel_spmd(nc, in_maps, core_ids=list(range(n_cores)), trace=trace)
    out = np.empty((B, L, D), np.float32)
    for c in range(n_cores):
        out[c * NS:(c + 1) * NS] = res.results[c]["outT"].transpose(0, 2, 1)
    return out, res


def kernel(**inputs):
    out, _ = run(inputs, n_cores=8)
    return out
```

```python
import math
import os
import numpy as np
import concourse.bass as bass
import concourse.mybir as mybir
from concourse.bass_utils import run_bass_kernel_spmd

F32 = mybir.dt.float32
BF16 = mybir.dt.bfloat16
I32 = mybir.dt.int32
AF = mybir.ActivationFunctionType
ALU = mybir.AluOpType

D = 1024
DFF = 2816
NJ = 22
H = 4
EPS = 1e-6
TWO_PI = float(2 * np.pi)
PI = float(np.pi)

ENGS = ("pe", "act", "dve", "pool", "sp")


class _Op:
    __slots__ = ("eng", "fn", "deps", "needs_inc", "sem", "val", "is_dma")

    def __init__(self, eng, fn, is_dma):
        self.eng = eng
        self.fn = fn
        self.deps = []
        self.needs_inc = is_dma
        self.sem = None
        self.val = 0
        self.is_dma = is_dma


class Prog:
    def __init__(self, nc, n_dma_sems=24):
        self.nc = nc
        self.ops = {e: [] for e in ENGS}
        self.last_w = {}
        self.readers = {}
        self.n_dma_sems = n_dma_sems
        self.dma_rr = 0
        self.dma_rr_sw = 0
        self.dma_last = {}
        self.dma_cnt = {}

    def add(self, eng, fn, reads=(), writes=(), is_dma=False):
        op = _Op(eng, fn, is_dma)
        deps = {}
        for k in reads:
            w = self.last_w.get(k)
            if w is not None:
                deps[id(w)] = w
        for k in writes:
            w = self.last_w.get(k)
            if w is not None:
                deps[id(w)] = w
            for r in self.readers.get(k, {}).values():
                deps[id(r)] = r
        for k in reads:
            self.readers.setdefault(k, {})[(eng, is_dma)] = op if not is_dma else op
            if is_dma:
                self.readers[k][(eng, id(op))] = op
        for k in writes:
            self.last_w[k] = op
            self.readers[k] = {}
        if is_dma:
            if eng == "pool":
                si = self.n_dma_sems - 8 + (self.dma_rr_sw % 8)
                self.dma_rr_sw += 1
            else:
                si = self.dma_rr % (self.n_dma_sems - 8)
                self.dma_rr += 1
            prev = self.dma_last.get(si)
            if prev is not None:
                deps[id(prev)] = prev
            self.dma_last[si] = op
            self.dma_cnt[si] = self.dma_cnt.get(si, 0) + 16
            op.sem = ("dma", si)
            op.val = self.dma_cnt[si]
        for d in deps.values():
            if d is op:
                continue
            if d.eng == "pe" and eng == "pe" and not d.is_dma and not is_dma:
                continue
            op.deps.append(d)
            d.needs_inc = True
        self.ops[eng].append(op)
        return op

    def emit(self):
        nc = self.nc
        sems = {}
        for e in ENGS:
            sems[("eng", e)] = nc.alloc_semaphore(f"s_{e}")
        for i in range(self.n_dma_sems):
            sems[("dma", i)] = nc.alloc_semaphore(f"s_dma{i}")
        for e in ENGS:
            c = 0
            for op in self.ops[e]:
                if op.is_dma:
                    continue
                if op.needs_inc:
                    c += 1
                    op.sem = ("eng", e)
                    op.val = c
            if os.environ.get("KDEBUG"):
                print("engine", e, "ops", len(self.ops[e]), "incs", c, flush=True)
        if os.environ.get("KDEBUG"):
            print("dma sem counts", self.dma_cnt, flush=True)
        engmap = {"pe": "tensor", "act": "scalar", "dve": "vector", "pool": "gpsimd", "sp": "sync"}
        with nc.Block() as block:
            for e in ENGS:
                ops = self.ops[e]
                if not ops:
                    continue

                def body(eng, ops=ops):
                    waited = {}
                    for op in ops:
                        for d in op.deps:
                            if waited.get(d.sem, 0) >= d.val:
                                continue
                            eng.wait_ge(sems[d.sem], d.val)
                            waited[d.sem] = d.val
                        ins = op.fn(eng)
                        if op.needs_inc:
                            ins.then_inc(sems[op.sem], 16 if op.is_dma else 1)
                    for op in ops:
                        if op.is_dma and waited.get(op.sem, 0) < op.val:
                            eng.wait_ge(sems[op.sem], op.val)
                            waited[op.sem] = op.val

                getattr(block, engmap[e])(body)


SLAB = 4096
def slab_plan():
    plan = []
    for h in range(H):
        for cg in ("qk", "v", "g"):
            plan.append(("win", h, cg))
    for mp in range(4):
        plan.append(("wout", mp))
    for jp in range(NJ // 2):
        plan.append(("up0", jp))
    for m in range(8):
        plan.append(("dn0", m))
    for half in range(2):
        plan.append(("s5in", half))
    for nm in ("Br", "Bi", "Cr", "Cin"):
        plan.append(("s5" + nm,))
    for mp in range(4):
        plan.append(("glu", mp))
    for jp in range(NJ // 2):
        plan.append(("up1", jp))
    for m in range(8):
        plan.append(("dn1", m))
    return plan


PLAN = slab_plan()
NSLAB = len(PLAN)
SLAB_IDX = {k: i for i, k in enumerate(PLAN)}
CIN_IDX = SLAB_IDX[("s5Cin",)]


def host_slabs(inp):
    W = np.zeros((NSLAB, 128, SLAB), np.float32)
    w_in = inp["ret_w_in"][0].reshape(8, 128, 6144)
    for h in range(H):
        cols = {
            "qk": np.concatenate([w_in[:, :, h * 256:(h + 1) * 256], w_in[:, :, 1024 + h * 256:1024 + (h + 1) * 256]], axis=2),
            "v": w_in[:, :, 2048 + h * 512:2048 + (h + 1) * 512],
            "g": w_in[:, :, 4096 + h * 512:4096 + (h + 1) * 512],
        }
        for cg in ("qk", "v", "g"):
            W[SLAB_IDX[("win", h, cg)]] = cols[cg].transpose(1, 0, 2).reshape(128, SLAB)
    w_out = inp["ret_w_out"][0].reshape(16, 128, 1024)
    for mp in range(4):
        W[SLAB_IDX[("wout", mp)]] = w_out[:, :, mp * 256:(mp + 1) * 256].transpose(1, 0, 2).reshape(128, SLAB)
    for l in range(2):
        w_up = inp["ffn_w_up"][l].reshape(8, 128, 2 * DFF)
        for jp in range(NJ // 2):
            blk = np.zeros((128, 2, 8, 256), np.float32)
            for jj in range(2):
                j = jp * 2 + jj
                blk[:, jj, :, 0:128] = w_up[:, :, j * 128:(j + 1) * 128].transpose(1, 0, 2)
                blk[:, jj, :, 128:256] = w_up[:, :, DFF + j * 128:DFF + (j + 1) * 128].transpose(1, 0, 2)
            W[SLAB_IDX[(f"up{l}", jp)]] = blk.reshape(128, SLAB)
        w_dn = inp["ffn_w_down"][l].reshape(NJ, 128, 1024)
        for m in range(8):
            W[SLAB_IDX[(f"dn{l}", m)], :, 0:NJ * 128] = w_dn[:, :, m * 128:(m + 1) * 128].transpose(1, 0, 2).reshape(128, NJ * 128)
    s5in = inp["s5_w_in"][0].reshape(8, 128, 1024)
    for half in range(2):
        W[SLAB_IDX[("s5in", half)]] = s5in[:, :, half * 512:(half + 1) * 512].transpose(1, 0, 2).reshape(128, SLAB)
    for nm, key in (("Br", "s5_b_re"), ("Bi", "s5_b_im")):
        B = inp[key][0]
        blk = np.zeros((128, 32, 128), np.float32)
        for ct in range(32):
            for gp in range(2):
                g = 2 * ct + gp
                r0 = 32 * (ct % 4) + 16 * gp
                blk[r0:r0 + 16, ct, gp * 64:(gp + 1) * 64] = B[g].T
        W[SLAB_IDX[("s5" + nm,)]] = blk.reshape(128, SLAB)
    for nm, key in (("Cr", "s5_c_re"), ("Cin", "s5_c_im")):
        C = inp[key][0]
        blk = np.zeros((128, 32, 128), np.float32)
        for ct in range(32):
            for gp in range(2):
                g = 2 * ct + gp
                c0 = 32 * (ct % 4) + 16 * gp
                blk[gp * 64:(gp + 1) * 64, ct, c0:c0 + 16] = C[g].T
        W[SLAB_IDX[("s5" + nm,)]] = blk.reshape(128, SLAB)
    wg = inp["s5_w_glu"][0].reshape(8, 128, 2048)
    for mp in range(4):
        blk = np.zeros((128, 8, 2, 256), np.float32)
        for mm in range(2):
            m = mp * 2 + mm
            blk[:, :, mm, 0:128] = wg[:, :, m * 128:(m + 1) * 128].transpose(1, 0, 2)
            blk[:, :, mm, 128:256] = wg[:, :, 1024 + m * 128:1024 + (m + 1) * 128].transpose(1, 0, 2)
        W[SLAB_IDX[("glu", mp)]] = blk.reshape(128, SLAB)
    return W


def host_consts():
    half = 128
    inv_freq = np.power(np.float32(10000.0), -np.arange(half, dtype=np.float32) / np.float32(half)).astype(np.float32)
    c = {}
    c["ident"] = np.eye(128, dtype=np.float32)
    c["ones"] = np.ones((128, 128), np.float32)
    m = np.arange(128)
    c["mask01"] = (m[None, :] >= m[:, None]).astype(np.float32)
    c["invf"] = np.broadcast_to(inv_freq[None, :], (128, 128)).copy()
    lg = np.log1p(-np.exp2(-5.0 - np.arange(H, dtype=np.float64)))
    p1 = (np.arange(128, dtype=np.float64) + 1.0)[:, None]
    dq = np.exp(lg[None, :] * p1)
    dk = np.exp(-lg[None, :] * p1) * (256.0 ** -0.5)
    c["dqk"] = np.concatenate([dq, dk], axis=1).astype(np.float32)
    c["iota1"] = np.broadcast_to((np.arange(512, dtype=np.float32) + 1.0)[None, :], (128, 512)).copy()
    return c, [float(np.exp(lg[h] * 128.0)) for h in range(H)]


_, GCH = host_consts()


def build_program(NS, L, stop_after=None, gelu_mode=None):
    if gelu_mode is None:
        gelu_mode = os.environ.get("K_GELU", "act")
    NCH = L // 512
    NTT = L // 128
    nc = bass.Bass("TRN2", target_bir_lowering=False)
    P = Prog(nc)

    def din(name, shape, dt=F32):
        return nc.dram_tensor(name, list(shape), dt, kind="ExternalInput").ap()

    xT_in = din("xT", [NS, D, L])
    cT_in = din("cT", [128, 8, NS])
    pos_in = din("pos", [NS, 128, NTT], I32)
    adaw_in = din("ada_w", [2, 128, 8, 6144])
    adab_in = din("ada_b", [128, 2, 48])
    wslab_in = din("wslab", [NSLAB, 128, SLAB])
    s5p_in = din("s5p", [128, 3, 32])
    s5d_in = din("s5d", [128, 8])
    convw_in = din("conv_w", [128, 2, NJ, 3])
    convb_in = din("conv_b", [128, 2, NJ])
    fng_in = din("fng", [128, 8])
    ident_in = din("ident", [128, 128])
    ones_in = din("ones", [128, 128])
    mask_in = din("mask01", [128, 128])
    invf_in = din("invf", [128, 128])
    dqk_in = din("dqk", [128, 8])
    iota_in = din("iota1", [128, 512])
    outT = nc.dram_tensor("outT", [NS, D, L], F32, kind="ExternalOutput").ap()
    wb = nc.dram_tensor("wb", [NSLAB, 128, SLAB], BF16).ap()
    tabs = nc.dram_tensor("tabs", [32, 128, 4, 512], F32).ap()

    def sb(name, shape, dt=F32):
        return nc.alloc_sbuf_tensor("sb_" + name, list(shape), dt)

    banks = [nc.alloc_psum_tensor(f"pb{i}", [128, 512], F32) for i in range(8)]
    b7 = banks[7][:].bitcast(BF16)
    b1 = banks[1][:].bitcast(BF16)

    xT = sb("xT", [128, 8, 512])
    hT = sb("hT", [128, 8, 512], BF16)
    rstd = sb("rstd", [128, 512])
    sq = [sb(f"sq{i}", [128, 512], BF16) for i in range(2)]
    tmpf = [sb(f"tmpf{i}", [128, 512]) for i in range(2)]
    NB = 6
    ring = [sb(f"ring{i}", [128, SLAB], BF16) for i in range(NB)]
    modT = sb("modT", [128, 2, 48, NS])
    cond = sb("cond", [128, 8, NS], BF16)
    cin = sb("cin", [128, 8, NS])
    adab = sb("adab", [128, 2, 48])
    ident = sb("ident", [128, 128], BF16)
    ones = sb("ones", [128, 128], BF16)
    mask01 = sb("mask01", [128, 128])
    invf = sb("invf", [128, 128])
    dqk = sb("dqk", [128, 8])
    posi = sb("posi", [128, NS, NTT], I32)
    posf = sb("posf", [128, NS, NTT])
    convw = sb("convw", [128, 2, NJ, 3])
    convb = sb("convb", [128, 2, NJ])
    fng = sb("fng", [128, 8])
    s5d = sb("s5d", [128, 8])
    s5p = sb("s5p", [128, 3, 32])
    rho = sb("rho", [128, 32])
    rABCD = sb("rABCD", [128, 4, 2, 128])
    ang = rABCD
    cs = sb("cs", [128, 4, 2, 128])
    mm4 = sb("mm4", [128, 4, 512])
    m1 = mm4[:, 0, :]; m2 = mm4[:, 1, :]; m3 = mm4[:, 2, :]; m4 = mm4[:, 3, :]
    oraw = mm4[:].rearrange("p a c -> p (a c)").bitcast(BF16).rearrange("p (a c) -> p a c", a=8)
    angm = mm4[:, 0:2, :]
    angi = mm4[:, 2:4, :].bitcast(I32)
    rA = rABCD[:, 0]; rB = rABCD[:, 1]; rC = rABCD[:, 2]; rD = rABCD[:, 3]
    RK = ["rA", "rB", "rC", "rD"]
    rot = sb("rot", [128, 2, 2, 128])
    qk_tm = [sb(f"qk_tm{i}", [128, 2, 256], BF16) for i in range(2)]
    qkT = [sb(f"qkT{i}", [128, 4, 128], BF16) for i in range(2)]
    v_sb = [sb(f"v_sb{i}", [128, 512], BF16) for i in range(2)]
    g_sb4 = sb("g_sb4", [128, 4, 512], BF16)
    st6x = sb("st6x", [128, 2, 4, 6]); mv4 = sb("mv4", [128, 2, 4, 2])
    gn = sb("gn", [128, 2, 6, 4])
    sT_sb = sb("sT_sb", [128, 128], BF16)
    Rst = sb("Rst", [128, H, 2, 512])
    Sbs = [sb(f"Sb{i}", [128, 2, 512], BF16) for i in range(2)]
    st6 = sb("st6", [128, 6]); mv = sb("mv", [128, 4])
    yn = sb("yn", [128, 512], BF16)
    y_tm = sb("y_tm", [128, 512], BF16)
    aT = sb("aT", [128, NJ, 512], BF16)
    yT = aT[:, 0:16, :]
    gbuf = [sb(f"gbuf{i}", [128, 514]) for i in range(2)]
    acc = [sb(f"acc{i}", [128, 512]) for i in range(2)]
    sil = [sb(f"sil{i}", [128, 512]) for i in range(2)]
    tails = sb("tails", [128, 2, NJ, 2])
    uT_bf = aT[:, 0:8, :]
    tab = [sb(f"tab{i}", [128, 4, 512]) for i in range(2)]
    wri = sb("wri", [128, 2, 512])
    zri = sb("zri", [128, 2, 512])
    s_bf = [sb(f"s_bf{i}", [128, 4, 512], BF16) for i in range(2)]
    cwk = sb("cwk", [128, 4])
    carry = sb("carry", [128, 2, 32])
    zl_all = sb("zl_all", [128, 2, 32])
    cslast = sb("cslast", [128, 2, 32])
    cw6 = sb("cw6", [128, 4, 32])
    yg = hT

    PL = os.environ.get("K_POOL", "dve")

    def MM(out, lhsT, rhs, start, stop, r, w):
        P.add("pe", lambda e: e.matmul(out, lhsT, rhs, start=start, stop=stop), reads=r, writes=w)

    def TR(out, in_, r, w):
        P.add("pe", lambda e: e.transpose(out, in_, ident[:]), reads=list(r) + ["ident"], writes=w)

    def ACT(out, in_, func, r, w, bias=0.0, scale=1.0):
        P.add("act", lambda e: e.activation(out=out, in_=in_, func=func, bias=bias, scale=scale), reads=r, writes=w)

    def TT(eng, out, in0, in1, op, r, w):
        P.add(eng, lambda e: e.tensor_tensor(out=out, in0=in0, in1=in1, op=op), reads=r, writes=w)

    def TS(eng, out, in0, s1, s2, op0, op1, r, w):
        if s2 is None:
            P.add(eng, lambda e: e.tensor_scalar(out=out, in0=in0, scalar1=s1, scalar2=None, op0=op0), reads=r, writes=w)
        else:
            P.add(eng, lambda e: e.tensor_scalar(out=out, in0=in0, scalar1=s1, scalar2=s2, op0=op0, op1=op1), reads=r, writes=w)

    def STT(eng, out, in0, scalar, in1, op0, op1, r, w):
        P.add(eng, lambda e: e.scalar_tensor_tensor(out=out, in0=in0, scalar=scalar, in1=in1, op0=op0, op1=op1), reads=r, writes=w)

    def CP(eng, out, in_, r, w):
        P.add(eng, lambda e: e.tensor_copy(out=out, in_=in_), reads=r, writes=w)

    def DMA(eng, out, in_, r, w):
        P.add(eng, lambda e: e.dma_start(out=out, in_=in_), reads=r, writes=w, is_dma=True)

    def range_reduce_sin(eng, x, xi, xm, out, kx, ki, km, kout):
        TS(eng, xi, x, float(1.0 / TWO_PI), None, ALU.mult, None, kx, ki)
        STT(eng, x, xi, -TWO_PI, x, ALU.mult, ALU.add, ki + kx, kx)
        TS(eng, xm, x, PI, -TWO_PI, ALU.is_gt, ALU.mult, kx, km)
        TT(eng, x, x, xm, ALU.add, kx + km, kx)
        TS(eng, xm, x, -PI, TWO_PI, ALU.is_lt, ALU.mult, kx, km)
        TT(eng, x, x, xm, ALU.add, kx + km, kx)
        ACT(out, x, AF.Sin, kx, kout)

    for (dst, src, nm) in ((mask01, mask_in, "mask01"), (invf, invf_in, "invf"), (dqk, dqk_in, "dqk"),
                           (adab, adab_in, "adab"), (cin, cT_in, "cin"), (convw, convw_in, "convw"),
                           (convb, convb_in, "convb"), (fng, fng_in, "fng"), (s5d, s5d_in, "s5d"),
                           (s5p, s5p_in, "s5p"), (posi, pos_in.rearrange("s p n -> p s n"), "posi")):
        DMA("sp", dst[:], src, [], [nm])
    DMA("pool", ident[:], ident_in, [], ["ident"])
    DMA("pool", ones[:], ones_in, [], ["ones"])
    GRP = 4
    for i0 in range(0, NSLAB, GRP):
        i1 = min(NSLAB, i0 + GRP)
        idxs = [i for i in range(i0, i1) if i != CIN_IDX]
        runs = []
        for i in idxs:
            if runs and runs[-1][1] == i:
                runs[-1][1] = i + 1
            else:
                runs.append([i, i + 1])
        for a, b in runs:
            DMA("pool", wb[a:b].rearrange("s p f -> (s p) f"), wslab_in[a:b].rearrange("s p f -> (s p) f"), [], [f"wb{i}" for i in range(a, b)])
    cst = aT[:].rearrange("p j c -> p (j c)").bitcast(F32)[:, 0:SLAB]
    DMA("sp", cst, wslab_in[CIN_IDX], [], ["aT"])
    TS("dve", ring[0][:], cst, -1.0, None, ALU.mult, None, ["aT"], ["ring0"])
    DMA("sp", wb[CIN_IDX], ring[0][:], ["ring0"], [f"wb{CIN_IDX}"])

    CP("dve", posf[:], posi[:], ["posi"], ["posf"])
    ACT(cond[:], cin[:], AF.Silu, ["cin"], ["cond"])
    for l in range(2):
        for cgp in range(12):
            slot = ring[1 + (cgp % 2)]
            key = f"ring{1 + (cgp % 2)}"
            DMA("pool", slot[:].rearrange("p (k c) -> p k c", k=8), adaw_in[l][:, :, cgp * 512:(cgp + 1) * 512], [], [key])
            sv = slot[:].rearrange("p (k c) -> p k c", k=8)
            for mm in range(4):
                m = cgp * 4 + mm
                for k in range(8):
                    MM(banks[0][:, m * NS:(m + 1) * NS], sv[:, k, mm * 128:(mm + 1) * 128], cond[:, k, :], k == 0, k == 7,
                       [key, "cond"], ["bank0"])
        bv = banks[0][:, 0:48 * NS].rearrange("p (m s) -> p m s", s=NS)
        for s in range(NS):
            TT("dve", modT[:, l, :, s], bv[:, :, s], adab[:, l, :], ALU.add, ["bank0", "adab"], ["modT"])
    for l in range(2):
        for c0 in (8, 32):
            TS("dve", modT[:, l, c0:c0 + 8, :], modT[:, l, c0:c0 + 8, :], 1.0, None, ALU.add, None, ["modT"], ["modT"])

    def mod(l, which, k, s):
        base = {"sh1": 0, "sc1": 8, "g1": 16, "sh2": 24, "sc2": 32, "g2": 40}[which]
        return modT[:, l, base + k, s:s + 1]

    lr = s5p[:, 0, :]; li = s5p[:, 1, :]; ldt = s5p[:, 2, :]
    sw = sb("s5w", [128, 16, 32])
    swi = sb("s5wi", [128, 2, 32], I32)
    dtp = sw[:, 0, :]; xx = sw[:, 1, :]; th = sw[:, 2, :]
    ACT(dtp, ldt, AF.Exp, ["s5p"], ["s5w"])
    TT("dve", xx, lr, dtp, ALU.mult, ["s5p", "s5w"], ["s5w"])
    TT("dve", th, li, dtp, ALU.mult, ["s5p", "s5w"], ["s5w"])
    ACT(rho[:], xx, AF.Exp, ["s5w"], ["rho"])
    TS("dve", sw[:, 3, :], th, 1.0, None, ALU.mult, None, ["s5w"], ["s5w"])
    TS("dve", sw[:, 4, :], th, float(PI / 2), None, ALU.add, None, ["s5w"], ["s5w"])
    range_reduce_sin("dve", sw[:, 3:5, :], swi[:, :, :], sw[:, 5:7, :], sw[:, 7:9, :], ["s5w"], ["s5wi"], ["s5w"], ["s5w"])
    sn0 = sw[:, 7, :]; cs0 = sw[:, 8, :]
    ar = sw[:, 9, :]; ai = sw[:, 10, :]
    TT("dve", ar, rho[:], cs0, ALU.mult, ["rho", "s5w"], ["s5w"])
    TT("dve", ai, rho[:], sn0, ALU.mult, ["rho", "s5w"], ["s5w"])
    nr = sw[:, 11, :]
    TS("dve", nr, ar, -1.0, None, ALU.add, None, ["s5w"], ["s5w"])
    den = sw[:, 12, :]; t0 = sw[:, 13, :]
    TT("dve", den, lr, lr, ALU.mult, ["s5p", "s5w"], ["s5w"])
    TT("dve", t0, li, li, ALU.mult, ["s5p", "s5w"], ["s5w"])
    TT("dve", den, den, t0, ALU.add, ["s5w"], ["s5w"])
    P.add("dve", lambda e: e.reciprocal(out=den, in_=den), reads=["s5w"], writes=["s5w"])
    fcoef = sb("fcoef", [128, 3, 32])
    TT("dve", t0, nr, lr, ALU.mult, ["s5w", "s5p"], ["s5w"])
    TT("dve", sw[:, 14, :], ai, li, ALU.mult, ["s5w", "s5p"], ["s5w"])
    TT("dve", t0, t0, sw[:, 14, :], ALU.add, ["s5w"], ["s5w"])
    TT("dve", fcoef[:, 0, :], t0, den, ALU.mult, ["s5w"], ["fcoef"])
    TT("dve", t0, ai, lr, ALU.mult, ["s5w", "s5p"], ["s5w"])
    TT("dve", sw[:, 14, :], nr, li, ALU.mult, ["s5w", "s5p"], ["s5w"])
    TT("dve", t0, t0, sw[:, 14, :], ALU.subtract, ["s5w"], ["s5w"])
    TT("dve", fcoef[:, 1, :], t0, den, ALU.mult, ["s5w"], ["fcoef"])
    TS("dve", fcoef[:, 2, :], fcoef[:, 0, :], -1.0, None, ALU.mult, None, ["fcoef"], ["fcoef"])
    iota1 = rstd
    DMA("sp", iota1[:], iota_in, [], ["rstd"])
    phx = zri; phi_ = angi
    phm = s_bf[0][:].rearrange("p a c -> p (a c)").bitcast(F32).rearrange("p (a c) -> p a c", a=2)
    thv = sb("thv", [128, 32])
    CP("dve", thv[:], th, ["s5w"], ["thv"])
    for ct in range(32):
        tb = tab[ct % 2]
        tk = f"tab{ct % 2}"
        TS("dve", phx[:, 0, :], iota1[:], thv[:, ct:ct + 1], None, ALU.mult, None, ["rstd", "thv"], ["zri"])
        TS("dve", phx[:, 1, :], phx[:, 0, :], float(PI / 2), None, ALU.add, None, ["zri"], ["zri"])
        range_reduce_sin("dve", phx[:], phi_, phm, wri[:], ["zri"], ["m3", "m4"], ["s_bf0"], ["wri"])
        CP(PL, tb[:, 3, :], wri[:, 0, :], ["wri"], [tk])
        CP(PL, tb[:, 2, :], wri[:, 1, :], ["wri"], [tk])
        CP(PL, cslast[:, 0, ct:ct + 1], wri[:, 1, 511:512], ["wri"], ["cslast"])
        CP(PL, cslast[:, 1, ct:ct + 1], wri[:, 0, 511:512], ["wri"], ["cslast"])
        TS("dve", m1, wri[:, 1, :], fcoef[:, 0, ct:ct + 1], None, ALU.mult, None, ["wri", "fcoef"], ["m1"])
        STT("dve", tb[:, 0, :], wri[:, 0, :], fcoef[:, 1, ct:ct + 1], m1, ALU.mult, ALU.add, ["wri", "fcoef", "m1"], [tk])
        TS("dve", m2, wri[:, 1, :], fcoef[:, 1, ct:ct + 1], None, ALU.mult, None, ["wri", "fcoef"], ["m2"])
        STT("dve", tb[:, 1, :], wri[:, 0, :], fcoef[:, 2, ct:ct + 1], m2, ALU.mult, ALU.add, ["wri", "fcoef", "m2"], [tk])
        DMA("sp", tabs[ct], tb[:], [tk], [f"tabs{ct}"])

    rs = {"n": 0, "loaded": 0, "total": NS * NCH * NSLAB}

    def slab_cols(key):
        return NJ * 128 if key[0] in ("dn0", "dn1") else SLAB

    def ring_next(expect, live=1):
        n = rs["n"]
        assert PLAN[n % NSLAB] == expect, (PLAN[n % NSLAB], expect)
        lim = min(rs["total"], n + NB - live + 1)
        while rs["loaded"] < lim:
            j = rs["loaded"]
            si = j % NSLAB
            nc_ = slab_cols(PLAN[si])
            DMA("sp", ring[j % NB][:, 0:nc_], wb[si][:, 0:nc_], [f"wb{si}"], [f"ring{j % NB}"])
            rs["loaded"] += 1
        rs["n"] = n + 1
        return ring[n % NB], f"ring{n % NB}"

    def norm_mod(l, which_sc, which_sh, s):
        for k in range(8):
            if k % 2 == 0:
                ACT(sq[0][:], xT[:, k, :], AF.Square, ["xT"], ["sq0"])
            else:
                TT("dve", sq[1][:], xT[:, k, :], xT[:, k, :], ALU.mult, ["xT"], ["sq1"])
            MM(banks[5][:], ones[:], sq[k % 2][:], k == 0, k == 7, ["ones", f"sq{k % 2}"], ["bank5"])
        ACT(rstd[:], banks[5][:], AF.Sqrt, ["bank5"], ["rstd"], bias=EPS, scale=1.0 / D)
        P.add("dve", lambda e: e.reciprocal(out=rstd[:], in_=rstd[:]), reads=["rstd"], writes=["rstd"])
        for k in range(8):
            t = tmpf[k % 2]
            TT("dve", t[:], xT[:, k, :], rstd[:], ALU.mult, ["xT", "rstd"], [f"tmpf{k % 2}"])
            ACT(hT[:, k, :], t[:], AF.Identity, [f"tmpf{k % 2}", "modT"], ["hT"],
                bias=mod(l, which_sh, k, s), scale=mod(l, which_sc, k, s))

    def ffn(l, s):
        norm_mod(l, "sc2", "sh2", s)
        for jp in range(NJ // 2):
            slot, key = ring_next((f"up{l}", jp))
            wv = slot[:].rearrange("p (j k c) -> p j k c", j=2, k=8)
            for jj in range(2):
                j = jp * 2 + jj
                pv = banks[(j % 3) * 2]; pg = banks[(j % 3) * 2 + 1]
                kv = f"bank{(j % 3) * 2}"; kg = f"bank{(j % 3) * 2 + 1}"
                for k in range(8):
                    MM(pv[:], wv[:, jj, k, 0:128], hT[:, k, :], k == 0, k == 7, [key, "hT"], [kv])
                for k in range(8):
                    MM(pg[:], wv[:, jj, k, 128:256], hT[:, k, :], k == 0, k == 7, [key, "hT"], [kg])
                gb = gbuf[j % 2]; gk = f"gbuf{j % 2}"
                ac = acc[j % 2]; ak = f"acc{j % 2}"
                sl = sil[j % 2]; sk = f"sil{j % 2}"
                ACT(gb[:, 0:2], tails[:, l, j, :], AF.Copy, ["tails"], [gk])
                ACT(gb[:, 2:514], pg[:], AF.Copy, [kg], [gk])
                ACT(ac[:], pg[:], AF.Identity, [kg, "convw", "convb"], [ak], bias=convb[:, l, j:j + 1], scale=convw[:, l, j, 2:3])
                ACT(tails[:, l, j, :], gb[:, 512:514], AF.Copy, [gk], ["tails"])
                STT("dve", ac[:], gb[:, 1:513], convw[:, l, j, 1:2], ac[:], ALU.mult, ALU.add, [gk, "convw", ak], [ak])
                STT("dve", ac[:], gb[:, 0:512], convw[:, l, j, 0:1], ac[:], ALU.mult, ALU.add, [gk, "convw", ak], [ak])
                ACT(sl[:], ac[:], AF.Silu, [ak], [sk])
                TT("dve", aT[:, j, :], sl[:], pv[:], ALU.mult, [sk, kv], ["aT"])
        for m in range(8):
            slot, key = ring_next((f"dn{l}", m))
            wv = slot[:, 0:NJ * 128].rearrange("p (j c) -> p j c", j=NJ)
            pb = banks[6 + (m % 2)]; pk = f"bank{6 + (m % 2)}" if m % 2 == 0 else "bank7"
            for j in range(NJ):
                MM(pb[:], wv[:, j, :], aT[:, j, :], j == 0, j == NJ - 1, [key, "aT"], [pk])
            STT("dve", xT[:, m, :], pb[:], mod(l, "g2", m, s), xT[:, m, :], ALU.mult, ALU.add, [pk, "modT", "xT"], ["xT"])

    def retention(s, c):
        l = 0
        norm_mod(l, "sc1", "sh1", s)
        for nt in range(4):
            ntg = c * 4 + nt
            TS("dve", ang[:, nt, 0, :], invf[:], posf[:, s, ntg:ntg + 1], None, ALU.mult, None, ["invf", "posf"], RK)
            TS("dve", ang[:, nt, 1, :], ang[:, nt, 0, :], float(PI / 2), None, ALU.add, None, RK, RK)
        range_reduce_sin("dve", ang[:].rearrange("p a b c -> p (a b c)"), angi.rearrange("p a c -> p (a c)"),
                         angm.rearrange("p a c -> p (a c)"), cs[:].rearrange("p a b c -> p (a b c)"),
                         RK, ["m3", "m4"], ["m1", "m2"], ["cs"])
        slabs_h = {}

        def stage_A(h, i):
            if i == 0:
                slabs_h[h] = [ring_next(("win", h, cg), live=li + 1) for li, cg in enumerate(("qk", "v", "g"))]
            slabs = slabs_h[h]
            par = i % 2
            for ci in range(3):
                slot, key = slabs[ci]
                wv = slot[:].rearrange("p (k c) -> p k c", k=8)
                for k in range(8):
                    MM(banks[2 + ci][:], hT[:, k, i * 128:(i + 1) * 128], wv[:, k, :], k == 0, k == 7, ["hT", key], [f"bank{2 + ci}"])
            qv = banks[2][:].rearrange("p (a b c) -> p a b c", a=2, b=2)
            t1 = qv[:, :, 0, :]; t2 = qv[:, :, 1, :]
            cosb = cs[:, i, 1, :].unsqueeze(1).to_broadcast([128, 2, 128])
            sinb = cs[:, i, 0, :].unsqueeze(1).to_broadcast([128, 2, 128])
            TT("dve", rA, t1, cosb, ALU.mult, ["bank2", "cs"], ["rA"])
            TT("dve", rB, t2, sinb, ALU.mult, ["bank2", "cs"], ["rB"])
            TT("dve", rC, t1, sinb, ALU.mult, ["bank2", "cs"], ["rC"])
            TT("dve", rD, t2, cosb, ALU.mult, ["bank2", "cs"], ["rD"])
            TT("dve", rot[:, :, 0, :], rA, rB, ALU.subtract, ["rA", "rB"], ["rot"])
            TT("dve", rot[:, :, 1, :], rC, rD, ALU.add, ["rC", "rD"], ["rot"])
            qt = qk_tm[par]; qtk = f"qk_tm{par}"
            ACT(qt[:, 0, :], rot[:, 0, :, :].rearrange("p b c -> p (b c)"), AF.Identity, ["rot", "dqk"], [qtk], scale=dqk[:, h:h + 1])
            ACT(qt[:, 1, :], rot[:, 1, :, :].rearrange("p b c -> p (b c)"), AF.Identity, ["rot", "dqk"], [qtk], scale=dqk[:, 4 + h:5 + h])
            ACT(v_sb[par][:], banks[3][:], AF.Copy, ["bank3"], [f"v_sb{par}"])
            gbuf_h = g_sb4 if h % 2 == 0 else s_bf[1]
            ACT(gbuf_h[:, i, :], banks[4][:], AF.Silu, ["bank4"], [f"g{h % 2}_{i}"])
            for a in range(2):
                for dd in range(2):
                    TR(b7[:, (a * 2 + dd) * 128:(a * 2 + dd + 1) * 128], qt[:, a, dd * 128:(dd + 1) * 128], [qtk], ["bank7"])
            qT = qkT[par]; qTk = f"qkT{par}"
            CP("dve", qT[:].rearrange("p a c -> p (a c)"), b7[:, 0:512], ["bank7"], [qTk])

        def stage_B(h, i):
            par = i % 2
            qt = qk_tm[par]; qtk = f"qk_tm{par}"
            qT = qkT[par]; qTk = f"qkT{par}"
            Sin = Sbs[i % 2]; Sink = f"Sb{i % 2}"
            Sout = Sbs[(i + 1) % 2]; Soutk = f"Sb{(i + 1) % 2}"
            if i == 0:
                for dd in range(2):
                    ACT(Sin[:, dd, :], Rst[:, h, dd, :], AF.Copy, ["Rst"], [Sink], scale=GCH[h])
            for dd in range(2):
                MM(banks[5][:, 0:128], qT[:, 2 + dd, :], qT[:, dd, :], dd == 0, dd == 1, [qTk], ["bank5"])
            TT("dve", sT_sb[:], banks[5][:, 0:128], mask01[:], ALU.mult, ["bank5", "mask01"], ["sT_sb"])
            for dd in range(2):
                MM(banks[6][:], qT[:, dd, :], Sin[:, dd, :], dd == 0, False, [qTk, Sink], ["bank6"])
            MM(banks[6][:], sT_sb[:], v_sb[par][:], False, True, ["sT_sb", f"v_sb{par}"], ["bank6"])
            for dd in range(2):
                MM(banks[0][:], qt[:, 1, dd * 128:(dd + 1) * 128], v_sb[par][:], True, True, [qtk, f"v_sb{par}"], ["bank0"])
                STT("dve", Rst[:, h, dd, :], Rst[:, h, dd, :], GCH[h], banks[0][:], ALU.mult, ALU.add, ["Rst", "bank0"], ["Rst"])
                if i < 3:
                    ACT(Sout[:, dd, :], Rst[:, h, dd, :], AF.Copy, ["Rst"], [Soutk], scale=GCH[h])
            hp = h % 2
            P.add("dve", lambda e: e.bn_stats(out=st6x[:, hp, i, :], in_=banks[6][:]), reads=["bank6"], writes=[f"st6x{hp}"])
            ACT(oraw[:, hp * 4 + i, :], banks[6][:], AF.Copy, ["bank6"], [f"or{hp}_{i}"])

        def stage_Chead(h):
            hp = h % 2
            var = gn[:, hp, 2, :]; rs_ = gn[:, hp, 3, :]; nb = gn[:, hp, 4, :]
            for i in range(4):
                P.add("dve", lambda e, i=i: e.bn_aggr(out=mv4[:, hp, i, :], in_=st6x[:, hp, i, :]), reads=[f"st6x{hp}"], writes=[f"mv4{hp}"])
            ACT(var, mv4[:, hp, :, 1], AF.Sqrt, [f"mv4{hp}"], [f"gn{hp}"], bias=EPS, scale=1.0)
            P.add("dve", lambda e: e.reciprocal(out=rs_, in_=var), reads=[f"gn{hp}"], writes=[f"gn{hp}"])
            STT("dve", nb, mv4[:, hp, :, 0], -1.0, rs_, ALU.mult, ALU.mult, [f"gn{hp}", f"mv4{hp}"], [f"gn{hp}"])

        def stage_Ctile_pre(h, i):
            hp = h % 2
            gbuf_h = g_sb4 if hp == 0 else s_bf[1]
            ACT(yn[:], oraw[:, hp * 4 + i, :], AF.Identity, [f"or{hp}_{i}", f"gn{hp}"], ["yn"], bias=gn[:, hp, 4, i:i + 1], scale=gn[:, hp, 3, i:i + 1])
            TT("dve", y_tm[:], yn[:], gbuf_h[:, i, :], ALU.mult, ["yn", f"g{hp}_{i}"], ["y_tm"])

        def stage_Ctile_post(h, i):
            for vt in range(4):
                TR(b1[:, vt * 128:(vt + 1) * 128], y_tm[:, vt * 128:(vt + 1) * 128], ["y_tm"], ["bank1"])
            ACT(yT[:, h * 4:(h + 1) * 4, i * 128:(i + 1) * 128], b1[:, 0:512].rearrange("p (a c) -> p a c", a=4), AF.Copy, ["bank1"], ["aT"])

        order = [(h, i) for h in range(H) for i in range(4)]
        stage_A(*order[0])
        for idx, (h, i) in enumerate(order):
            if h > 0:
                stage_Ctile_pre(h - 1, i)
            if idx + 1 < len(order):
                stage_A(*order[idx + 1])
            if h > 0:
                stage_Ctile_post(h - 1, i)
            stage_B(h, i)
            if i == 3:
                stage_Chead(h)
        for i in range(4):
            stage_Ctile_pre(H - 1, i)
            stage_Ctile_post(H - 1, i)
        for mp in range(4):
            slot, key = ring_next(("wout", mp))
            wv = slot[:].rearrange("p (k c) -> p k c", k=16)
            for mm in range(2):
                m = mp * 2 + mm
                pb = banks[m % 2]; pk = f"bank{m % 2}"
                for kt in range(16):
                    MM(pb[:], wv[:, kt, mm * 128:(mm + 1) * 128], yT[:, kt, :], kt == 0, kt == 15, [key, "aT"], [pk])
                STT("dve", xT[:, m, :], pb[:], mod(l, "g1", m, s), xT[:, m, :], ALU.mult, ALU.add, [pk, "modT", "xT"], ["xT"])

    S5LV = int(os.environ.get("S5LV", "9"))

    def s5(s):
        l = 1
        norm_mod(l, "sc1", "sh1", s)
        for half in range(2):
            slot, key = ring_next(("s5in", half))
            wv = slot[:].rearrange("p (k c) -> p k c", k=8)
            for mm in range(4):
                m = half * 4 + mm
                pb = banks[m % 2]; pk = f"bank{m % 2}"
                for k in range(8):
                    MM(pb[:], wv[:, k, mm * 128:(mm + 1) * 128], hT[:, k, :], k == 0, k == 7, [key, "hT"], [pk])
                ACT(uT_bf[:, m, :], pb[:], AF.Copy, [pk], ["aT"])
        sBr, kBr = ring_next(("s5Br",), live=1); sBi, kBi = ring_next(("s5Bi",), live=2)
        sCr, kCr = ring_next(("s5Cr",), live=3); sCi, kCi = ring_next(("s5Cin",), live=4)
        vBr = sBr[:].rearrange("p (t c) -> p t c", t=32); vBi = sBi[:].rearrange("p (t c) -> p t c", t=32)
        vCr = sCr[:].rearrange("p (t c) -> p t c", t=32); vCi = sCi[:].rearrange("p (t c) -> p t c", t=32)
        def s5_B(ct):
            cht = ct // 4
            tb = tab[ct % 2]; tk = f"tab{ct % 2}"
            DMA("sp", tb[:], tabs[ct], [f"tabs{ct}"], [tk])
            pr = banks[2 + 2 * (ct % 2)]; pi = banks[3 + 2 * (ct % 2)]
            kr = f"bank{2 + 2 * (ct % 2)}"; ki = f"bank{3 + 2 * (ct % 2)}"
            MM(pr[:], vBr[:, ct, :], uT_bf[:, cht, :], True, True, [kBr, "aT"], [kr])
            MM(pi[:], vBi[:, ct, :], uT_bf[:, cht, :], True, True, [kBi, "aT"], [ki])

        s5_B(0)
        for ct in range(32):
            cht = ct // 4
            tb = tab[ct % 2]; tk = f"tab{ct % 2}"
            pr = banks[2 + 2 * (ct % 2)]; pi = banks[3 + 2 * (ct % 2)]
            kr = f"bank{2 + 2 * (ct % 2)}"; ki = f"bank{3 + 2 * (ct % 2)}"
            if ct + 1 < 32:
                s5_B(ct + 1)
            X = mm4[:, 0:2, :]; Y = mm4[:, 2:4, :]
            TT("dve", X, pr[:].unsqueeze(1).to_broadcast([128, 2, 512]), tb[:, 0:2, :], ALU.mult, [kr, tk], ["m1", "m2"])
            TT("dve", Y, pi[:].unsqueeze(1).to_broadcast([128, 2, 512]), tb[:, 0:2, :], ALU.mult, [ki, tk], ["m3", "m4"])
            TT("dve", wri[:, 0, :], mm4[:, 0, :], mm4[:, 3, :], ALU.subtract, ["m1", "m2", "m3", "m4"], ["wri"])
            TT("dve", wri[:, 1, :], mm4[:, 2, :], mm4[:, 1, :], ALU.add, ["m1", "m2", "m3", "m4"], ["wri"])
            rb = rho[:, ct:ct + 1].to_broadcast([128, 512])
            for ri in range(2):
                P.add("dve", lambda e, ri=ri, rb=rb, ct=ct: e.tensor_tensor_scan(out=zri[:, ri, :], data0=rb, data1=wri[:, ri, :],
                                                                         initial=carry[:, ri, ct:ct + 1], op0=ALU.mult, op1=ALU.add),
                      reads=["rho", "wri", "carry"], writes=["zri"])
            pp = s_bf[ct % 2]; ppk = f"s_bf{ct % 2}"
            TT("dve", pp[:, 0:2, :], zri[:, 0, :].unsqueeze(1).to_broadcast([128, 2, 512]), tb[:, 2:4, :], ALU.mult, ["zri", tk], [ppk])
            STT("dve", pp[:, 2, :], zri[:, 1, :], -1.0, tb[:, 3, :], ALU.mult, ALU.mult, ["zri", tk], [ppk])
            TT("dve", pp[:, 3, :], zri[:, 1, :], tb[:, 2, :], ALU.mult, ["zri", tk], [ppk])
            ACT(zl_all[:, 0, ct:ct + 1], zri[:, 0, 511:512], AF.Copy, ["zri"], ["zl_all"])
            ACT(zl_all[:, 1, ct:ct + 1], zri[:, 1, 511:512], AF.Copy, ["zri"], ["zl_all"])
            MM(banks[6][:], vCr[:, ct, :], pp[:, 0, :], ct % 4 == 0, False, [kCr, ppk], ["bank6"])
            MM(banks[6][:], vCr[:, ct, :], pp[:, 2, :], False, False, [kCr, ppk], ["bank6"])
            MM(banks[6][:], vCi[:, ct, :], pp[:, 1, :], False, False, [kCi, ppk], ["bank6"])
            MM(banks[6][:], vCi[:, ct, :], pp[:, 3, :], False, ct % 4 == 3, [kCi, ppk], ["bank6"])
            if ct % 4 == 3:
                t = tmpf[0]
                STT("dve", t[:], uT_bf[:, cht, :], s5d[:, cht:cht + 1], banks[6][:], ALU.mult, ALU.add, ["aT", "s5d", "bank6"], ["tmpf0"])
                ACT(yg[:, cht, :], t[:], AF.Gelu_apprx_tanh, ["tmpf0"], ["hT"])
        zr_ = zl_all[:, 0, :]; zi_ = zl_all[:, 1, :]; cl = cslast[:, 0, :]; sl_ = cslast[:, 1, :]
        TT("dve", cw6[:, 0, :], zr_, cl, ALU.mult, ["zl_all", "cslast"], ["cw6"])
        TT("dve", cw6[:, 1, :], zi_, sl_, ALU.mult, ["zl_all", "cslast"], ["cw6"])
        TT("dve", cw6[:, 2, :], zr_, sl_, ALU.mult, ["zl_all", "cslast"], ["cw6"])
        TT("dve", cw6[:, 3, :], zi_, cl, ALU.mult, ["zl_all", "cslast"], ["cw6"])
        TT("dve", carry[:, 0, :], cw6[:, 0, :], cw6[:, 1, :], ALU.subtract, ["cw6"], ["carry"])
        TT("dve", carry[:, 1, :], cw6[:, 2, :], cw6[:, 3, :], ALU.add, ["cw6"], ["carry"])
        for mp in range(4):
            slot, key = ring_next(("glu", mp))
            wv = slot[:].rearrange("p (k m c) -> p k m c", k=8, m=2)
            for mm in range(2 if S5LV >= 6 else 0):
                m = mp * 2 + mm
                pa = banks[(m % 2) * 2]; pbb = banks[(m % 2) * 2 + 1]
                ka = f"bank{(m % 2) * 2}"; kb = f"bank{(m % 2) * 2 + 1}"
                for k in range(8):
                    MM(pa[:], wv[:, k, mm, 0:128], yg[:, k, :], k == 0, k == 7, [key, "hT"], [ka])
                for k in range(8):
                    MM(pbb[:], wv[:, k, mm, 128:256], yg[:, k, :], k == 0, k == 7, [key, "hT"], [kb])
                t = tmpf[m % 2]; tkk = f"tmpf{m % 2}"
                ACT(t[:], pbb[:], AF.Sigmoid, [kb], [tkk])
                TT("dve", t[:], pa[:], t[:], ALU.mult, [ka, tkk], [tkk])
                STT("dve", xT[:, m, :], t[:], mod(l, "g1", m, s), xT[:, m, :], ALU.mult, ALU.add, [tkk, "modT", "xT"], ["xT"])

    def final_norm_out(s, c):
        for k in range(8):
            ACT(sq[k % 2][:], xT[:, k, :], AF.Square, ["xT"], [f"sq{k % 2}"])
            MM(banks[5][:], ones[:], sq[k % 2][:], k == 0, k == 7, ["ones", f"sq{k % 2}"], ["bank5"])
        ACT(rstd[:], banks[5][:], AF.Sqrt, ["bank5"], ["rstd"], bias=EPS, scale=1.0 / D)
        P.add("dve", lambda e: e.reciprocal(out=rstd[:], in_=rstd[:]), reads=["rstd"], writes=["rstd"])
        for k in range(8):
            t = tmpf[k % 2]; tk = f"tmpf{k % 2}"
            STT("dve", t[:], xT[:, k, :], fng[:, k:k + 1], rstd[:], ALU.mult, ALU.mult, ["xT", "fng", "rstd"], [tk])
            DMA("sp", outT[s, k * 128:(k + 1) * 128, c * 512:(c + 1) * 512], t[:], [tk], [])

    for s in range(NS):
        P.add("dve", lambda e: e.memset(Rst[:], 0.0), reads=[], writes=["Rst"])
        P.add("dve", lambda e: e.memset(carry[:], 0.0), reads=[], writes=["carry"])
        P.add(PL, lambda e: e.memset(tails[:], 0.0), reads=[], writes=["tails"])
        for c in range(NCH):
            DMA("sp", xT[:], xT_in[s].rearrange("(k p) t -> p k t", p=128)[:, :, c * 512:(c + 1) * 512], [], ["xT"])
            retention(s, c)
            stages = [("ret", None), ("ffn0", lambda: ffn(0, s)), ("s5", lambda: s5(s)), ("ffn1", lambda: ffn(1, s))]
            done = False
            for nm, fn in stages:
                if fn is not None and not done:
                    fn()
                if stop_after == nm and not done:
                    done = True
                    DMA("sp", outT[s].rearrange("(k p) t -> p k t", p=128)[:, :, c * 512:(c + 1) * 512], xT[:], ["xT"], [])
                    while rs["n"] % NSLAB != 0:
                        ring_next(PLAN[rs["n"] % NSLAB])
            if not done:
                final_norm_out(s, c)
    P.emit()
    return nc


def prep_core_inputs(inp, seqs, W, consts):
    L = inp["x"].shape[1]
    NS = len(seqs)
    d = {}
    d["xT"] = np.ascontiguousarray(inp["x"][seqs].transpose(0, 2, 1))
    d["cT"] = np.ascontiguousarray(inp["c"][seqs].reshape(NS, 8, 128).transpose(2, 1, 0))
    d["pos"] = np.ascontiguousarray(inp["pos"][seqs].reshape(NS, L // 128, 128).transpose(0, 2, 1)).astype(np.int32)
    d["ada_w"] = np.ascontiguousarray(inp["ada_w"].reshape(2, 8, 128, 6144).transpose(0, 2, 1, 3))
    d["ada_b"] = np.ascontiguousarray(inp["ada_b"].reshape(2, 48, 128).transpose(2, 0, 1))
    d["wslab"] = W
    lam = np.stack([inp["s5_lam_re"][0].reshape(32, 128).T, inp["s5_lam_im"][0].reshape(32, 128).T,
                    np.repeat(inp["s5_log_dt"][0], 64).reshape(32, 128).T], axis=1)
    d["s5p"] = np.ascontiguousarray(lam.astype(np.float32))
    d["s5d"] = np.ascontiguousarray(inp["s5_d"][0].reshape(8, 128).T)
    d["conv_w"] = np.ascontiguousarray(inp["ffn_conv_w"][:, :, 0, :].reshape(2, 3, NJ, 128).transpose(3, 0, 2, 1))
    d["conv_b"] = np.ascontiguousarray(inp["ffn_conv_b"].reshape(2, NJ, 128).transpose(2, 0, 1))
    d["fng"] = np.ascontiguousarray(inp["final_norm_g"].reshape(8, 128).T)
    for k, v in consts.items():
        d[k] = v
    return {k: np.ascontiguousarray(v) for k, v in d.items()}


def run(inp, n_cores=8, stop_after=None, trace=False):
    inp = {k: np.asarray(v) for k, v in inp.items()}
    B, L, _ = inp["x"].shape
    NS = B // n_cores
    W = host_slabs(inp)
    consts, _ = host_consts()
    nc = build_program(NS, L, stop_after=stop_after)
    in_maps = [prep_core_inputs(inp, list(range(c * NS, (c + 1) * NS)), W, consts) for c in range(n_cores)]
    res = run_bass_kernel_spmd(nc, in_maps, core_ids=list(range(n_cores)), trace=trace)
    out = np.empty((B, L, D), np.float32)
    for c in range(n_cores):
        out[c * NS:(c + 1) * NS] = res.results[c]["outT"].transpose(0, 2, 1)
    return out, res


def kernel(**inputs):
    out, _ = run(inputs, n_cores=8)
    return out
```

```python
import math
import os
import numpy as np
import concourse.bass as bass
import concourse.mybir as mybir
from concourse.bass_utils import run_bass_kernel_spmd

F32 = mybir.dt.float32
BF16 = mybir.dt.bfloat16
I32 = mybir.dt.int32
AF = mybir.ActivationFunctionType
ALU = mybir.AluOpType

D = 1024
DFF = 2816
NJ = 22
H = 4
EPS = 1e-6
TWO_PI = float(2 * np.pi)
PI = float(np.pi)

ENGS = ("pe", "act", "dve", "pool", "sp")


class _Op:
    __slots__ = ("eng", "fn", "deps", "needs_inc", "sem", "val", "is_dma")

    def __init__(self, eng, fn, is_dma):
        self.eng = eng
        self.fn = fn
        self.deps = []
        self.needs_inc = is_dma
        self.sem = None
        self.val = 0
        self.is_dma = is_dma


class Prog:
    def __init__(self, nc, n_dma_sems=24):
        self.nc = nc
        self.ops = {e: [] for e in ENGS}
        self.last_w = {}
        self.readers = {}
        self.n_dma_sems = n_dma_sems
        self.dma_rr = 0
        self.dma_rr_sw = 0
        self.dma_last = {}
        self.dma_cnt = {}

    def add(self, eng, fn, reads=(), writes=(), is_dma=False):
        op = _Op(eng, fn, is_dma)
        deps = {}
        for k in reads:
            w = self.last_w.get(k)
            if w is not None:
                deps[id(w)] = w
        for k in writes:
            w = self.last_w.get(k)
            if w is not None:
                deps[id(w)] = w
            for r in self.readers.get(k, {}).values():
                deps[id(r)] = r
        for k in reads:
            self.readers.setdefault(k, {})[(eng, is_dma)] = op if not is_dma else op
            if is_dma:
                self.readers[k][(eng, id(op))] = op
        for k in writes:
            self.last_w[k] = op
            self.readers[k] = {}
        if is_dma:
            if eng == "pool":
                si = self.n_dma_sems - 8 + (self.dma_rr_sw % 8)
                self.dma_rr_sw += 1
            else:
                si = self.dma_rr % (self.n_dma_sems - 8)
                self.dma_rr += 1
            prev = self.dma_last.get(si)
            if prev is not None:
                deps[id(prev)] = prev
            self.dma_last[si] = op
            self.dma_cnt[si] = self.dma_cnt.get(si, 0) + 16
            op.sem = ("dma", si)
            op.val = self.dma_cnt[si]
        for d in deps.values():
            if d is op:
                continue
            if d.eng == "pe" and eng == "pe" and not d.is_dma and not is_dma:
                continue
            op.deps.append(d)
            d.needs_inc = True
        self.ops[eng].append(op)
        return op

    def emit(self):
        nc = self.nc
        sems = {}
        for e in ENGS:
            sems[("eng", e)] = nc.alloc_semaphore(f"s_{e}")
        for i in range(self.n_dma_sems):
            sems[("dma", i)] = nc.alloc_semaphore(f"s_dma{i}")
        for e in ENGS:
            c = 0
            for op in self.ops[e]:
                if op.is_dma:
                    continue
                if op.needs_inc:
                    c += 1
                    op.sem = ("eng", e)
                    op.val = c
            if os.environ.get("KDEBUG"):
                print("engine", e, "ops", len(self.ops[e]), "incs", c, flush=True)
        if os.environ.get("KDEBUG"):
            print("dma sem counts", self.dma_cnt, flush=True)
        engmap = {"pe": "tensor", "act": "scalar", "dve": "vector", "pool": "gpsimd", "sp": "sync"}
        with nc.Block() as block:
            for e in ENGS:
                ops = self.ops[e]
                if not ops:
                    continue

                def body(eng, ops=ops):
                    waited = {}
                    for op in ops:
                        for d in op.deps:
                            if waited.get(d.sem, 0) >= d.val:
                                continue
                            eng.wait_ge(sems[d.sem], d.val)
                            waited[d.sem] = d.val
                        ins = op.fn(eng)
                        if op.needs_inc:
                            ins.then_inc(sems[op.sem], 16 if op.is_dma else 1)
                    for op in ops:
                        if op.is_dma and waited.get(op.sem, 0) < op.val:
                            eng.wait_ge(sems[op.sem], op.val)
                            waited[op.sem] = op.val

                getattr(block, engmap[e])(body)


SLAB = 4096
def slab_plan():
    plan = []
    for h in range(H):
        for cg in ("qk", "v", "g"):
            plan.append(("win", h, cg))
    for mp in range(4):
        plan.append(("wout", mp))
    for jp in range(NJ // 2):
        plan.append(("up0", jp))
    for m in range(8):
        plan.append(("dn0", m))
    for half in range(2):
        plan.append(("s5in", half))
    for nm in ("Br", "Bi", "Cr", "Cin"):
        plan.append(("s5" + nm,))
    for mp in range(4):
        plan.append(("glu", mp))
    for jp in range(NJ // 2):
        plan.append(("up1", jp))
    for m in range(8):
        plan.append(("dn1", m))
    return plan


PLAN = slab_plan()
NSLAB = len(PLAN)
SLAB_IDX = {k: i for i, k in enumerate(PLAN)}
CIN_IDX = SLAB_IDX[("s5Cin",)]


def host_slabs(inp):
    W = np.zeros((NSLAB, 128, SLAB), np.float32)
    w_in = inp["ret_w_in"][0].reshape(8, 128, 6144)
    for h in range(H):
        cols = {
            "qk": np.concatenate([w_in[:, :, h * 256:(h + 1) * 256], w_in[:, :, 1024 + h * 256:1024 + (h + 1) * 256]], axis=2),
            "v": w_in[:, :, 2048 + h * 512:2048 + (h + 1) * 512],
            "g": w_in[:, :, 4096 + h * 512:4096 + (h + 1) * 512],
        }
        for cg in ("qk", "v", "g"):
            W[SLAB_IDX[("win", h, cg)]] = cols[cg].transpose(1, 0, 2).reshape(128, SLAB)
    w_out = inp["ret_w_out"][0].reshape(16, 128, 1024)
    for mp in range(4):
        W[SLAB_IDX[("wout", mp)]] = w_out[:, :, mp * 256:(mp + 1) * 256].transpose(1, 0, 2).reshape(128, SLAB)
    for l in range(2):
        w_up = inp["ffn_w_up"][l].reshape(8, 128, 2 * DFF)
        for jp in range(NJ // 2):
            blk = np.zeros((128, 2, 8, 256), np.float32)
            for jj in range(2):
                j = jp * 2 + jj
                blk[:, jj, :, 0:128] = w_up[:, :, j * 128:(j + 1) * 128].transpose(1, 0, 2)
                blk[:, jj, :, 128:256] = w_up[:, :, DFF + j * 128:DFF + (j + 1) * 128].transpose(1, 0, 2)
            W[SLAB_IDX[(f"up{l}", jp)]] = blk.reshape(128, SLAB)
        w_dn = inp["ffn_w_down"][l].reshape(NJ, 128, 1024)
        for m in range(8):
            W[SLAB_IDX[(f"dn{l}", m)], :, 0:NJ * 128] = w_dn[:, :, m * 128:(m + 1) * 128].transpose(1, 0, 2).reshape(128, NJ * 128)
    s5in = inp["s5_w_in"][0].reshape(8, 128, 1024)
    for half in range(2):
        W[SLAB_IDX[("s5in", half)]] = s5in[:, :, half * 512:(half + 1) * 512].transpose(1, 0, 2).reshape(128, SLAB)
    for nm, key in (("Br", "s5_b_re"), ("Bi", "s5_b_im")):
        B = inp[key][0]
        blk = np.zeros((128, 32, 128), np.float32)
        for ct in range(32):
            for gp in range(2):
                g = 2 * ct + gp
                r0 = 32 * (ct % 4) + 16 * gp
                blk[r0:r0 + 16, ct, gp * 64:(gp + 1) * 64] = B[g].T
        W[SLAB_IDX[("s5" + nm,)]] = blk.reshape(128, SLAB)
    for nm, key in (("Cr", "s5_c_re"), ("Cin", "s5_c_im")):
        C = inp[key][0]
        blk = np.zeros((128, 32, 128), np.float32)
        for ct in range(32):
            for gp in range(2):
                g = 2 * ct + gp
                c0 = 32 * (ct % 4) + 16 * gp
                blk[gp * 64:(gp + 1) * 64, ct, c0:c0 + 16] = C[g].T
        W[SLAB_IDX[("s5" + nm,)]] = blk.reshape(128, SLAB)
    wg = inp["s5_w_glu"][0].reshape(8, 128, 2048)
    for mp in range(4):
        blk = np.zeros((128, 8, 2, 256), np.float32)
        for mm in range(2):
            m = mp * 2 + mm
            blk[:, :, mm, 0:128] = wg[:, :, m * 128:(m + 1) * 128].transpose(1, 0, 2)
            blk[:, :, mm, 128:256] = wg[:, :, 1024 + m * 128:1024 + (m + 1) * 128].transpose(1, 0, 2)
        W[SLAB_IDX[("glu", mp)]] = blk.reshape(128, SLAB)
    return W


def host_consts():
    half = 128
    inv_freq = np.power(np.float32(10000.0), -np.arange(half, dtype=np.float32) / np.float32(half)).astype(np.float32)
    c = {}
    c["ident"] = np.eye(128, dtype=np.float32)
    c["ones"] = np.ones((128, 128), np.float32)
    m = np.arange(128)
    c["mask01"] = (m[None, :] >= m[:, None]).astype(np.float32)
    c["invf"] = np.broadcast_to(inv_freq[None, :], (128, 128)).copy()
    lg = np.log1p(-np.exp2(-5.0 - np.arange(H, dtype=np.float64)))
    p1 = (np.arange(128, dtype=np.float64) + 1.0)[:, None]
    dq = np.exp(lg[None, :] * p1)
    dk = np.exp(-lg[None, :] * p1) * (256.0 ** -0.5)
    c["dqk"] = np.concatenate([dq, dk], axis=1).astype(np.float32)
    c["iota1"] = np.broadcast_to((np.arange(512, dtype=np.float32) + 1.0)[None, :], (128, 512)).copy()
    return c, [float(np.exp(lg[h] * 128.0)) for h in range(H)]


_, GCH = host_consts()


def build_program(NS, L, stop_after=None, gelu_mode=None):
    if gelu_mode is None:
        gelu_mode = os.environ.get("K_GELU", "act")
    NCH = L // 512
    NTT = L // 128
    nc = bass.Bass("TRN2", target_bir_lowering=False)
    P = Prog(nc)

    def din(name, shape, dt=F32):
        return nc.dram_tensor(name, list(shape), dt, kind="ExternalInput").ap()

    xT_in = din("xT", [NS, D, L])
    cT_in = din("cT", [128, 8, NS])
    pos_in = din("pos", [NS, 128, NTT], I32)
    adaw_in = din("ada_w", [2, 128, 8, 6144])
    adab_in = din("ada_b", [128, 2, 48])
    wslab_in = din("wslab", [NSLAB, 128, SLAB])
    s5p_in = din("s5p", [128, 3, 32])
    s5d_in = din("s5d", [128, 8])
    convw_in = din("conv_w", [128, 2, NJ, 3])
    convb_in = din("conv_b", [128, 2, NJ])
    fng_in = din("fng", [128, 8])
    ident_in = din("ident", [128, 128])
    ones_in = din("ones", [128, 128])
    mask_in = din("mask01", [128, 128])
    invf_in = din("invf", [128, 128])
    dqk_in = din("dqk", [128, 8])
    iota_in = din("iota1", [128, 512])
    outT = nc.dram_tensor("outT", [NS, D, L], F32, kind="ExternalOutput").ap()
    wb = nc.dram_tensor("wb", [NSLAB, 128, SLAB], BF16).ap()
    tabs = nc.dram_tensor("tabs", [32, 128, 4, 512], F32).ap()

    def sb(name, shape, dt=F32):
        return nc.alloc_sbuf_tensor("sb_" + name, list(shape), dt)

    banks = [nc.alloc_psum_tensor(f"pb{i}", [128, 512], F32) for i in range(8)]
    b7 = banks[7][:].bitcast(BF16)
    b1 = banks[1][:].bitcast(BF16)

    xT = sb("xT", [128, 8, 512])
    hT = sb("hT", [128, 8, 512], BF16)
    rstd = sb("rstd", [128, 512])
    sq = [sb(f"sq{i}", [128, 512], BF16) for i in range(2)]
    tmpf = [sb(f"tmpf{i}", [128, 512]) for i in range(2)]
    NB = 6
    ring = [sb(f"ring{i}", [128, SLAB], BF16) for i in range(NB)]
    modT = sb("modT", [128, 2, 48, NS])
    cond = sb("cond", [128, 8, NS], BF16)
    cin = sb("cin", [128, 8, NS])
    adab = sb("adab", [128, 2, 48])
    ident = sb("ident", [128, 128], BF16)
    ones = sb("ones", [128, 128], BF16)
    mask01 = sb("mask01", [128, 128])
    invf = sb("invf", [128, 128])
    dqk = sb("dqk", [128, 8])
    posi = sb("posi", [128, NS, NTT], I32)
    posf = sb("posf", [128, NS, NTT])
    convw = sb("convw", [128, 2, NJ, 3])
    convb = sb("convb", [128, 2, NJ])
    fng = sb("fng", [128, 8])
    s5d = sb("s5d", [128, 8])
    s5p = sb("s5p", [128, 3, 32])
    rho = sb("rho", [128, 32])
    rABCD = sb("rABCD", [128, 4, 2, 128])
    ang = rABCD
    cs = sb("cs", [128, 4, 2, 128])
    mm4 = sb("mm4", [128, 4, 512])
    m1 = mm4[:, 0, :]; m2 = mm4[:, 1, :]; m3 = mm4[:, 2, :]; m4 = mm4[:, 3, :]
    oraw = mm4[:].rearrange("p a c -> p (a c)").bitcast(BF16).rearrange("p (a c) -> p a c", a=8)
    angm = mm4[:, 0:2, :]
    angi = mm4[:, 2:4, :].bitcast(I32)
    rA = rABCD[:, 0]; rB = rABCD[:, 1]; rC = rABCD[:, 2]; rD = rABCD[:, 3]
    RK = ["rA", "rB", "rC", "rD"]
    rot = sb("rot", [128, 2, 2, 128])
    qk_tm = [sb(f"qk_tm{i}", [128, 2, 256], BF16) for i in range(2)]
    qkT = [sb(f"qkT{i}", [128, 4, 128], BF16) for i in range(2)]
    v_sb = [sb(f"v_sb{i}", [128, 512], BF16) for i in range(2)]
    g_sb4 = sb("g_sb4", [128, 4, 512], BF16)
    st6x = sb("st6x", [128, 2, 4, 6]); mv4 = sb("mv4", [128, 2, 4, 2])
    gn = sb("gn", [128, 2, 6, 4])
    sT_sb = sb("sT_sb", [128, 128], BF16)
    Rst = sb("Rst", [128, H, 2, 512])
    Sbs = [sb(f"Sb{i}", [128, 2, 512], BF16) for i in range(2)]
    st6 = sb("st6", [128, 6]); mv = sb("mv", [128, 4])
    yn = sb("yn", [128, 512], BF16)
    y_tm = sb("y_tm", [128, 512], BF16)
    aT = sb("aT", [128, NJ, 512], BF16)
    yT = aT[:, 0:16, :]
    gbuf = [sb(f"gbuf{i}", [128, 514]) for i in range(2)]
    acc = [sb(f"acc{i}", [128, 512]) for i in range(2)]
    sil = [sb(f"sil{i}", [128, 512]) for i in range(2)]
    tails = sb("tails", [128, 2, NJ, 2])
    uT_bf = aT[:, 0:8, :]
    tab = [sb(f"tab{i}", [128, 4, 512]) for i in range(2)]
    wri = sb("wri", [128, 2, 512])
    zri = sb("zri", [128, 2, 512])
    s_bf = [sb(f"s_bf{i}", [128, 4, 512], BF16) for i in range(2)]
    cwk = sb("cwk", [128, 4])
    carry = sb("carry", [128, 2, 32])
    zl_all = sb("zl_all", [128, 2, 32])
    cslast = sb("cslast", [128, 2, 32])
    cw6 = sb("cw6", [128, 4, 32])
    yg = hT

    PL = os.environ.get("K_POOL", "dve")

    def MM(out, lhsT, rhs, start, stop, r, w):
        P.add("pe", lambda e: e.matmul(out, lhsT, rhs, start=start, stop=stop), reads=r, writes=w)

    def TR(out, in_, r, w):
        P.add("pe", lambda e: e.transpose(out, in_, ident[:]), reads=list(r) + ["ident"], writes=w)

    def ACT(out, in_, func, r, w, bias=0.0, scale=1.0):
        P.add("act", lambda e: e.activation(out=out, in_=in_, func=func, bias=bias, scale=scale), reads=r, writes=w)

    def TT(eng, out, in0, in1, op, r, w):
        P.add(eng, lambda e: e.tensor_tensor(out=out, in0=in0, in1=in1, op=op), reads=r, writes=w)

    def TS(eng, out, in0, s1, s2, op0, op1, r, w):
        if s2 is None:
            P.add(eng, lambda e: e.tensor_scalar(out=out, in0=in0, scalar1=s1, scalar2=None, op0=op0), reads=r, writes=w)
        else:
            P.add(eng, lambda e: e.tensor_scalar(out=out, in0=in0, scalar1=s1, scalar2=s2, op0=op0, op1=op1), reads=r, writes=w)

    def STT(eng, out, in0, scalar, in1, op0, op1, r, w):
        P.add(eng, lambda e: e.scalar_tensor_tensor(out=out, in0=in0, scalar=scalar, in1=in1, op0=op0, op1=op1), reads=r, writes=w)

    def CP(eng, out, in_, r, w):
        P.add(eng, lambda e: e.tensor_copy(out=out, in_=in_), reads=r, writes=w)

    def DMA(eng, out, in_, r, w):
        P.add(eng, lambda e: e.dma_start(out=out, in_=in_), reads=r, writes=w, is_dma=True)

    def range_reduce_sin(eng, x, xi, xm, out, kx, ki, km, kout):
        TS(eng, xi, x, float(1.0 / TWO_PI), None, ALU.mult, None, kx, ki)
        STT(eng, x, xi, -TWO_PI, x, ALU.mult, ALU.add, ki + kx, kx)
        TS(eng, xm, x, PI, -TWO_PI, ALU.is_gt, ALU.mult, kx, km)
        TT(eng, x, x, xm, ALU.add, kx + km, kx)
        TS(eng, xm, x, -PI, TWO_PI, ALU.is_lt, ALU.mult, kx, km)
        TT(eng, x, x, xm, ALU.add, kx + km, kx)
        ACT(out, x, AF.Sin, kx, kout)

    for (dst, src, nm) in ((mask01, mask_in, "mask01"), (invf, invf_in, "invf"), (dqk, dqk_in, "dqk"),
                           (adab, adab_in, "adab"), (cin, cT_in, "cin"), (convw, convw_in, "convw"),
                           (convb, convb_in, "convb"), (fng, fng_in, "fng"), (s5d, s5d_in, "s5d"),
                           (s5p, s5p_in, "s5p"), (posi, pos_in.rearrange("s p n -> p s n"), "posi")):
        DMA("sp", dst[:], src, [], [nm])
    DMA("pool", ident[:], ident_in, [], ["ident"])
    DMA("pool", ones[:], ones_in, [], ["ones"])
    GRP = 4
    for i0 in range(0, NSLAB, GRP):
        i1 = min(NSLAB, i0 + GRP)
        idxs = [i for i in range(i0, i1) if i != CIN_IDX]
        runs = []
        for i in idxs:
            if runs and runs[-1][1] == i:
                runs[-1][1] = i + 1
            else:
                runs.append([i, i + 1])
        for a, b in runs:
            DMA("pool", wb[a:b].rearrange("s p f -> (s p) f"), wslab_in[a:b].rearrange("s p f -> (s p) f"), [], [f"wb{i}" for i in range(a, b)])
    cst = aT[:].rearrange("p j c -> p (j c)").bitcast(F32)[:, 0:SLAB]
    DMA("sp", cst, wslab_in[CIN_IDX], [], ["aT"])
    TS("dve", ring[0][:], cst, -1.0, None, ALU.mult, None, ["aT"], ["ring0"])
    DMA("sp", wb[CIN_IDX], ring[0][:], ["ring0"], [f"wb{CIN_IDX}"])

    CP("dve", posf[:], posi[:], ["posi"], ["posf"])
    ACT(cond[:], cin[:], AF.Silu, ["cin"], ["cond"])
    for l in range(2):
        for cgp in range(12):
            slot = ring[1 + (cgp % 2)]
            key = f"ring{1 + (cgp % 2)}"
            DMA("pool", slot[:].rearrange("p (k c) -> p k c", k=8), adaw_in[l][:, :, cgp * 512:(cgp + 1) * 512], [], [key])
            sv = slot[:].rearrange("p (k c) -> p k c", k=8)
            for mm in range(4):
                m = cgp * 4 + mm
                for k in range(8):
                    MM(banks[0][:, m * NS:(m + 1) * NS], sv[:, k, mm * 128:(mm + 1) * 128], cond[:, k, :], k == 0, k == 7,
                       [key, "cond"], ["bank0"])
        bv = banks[0][:, 0:48 * NS].rearrange("p (m s) -> p m s", s=NS)
        for s in range(NS):
            TT("dve", modT[:, l, :, s], bv[:, :, s], adab[:, l, :], ALU.add, ["bank0", "adab"], ["modT"])
    for l in range(2):
        for c0 in (8, 32):
            TS("dve", modT[:, l, c0:c0 + 8, :], modT[:, l, c0:c0 + 8, :], 1.0, None, ALU.add, None, ["modT"], ["modT"])

    def mod(l, which, k, s):
        base = {"sh1": 0, "sc1": 8, "g1": 16, "sh2": 24, "sc2": 32, "g2": 40}[which]
        return modT[:, l, base + k, s:s + 1]

    lr = s5p[:, 0, :]; li = s5p[:, 1, :]; ldt = s5p[:, 2, :]
    sw = sb("s5w", [128, 16, 32])
    swi = sb("s5wi", [128, 2, 32], I32)
    dtp = sw[:, 0, :]; xx = sw[:, 1, :]; th = sw[:, 2, :]
    ACT(dtp, ldt, AF.Exp, ["s5p"], ["s5w"])
    TT("dve", xx, lr, dtp, ALU.mult, ["s5p", "s5w"], ["s5w"])
    TT("dve", th, li, dtp, ALU.mult, ["s5p", "s5w"], ["s5w"])
    ACT(rho[:], xx, AF.Exp, ["s5w"], ["rho"])
    TS("dve", sw[:, 3, :], th, 1.0, None, ALU.mult, None, ["s5w"], ["s5w"])
    TS("dve", sw[:, 4, :], th, float(PI / 2), None, ALU.add, None, ["s5w"], ["s5w"])
    range_reduce_sin("dve", sw[:, 3:5, :], swi[:, :, :], sw[:, 5:7, :], sw[:, 7:9, :], ["s5w"], ["s5wi"], ["s5w"], ["s5w"])
    sn0 = sw[:, 7, :]; cs0 = sw[:, 8, :]
    ar = sw[:, 9, :]; ai = sw[:, 10, :]
    TT("dve", ar, rho[:], cs0, ALU.mult, ["rho", "s5w"], ["s5w"])
    TT("dve", ai, rho[:], sn0, ALU.mult, ["rho", "s5w"], ["s5w"])
    nr = sw[:, 11, :]
    TS("dve", nr, ar, -1.0, None, ALU.add, None, ["s5w"], ["s5w"])
    den = sw[:, 12, :]; t0 = sw[:, 13, :]
    TT("dve", den, lr, lr, ALU.mult, ["s5p", "s5w"], ["s5w"])
    TT("dve", t0, li, li, ALU.mult, ["s5p", "s5w"], ["s5w"])
    TT("dve", den, den, t0, ALU.add, ["s5w"], ["s5w"])
    P.add("dve", lambda e: e.reciprocal(out=den, in_=den), reads=["s5w"], writes=["s5w"])
    fcoef = sb("fcoef", [128, 3, 32])
    TT("dve", t0, nr, lr, ALU.mult, ["s5w", "s5p"], ["s5w"])
    TT("dve", sw[:, 14, :], ai, li, ALU.mult, ["s5w", "s5p"], ["s5w"])
    TT("dve", t0, t0, sw[:, 14, :], ALU.add, ["s5w"], ["s5w"])
    TT("dve", fcoef[:, 0, :], t0, den, ALU.mult, ["s5w"], ["fcoef"])
    TT("dve", t0, ai, lr, ALU.mult, ["s5w", "s5p"], ["s5w"])
    TT("dve", sw[:, 14, :], nr, li, ALU.mult, ["s5w", "s5p"], ["s5w"])
    TT("dve", t0, t0, sw[:, 14, :], ALU.subtract, ["s5w"], ["s5w"])
    TT("dve", fcoef[:, 1, :], t0, den, ALU.mult, ["s5w"], ["fcoef"])
    TS("dve", fcoef[:, 2, :], fcoef[:, 0, :], -1.0, None, ALU.mult, None, ["fcoef"], ["fcoef"])
    iota1 = rstd
    DMA("sp", iota1[:], iota_in, [], ["rstd"])
    phx = zri; phi_ = angi
    phm = s_bf[0][:].rearrange("p a c -> p (a c)").bitcast(F32).rearrange("p (a c) -> p a c", a=2)
    thv = sb("thv", [128, 32])
    CP("dve", thv[:], th, ["s5w"], ["thv"])
    for ct in range(32):
        tb = tab[ct % 2]
        tk = f"tab{ct % 2}"
        TS("dve", phx[:, 0, :], iota1[:], thv[:, ct:ct + 1], None, ALU.mult, None, ["rstd", "thv"], ["zri"])
        TS("dve", phx[:, 1, :], phx[:, 0, :], float(PI / 2), None, ALU.add, None, ["zri"], ["zri"])
        range_reduce_sin("dve", phx[:], phi_, phm, wri[:], ["zri"], ["m3", "m4"], ["s_bf0"], ["wri"])
        CP(PL, tb[:, 3, :], wri[:, 0, :], ["wri"], [tk])
        CP(PL, tb[:, 2, :], wri[:, 1, :], ["wri"], [tk])
        CP(PL, cslast[:, 0, ct:ct + 1], wri[:, 1, 511:512], ["wri"], ["cslast"])
        CP(PL, cslast[:, 1, ct:ct + 1], wri[:, 0, 511:512], ["wri"], ["cslast"])
        TS("dve", m1, wri[:, 1, :], fcoef[:, 0, ct:ct + 1], None, ALU.mult, None, ["wri", "fcoef"], ["m1"])
        STT("dve", tb[:, 0, :], wri[:, 0, :], fcoef[:, 1, ct:ct + 1], m1, ALU.mult, ALU.add, ["wri", "fcoef", "m1"], [tk])
        TS("dve", m2, wri[:, 1, :], fcoef[:, 1, ct:ct + 1], None, ALU.mult, None, ["wri", "fcoef"], ["m2"])
        STT("dve", tb[:, 1, :], wri[:, 0, :], fcoef[:, 2, ct:ct + 1], m2, ALU.mult, ALU.add, ["wri", "fcoef", "m2"], [tk])
        DMA("sp", tabs[ct], tb[:], [tk], [f"tabs{ct}"])

    rs = {"n": 0, "loaded": 0, "total": NS * NCH * NSLAB}

    def slab_cols(key):
        return NJ * 128 if key[0] in ("dn0", "dn1") else SLAB

    def ring_next(expect, live=1):
        n = rs["n"]
        assert PLAN[n % NSLAB] == expect, (PLAN[n % NSLAB], expect)
        lim = min(rs["total"], n + NB - live + 1)
        while rs["loaded"] < lim:
            j = rs["loaded"]
            si = j % NSLAB
            nc_ = slab_cols(PLAN[si])
            DMA("sp", ring[j % NB][:, 0:nc_], wb[si][:, 0:nc_], [f"wb{si}"], [f"ring{j % NB}"])
            rs["loaded"] += 1
        rs["n"] = n + 1
        return ring[n % NB], f"ring{n % NB}"

    def norm_mod(l, which_sc, which_sh, s):
        for k in range(8):
            if k % 2 == 0:
                ACT(sq[0][:], xT[:, k, :], AF.Square, [f"xT{k}"], ["sq0"])
            else:
                TT("dve", sq[1][:], xT[:, k, :], xT[:, k, :], ALU.mult, [f"xT{k}"], ["sq1"])
            MM(banks[5][:], ones[:], sq[k % 2][:], k == 0, k == 7, ["ones", f"sq{k % 2}"], ["bank5"])
        ACT(rstd[:], banks[5][:], AF.Sqrt, ["bank5"], ["rstd"], bias=EPS, scale=1.0 / D)
        P.add("dve", lambda e: e.reciprocal(out=rstd[:], in_=rstd[:]), reads=["rstd"], writes=["rstd"])
        for k in range(8):
            t = tmpf[k % 2]
            TT("dve", t[:], xT[:, k, :], rstd[:], ALU.mult, [f"xT{k}", "rstd"], [f"tmpf{k % 2}"])
            ACT(hT[:, k, :], t[:], AF.Identity, [f"tmpf{k % 2}", "modT"], ["hT"],
                bias=mod(l, which_sh, k, s), scale=mod(l, which_sc, k, s))

    def ffn(l, s):
        norm_mod(l, "sc2", "sh2", s)
        for jp in range(NJ // 2):
            slot, key = ring_next((f"up{l}", jp))
            wv = slot[:].rearrange("p (j k c) -> p j k c", j=2, k=8)
            for jj in range(2):
                j = jp * 2 + jj
                pv = banks[(j % 3) * 2]; pg = banks[(j % 3) * 2 + 1]
                kv = f"bank{(j % 3) * 2}"; kg = f"bank{(j % 3) * 2 + 1}"
                for k in range(8):
                    MM(pv[:], wv[:, jj, k, 0:128], hT[:, k, :], k == 0, k == 7, [key, "hT"], [kv])
                for k in range(8):
                    MM(pg[:], wv[:, jj, k, 128:256], hT[:, k, :], k == 0, k == 7, [key, "hT"], [kg])
                gb = gbuf[j % 2]; gk = f"gbuf{j % 2}"
                ac = acc[j % 2]; ak = f"acc{j % 2}"
                sl = sil[j % 2]; sk = f"sil{j % 2}"
                ACT(gb[:, 0:2], tails[:, l, j, :], AF.Copy, ["tails"], [gk])
                ACT(gb[:, 2:514], pg[:], AF.Copy, [kg], [gk])
                ACT(ac[:], pg[:], AF.Identity, [kg, "convw", "convb"], [ak], bias=convb[:, l, j:j + 1], scale=convw[:, l, j, 2:3])
                ACT(tails[:, l, j, :], gb[:, 512:514], AF.Copy, [gk], ["tails"])
                STT("dve", ac[:], gb[:, 1:513], convw[:, l, j, 1:2], ac[:], ALU.mult, ALU.add, [gk, "convw", ak], [ak])
                STT("dve", ac[:], gb[:, 0:512], convw[:, l, j, 0:1], ac[:], ALU.mult, ALU.add, [gk, "convw", ak], [ak])
                ACT(sl[:], ac[:], AF.Silu, [ak], [sk])
                TT("dve", aT[:, j, :], sl[:], pv[:], ALU.mult, [sk, kv], ["aT"])
        for m in range(8):
            slot, key = ring_next((f"dn{l}", m))
            wv = slot[:, 0:NJ * 128].rearrange("p (j c) -> p j c", j=NJ)
            pb = banks[6 + (m % 2)]; pk = f"bank{6 + (m % 2)}" if m % 2 == 0 else "bank7"
            for j in range(NJ):
                MM(pb[:], wv[:, j, :], aT[:, j, :], j == 0, j == NJ - 1, [key, "aT"], [pk])
            STT("dve", xT[:, m, :], pb[:], mod(l, "g2", m, s), xT[:, m, :], ALU.mult, ALU.add, [pk, "modT", f"xT{m}"], [f"xT{m}"])

    def retention(s, c):
        l = 0
        norm_mod(l, "sc1", "sh1", s)
        for nt in range(4):
            ntg = c * 4 + nt
            TS("dve", ang[:, nt, 0, :], invf[:], posf[:, s, ntg:ntg + 1], None, ALU.mult, None, ["invf", "posf"], RK)
            TS("dve", ang[:, nt, 1, :], ang[:, nt, 0, :], float(PI / 2), None, ALU.add, None, RK, RK)
        range_reduce_sin("dve", ang[:].rearrange("p a b c -> p (a b c)"), angi.rearrange("p a c -> p (a c)"),
                         angm.rearrange("p a c -> p (a c)"), cs[:].rearrange("p a b c -> p (a b c)"),
                         RK, ["m3", "m4"], ["m1", "m2"], ["cs"])
        slabs_h = {}

        def stage_A(h, i):
            if i == 0:
                slabs_h[h] = [ring_next(("win", h, cg), live=li + 1) for li, cg in enumerate(("qk", "v", "g"))]
            slabs = slabs_h[h]
            par = i % 2
            for ci in range(3):
                slot, key = slabs[ci]
                wv = slot[:].rearrange("p (k c) -> p k c", k=8)
                for k in range(8):
                    MM(banks[2 + ci][:], hT[:, k, i * 128:(i + 1) * 128], wv[:, k, :], k == 0, k == 7, ["hT", key], [f"bank{2 + ci}"])
            qv = banks[2][:].rearrange("p (a b c) -> p a b c", a=2, b=2)
            t1 = qv[:, :, 0, :]; t2 = qv[:, :, 1, :]
            cosb = cs[:, i, 1, :].unsqueeze(1).to_broadcast([128, 2, 128])
            sinb = cs[:, i, 0, :].unsqueeze(1).to_broadcast([128, 2, 128])
            TT("dve", rA, t1, cosb, ALU.mult, ["bank2", "cs"], ["rA"])
            TT("dve", rB, t2, sinb, ALU.mult, ["bank2", "cs"], ["rB"])
            TT("dve", rC, t1, sinb, ALU.mult, ["bank2", "cs"], ["rC"])
            TT("dve", rD, t2, cosb, ALU.mult, ["bank2", "cs"], ["rD"])
            TT("dve", rot[:, :, 0, :], rA, rB, ALU.subtract, ["rA", "rB"], ["rot"])
            TT("dve", rot[:, :, 1, :], rC, rD, ALU.add, ["rC", "rD"], ["rot"])
            qt = qk_tm[par]; qtk = f"qk_tm{par}"
            ACT(qt[:, 0, :], rot[:, 0, :, :].rearrange("p b c -> p (b c)"), AF.Identity, ["rot", "dqk"], [qtk], scale=dqk[:, h:h + 1])
            ACT(qt[:, 1, :], rot[:, 1, :, :].rearrange("p b c -> p (b c)"), AF.Identity, ["rot", "dqk"], [qtk], scale=dqk[:, 4 + h:5 + h])
            ACT(v_sb[par][:], banks[3][:], AF.Copy, ["bank3"], [f"v_sb{par}"])
            gbuf_h = g_sb4 if h % 2 == 0 else s_bf[1]
            ACT(gbuf_h[:, i, :], banks[4][:], AF.Silu, ["bank4"], [f"g{h % 2}_{i}"])
            for a in range(2):
                for dd in range(2):
                    TR(b7[:, (a * 2 + dd) * 128:(a * 2 + dd + 1) * 128], qt[:, a, dd * 128:(dd + 1) * 128], [qtk], ["bank7"])
            qT = qkT[par]; qTk = f"qkT{par}"
            CP("dve", qT[:].rearrange("p a c -> p (a c)"), b7[:, 0:512], ["bank7"], [qTk])

        def stage_B(h, i):
            par = i % 2
            qt = qk_tm[par]; qtk = f"qk_tm{par}"
            qT = qkT[par]; qTk = f"qkT{par}"
            Sin = Sbs[i % 2]; Sink = f"Sb{i % 2}"
            Sout = Sbs[(i + 1) % 2]; Soutk = f"Sb{(i + 1) % 2}"
            if i == 0:
                for dd in range(2):
                    ACT(Sin[:, dd, :], Rst[:, h, dd, :], AF.Copy, [f"Rst{dd}"], [Sink], scale=GCH[h])
            def dkv(dd):
                MM(banks[0][:], qt[:, 1, dd * 128:(dd + 1) * 128], v_sb[par][:], True, True, [qtk, f"v_sb{par}"], ["bank0"])
                STT("dve", Rst[:, h, dd, :], Rst[:, h, dd, :], GCH[h], banks[0][:], ALU.mult, ALU.add, [f"Rst{dd}", "bank0"], [f"Rst{dd}"])
                if i < 3:
                    ACT(Sout[:, dd, :], Rst[:, h, dd, :], AF.Copy, [f"Rst{dd}"], [Soutk], scale=GCH[h])
            for dd in range(2):
                MM(banks[5][:, 0:128], qT[:, 2 + dd, :], qT[:, dd, :], dd == 0, dd == 1, [qTk], ["bank5"])
            TT("dve", sT_sb[:], banks[5][:, 0:128], mask01[:], ALU.mult, ["bank5", "mask01"], ["sT_sb"])
            dkv(0)
            for dd in range(2):
                MM(banks[6][:], qT[:, dd, :], Sin[:, dd, :], dd == 0, False, [qTk, Sink], ["bank6"])
            MM(banks[6][:], sT_sb[:], v_sb[par][:], False, True, ["sT_sb", f"v_sb{par}"], ["bank6"])
            dkv(1)
            hp = h % 2
            P.add("dve", lambda e: e.bn_stats(out=st6x[:, hp, i, :], in_=banks[6][:]), reads=["bank6"], writes=[f"st6x{hp}"])
            ACT(oraw[:, hp * 4 + i, :], banks[6][:], AF.Copy, ["bank6"], [f"or{hp}_{i}"])

        def stage_Chead(h):
            hp = h % 2
            var = gn[:, hp, 2, :]; rs_ = gn[:, hp, 3, :]; nb = gn[:, hp, 4, :]
            for i in range(4):
                P.add("dve", lambda e, i=i: e.bn_aggr(out=mv4[:, hp, i, :], in_=st6x[:, hp, i, :]), reads=[f"st6x{hp}"], writes=[f"mv4{hp}"])
            ACT(var, mv4[:, hp, :, 1], AF.Sqrt, [f"mv4{hp}"], [f"gn{hp}"], bias=EPS, scale=1.0)
            P.add("dve", lambda e: e.reciprocal(out=rs_, in_=var), reads=[f"gn{hp}"], writes=[f"gn{hp}"])
            STT("dve", nb, mv4[:, hp, :, 0], -1.0, rs_, ALU.mult, ALU.mult, [f"gn{hp}", f"mv4{hp}"], [f"gn{hp}"])

        def stage_Ctile_pre(h, i):
            hp = h % 2
            gbuf_h = g_sb4 if hp == 0 else s_bf[1]
            ACT(yn[:], oraw[:, hp * 4 + i, :], AF.Identity, [f"or{hp}_{i}", f"gn{hp}"], ["yn"], bias=gn[:, hp, 4, i:i + 1], scale=gn[:, hp, 3, i:i + 1])
            TT("dve", y_tm[:], yn[:], gbuf_h[:, i, :], ALU.mult, ["yn", f"g{hp}_{i}"], ["y_tm"])

        def stage_Ctile_post(h, i):
            for vt in range(4):
                TR(b1[:, vt * 128:(vt + 1) * 128], y_tm[:, vt * 128:(vt + 1) * 128], ["y_tm"], ["bank1"])
            ACT(yT[:, h * 4:(h + 1) * 4, i * 128:(i + 1) * 128], b1[:, 0:512].rearrange("p (a c) -> p a c", a=4), AF.Copy, ["bank1"], ["aT"])

        order = [(h, i) for h in range(H) for i in range(4)]
        stage_A(*order[0])
        for idx, (h, i) in enumerate(order):
            if h > 0:
                stage_Ctile_pre(h - 1, i)
            if idx + 1 < len(order):
                stage_A(*order[idx + 1])
            if h > 0:
                stage_Ctile_post(h - 1, i)
            stage_B(h, i)
            if i == 3:
                stage_Chead(h)
        for i in range(4):
            stage_Ctile_pre(H - 1, i)
            stage_Ctile_post(H - 1, i)
        for mp in range(4):
            slot, key = ring_next(("wout", mp))
            wv = slot[:].rearrange("p (k c) -> p k c", k=16)
            for mm in range(2):
                m = mp * 2 + mm
                pb = banks[m % 2]; pk = f"bank{m % 2}"
                for kt in range(16):
                    MM(pb[:], wv[:, kt, mm * 128:(mm + 1) * 128], yT[:, kt, :], kt == 0, kt == 15, [key, "aT"], [pk])
                STT("dve", xT[:, m, :], pb[:], mod(l, "g1", m, s), xT[:, m, :], ALU.mult, ALU.add, [pk, "modT", f"xT{m}"], [f"xT{m}"])

    S5LV = int(os.environ.get("S5LV", "9"))

    def s5(s):
        l = 1
        norm_mod(l, "sc1", "sh1", s)
        for half in range(2):
            slot, key = ring_next(("s5in", half))
            wv = slot[:].rearrange("p (k c) -> p k c", k=8)
            for mm in range(4):
                m = half * 4 + mm
                pb = banks[m % 2]; pk = f"bank{m % 2}"
                for k in range(8):
                    MM(pb[:], wv[:, k, mm * 128:(mm + 1) * 128], hT[:, k, :], k == 0, k == 7, [key, "hT"], [pk])
                ACT(uT_bf[:, m, :], pb[:], AF.Copy, [pk], ["aT"])
        sBr, kBr = ring_next(("s5Br",), live=1); sBi, kBi = ring_next(("s5Bi",), live=2)
        sCr, kCr = ring_next(("s5Cr",), live=3); sCi, kCi = ring_next(("s5Cin",), live=4)
        vBr = sBr[:].rearrange("p (t c) -> p t c", t=32); vBi = sBi[:].rearrange("p (t c) -> p t c", t=32)
        vCr = sCr[:].rearrange("p (t c) -> p t c", t=32); vCi = sCi[:].rearrange("p (t c) -> p t c", t=32)
        def s5_B(ct):
            cht = ct // 4
            tb = tab[ct % 2]; tk = f"tab{ct % 2}"
            DMA("sp", tb[:], tabs[ct], [f"tabs{ct}"], [tk])
            pr = banks[2 + 2 * (ct % 2)]; pi = banks[3 + 2 * (ct % 2)]
            kr = f"bank{2 + 2 * (ct % 2)}"; ki = f"bank{3 + 2 * (ct % 2)}"
            MM(pr[:], vBr[:, ct, :], uT_bf[:, cht, :], True, True, [kBr, "aT"], [kr])
            MM(pi[:], vBi[:, ct, :], uT_bf[:, cht, :], True, True, [kBi, "aT"], [ki])

        s5_B(0)
        for ct in range(32):
            cht = ct // 4
            tb = tab[ct % 2]; tk = f"tab{ct % 2}"
            pr = banks[2 + 2 * (ct % 2)]; pi = banks[3 + 2 * (ct % 2)]
            kr = f"bank{2 + 2 * (ct % 2)}"; ki = f"bank{3 + 2 * (ct % 2)}"
            if ct + 1 < 32:
                s5_B(ct + 1)
            X = mm4[:, 0:2, :]; Y = mm4[:, 2:4, :]
            TT("dve", X, pr[:].unsqueeze(1).to_broadcast([128, 2, 512]), tb[:, 0:2, :], ALU.mult, [kr, tk], ["m1", "m2"])
            TT("dve", Y, pi[:].unsqueeze(1).to_broadcast([128, 2, 512]), tb[:, 0:2, :], ALU.mult, [ki, tk], ["m3", "m4"])
            TT("dve", wri[:, 0, :], mm4[:, 0, :], mm4[:, 3, :], ALU.subtract, ["m1", "m2", "m3", "m4"], ["wri"])
            TT("dve", wri[:, 1, :], mm4[:, 2, :], mm4[:, 1, :], ALU.add, ["m1", "m2", "m3", "m4"], ["wri"])
            rb = rho[:, ct:ct + 1].to_broadcast([128, 512])
            for ri in range(2):
                P.add("dve", lambda e, ri=ri, rb=rb, ct=ct: e.tensor_tensor_scan(out=zri[:, ri, :], data0=rb, data1=wri[:, ri, :],
                                                                         initial=carry[:, ri, ct:ct + 1], op0=ALU.mult, op1=ALU.add),
                      reads=["rho", "wri", "carry"], writes=["zri"])
            pp = s_bf[ct % 2]; ppk = f"s_bf{ct % 2}"
            TT("dve", pp[:, 0:2, :], zri[:, 0, :].unsqueeze(1).to_broadcast([128, 2, 512]), tb[:, 2:4, :], ALU.mult, ["zri", tk], [ppk])
            STT("dve", pp[:, 2, :], zri[:, 1, :], -1.0, tb[:, 3, :], ALU.mult, ALU.mult, ["zri", tk], [ppk])
            TT("dve", pp[:, 3, :], zri[:, 1, :], tb[:, 2, :], ALU.mult, ["zri", tk], [ppk])
            ACT(zl_all[:, 0, ct:ct + 1], zri[:, 0, 511:512], AF.Copy, ["zri"], ["zl_all"])
            ACT(zl_all[:, 1, ct:ct + 1], zri[:, 1, 511:512], AF.Copy, ["zri"], ["zl_all"])
            MM(banks[6][:], vCr[:, ct, :], pp[:, 0, :], ct % 4 == 0, False, [kCr, ppk], ["bank6"])
            MM(banks[6][:], vCr[:, ct, :], pp[:, 2, :], False, False, [kCr, ppk], ["bank6"])
            MM(banks[6][:], vCi[:, ct, :], pp[:, 1, :], False, False, [kCi, ppk], ["bank6"])
            MM(banks[6][:], vCi[:, ct, :], pp[:, 3, :], False, ct % 4 == 3, [kCi, ppk], ["bank6"])
            if ct % 4 == 3:
                t = tmpf[0]
                STT("dve", t[:], uT_bf[:, cht, :], s5d[:, cht:cht + 1], banks[6][:], ALU.mult, ALU.add, ["aT", "s5d", "bank6"], ["tmpf0"])
                ACT(yg[:, cht, :], t[:], AF.Gelu_apprx_tanh, ["tmpf0"], ["hT"])
        zr_ = zl_all[:, 0, :]; zi_ = zl_all[:, 1, :]; cl = cslast[:, 0, :]; sl_ = cslast[:, 1, :]
        TT("dve", cw6[:, 0, :], zr_, cl, ALU.mult, ["zl_all", "cslast"], ["cw6"])
        TT("dve", cw6[:, 1, :], zi_, sl_, ALU.mult, ["zl_all", "cslast"], ["cw6"])
        TT("dve", cw6[:, 2, :], zr_, sl_, ALU.mult, ["zl_all", "cslast"], ["cw6"])
        TT("dve", cw6[:, 3, :], zi_, cl, ALU.mult, ["zl_all", "cslast"], ["cw6"])
        TT("dve", carry[:, 0, :], cw6[:, 0, :], cw6[:, 1, :], ALU.subtract, ["cw6"], ["carry"])
        TT("dve", carry[:, 1, :], cw6[:, 2, :], cw6[:, 3, :], ALU.add, ["cw6"], ["carry"])
        for mp in range(4):
            slot, key = ring_next(("glu", mp))
            wv = slot[:].rearrange("p (k m c) -> p k m c", k=8, m=2)
            for mm in range(2 if S5LV >= 6 else 0):
                m = mp * 2 + mm
                pa = banks[(m % 2) * 2]; pbb = banks[(m % 2) * 2 + 1]
                ka = f"bank{(m % 2) * 2}"; kb = f"bank{(m % 2) * 2 + 1}"
                for k in range(8):
                    MM(pa[:], wv[:, k, mm, 0:128], yg[:, k, :], k == 0, k == 7, [key, "hT"], [ka])
                for k in range(8):
                    MM(pbb[:], wv[:, k, mm, 128:256], yg[:, k, :], k == 0, k == 7, [key, "hT"], [kb])
                t = tmpf[m % 2]; tkk = f"tmpf{m % 2}"
                ACT(t[:], pbb[:], AF.Sigmoid, [kb], [tkk])
                TT("dve", t[:], pa[:], t[:], ALU.mult, [ka, tkk], [tkk])
                STT("dve", xT[:, m, :], t[:], mod(l, "g1", m, s), xT[:, m, :], ALU.mult, ALU.add, [tkk, "modT", f"xT{m}"], [f"xT{m}"])

    def final_norm_out(s, c):
        for k in range(8):
            ACT(sq[k % 2][:], xT[:, k, :], AF.Square, [f"xT{k}"], [f"sq{k % 2}"])
            MM(banks[5][:], ones[:], sq[k % 2][:], k == 0, k == 7, ["ones", f"sq{k % 2}"], ["bank5"])
        ACT(rstd[:], banks[5][:], AF.Sqrt, ["bank5"], ["rstd"], bias=EPS, scale=1.0 / D)
        P.add("dve", lambda e: e.reciprocal(out=rstd[:], in_=rstd[:]), reads=["rstd"], writes=["rstd"])
        for k in range(8):
            t = tmpf[k % 2]; tk = f"tmpf{k % 2}"
            STT("dve", t[:], xT[:, k, :], fng[:, k:k + 1], rstd[:], ALU.mult, ALU.mult, [f"xT{k}", "fng", "rstd"], [tk])
            DMA("sp", outT[s, k * 128:(k + 1) * 128, c * 512:(c + 1) * 512], t[:], [tk], [])

    for s in range(NS):
        P.add("dve", lambda e: e.memset(Rst[:], 0.0), reads=[], writes=["Rst0", "Rst1"])
        P.add("dve", lambda e: e.memset(carry[:], 0.0), reads=[], writes=["carry"])
        P.add(PL, lambda e: e.memset(tails[:], 0.0), reads=[], writes=["tails"])
        for c in range(NCH):
            DMA("sp", xT[:], xT_in[s].rearrange("(k p) t -> p k t", p=128)[:, :, c * 512:(c + 1) * 512], [], [f"xT{k}" for k in range(8)])
            retention(s, c)
            stages = [("ret", None), ("ffn0", lambda: ffn(0, s)), ("s5", lambda: s5(s)), ("ffn1", lambda: ffn(1, s))]
            done = False
            for nm, fn in stages:
                if fn is not None and not done:
                    fn()
                if stop_after == nm and not done:
                    done = True
                    DMA("sp", outT[s].rearrange("(k p) t -> p k t", p=128)[:, :, c * 512:(c + 1) * 512], xT[:], [f"xT{k}" for k in range(8)], [])
                    while rs["n"] % NSLAB != 0:
                        ring_next(PLAN[rs["n"] % NSLAB])
            if not done:
                final_norm_out(s, c)
    P.emit()
    return nc


def prep_core_inputs(inp, seqs, W, consts):
    L = inp["x"].shape[1]
    NS = len(seqs)
    d = {}
    d["xT"] = np.ascontiguousarray(inp["x"][seqs].transpose(0, 2, 1))
    d["cT"] = np.ascontiguousarray(inp["c"][seqs].reshape(NS, 8, 128).transpose(2, 1, 0))
    d["pos"] = np.ascontiguousarray(inp["pos"][seqs].reshape(NS, L // 128, 128).transpose(0, 2, 1)).astype(np.int32)
    d["ada_w"] = np.ascontiguousarray(inp["ada_w"].reshape(2, 8, 128, 6144).transpose(0, 2, 1, 3))
    d["ada_b"] = np.ascontiguousarray(inp["ada_b"].reshape(2, 48, 128).transpose(2, 0, 1))
    d["wslab"] = W
    lam = np.stack([inp["s5_lam_re"][0].reshape(32, 128).T, inp["s5_lam_im"][0].reshape(32, 128).T,
                    np.repeat(inp["s5_log_dt"][0], 64).reshape(32, 128).T], axis=1)
    d["s5p"] = np.ascontiguousarray(lam.astype(np.float32))
    d["s5d"] = np.ascontiguousarray(inp["s5_d"][0].reshape(8, 128).T)
    d["conv_w"] = np.ascontiguousarray(inp["ffn_conv_w"][:, :, 0, :].reshape(2, 3, NJ, 128).transpose(3, 0, 2, 1))
    d["conv_b"] = np.ascontiguousarray(inp["ffn_conv_b"].reshape(2, NJ, 128).transpose(2, 0, 1))
    d["fng"] = np.ascontiguousarray(inp["final_norm_g"].reshape(8, 128).T)
    for k, v in consts.items():
        d[k] = v
    return {k: np.ascontiguousarray(v) for k, v in d.items()}


def run(inp, n_cores=8, stop_after=None, trace=False):
    inp = {k: np.asarray(v) for k, v in inp.items()}
    B, L, _ = inp["x"].shape
    NS = B // n_cores
    W = host_slabs(inp)
    consts, _ = host_consts()
    nc = build_program(NS, L, stop_after=stop_after)
    in_maps = [prep_core_inputs(inp, list(range(c * NS, (c + 1) * NS)), W, consts) for c in range(n_cores)]
    res = run_bass_kernel_spmd(nc, in_maps, core_ids=list(range(n_cores)), trace=trace)
    out = np.empty((B, L, D), np.float32)
    for c in range(n_cores):
        out[c * NS:(c + 1) * NS] = res.results[c]["outT"].transpose(0, 2, 1)
    return out, res


def kernel(**inputs):
    out, _ = run(inputs, n_cores=8)
    return out
```

```python
import math
import os
import numpy as np
import concourse.bass as bass
import concourse.mybir as mybir
from concourse.bass_utils import run_bass_kernel_spmd

F32 = mybir.dt.float32
BF16 = mybir.dt.bfloat16
I32 = mybir.dt.int32
AF = mybir.ActivationFunctionType
ALU = mybir.AluOpType

D = 1024
DFF = 2816
NJ = 22
H = 4
EPS = 1e-6
TWO_PI = float(2 * np.pi)
PI = float(np.pi)

ENGS = ("pe", "act", "dve", "pool", "sp")


class _Op:
    __slots__ = ("eng", "fn", "deps", "needs_inc", "sem", "val", "is_dma")

    def __init__(self, eng, fn, is_dma):
        self.eng = eng
        self.fn = fn
        self.deps = []
        self.needs_inc = is_dma
        self.sem = None
        self.val = 0
        self.is_dma = is_dma


class Prog:
    def __init__(self, nc, n_dma_sems=24):
        self.nc = nc
        self.ops = {e: [] for e in ENGS}
        self.last_w = {}
        self.readers = {}
        self.n_dma_sems = n_dma_sems
        self.dma_rr = 0
        self.dma_rr_sw = 0
        self.dma_last = {}
        self.dma_cnt = {}

    def add(self, eng, fn, reads=(), writes=(), is_dma=False):
        op = _Op(eng, fn, is_dma)
        deps = {}
        for k in reads:
            w = self.last_w.get(k)
            if w is not None:
                deps[id(w)] = w
        for k in writes:
            w = self.last_w.get(k)
            if w is not None:
                deps[id(w)] = w
            for r in self.readers.get(k, {}).values():
                deps[id(r)] = r
        for k in reads:
            self.readers.setdefault(k, {})[(eng, is_dma)] = op if not is_dma else op
            if is_dma:
                self.readers[k][(eng, id(op))] = op
        for k in writes:
            self.last_w[k] = op
            self.readers[k] = {}
        if is_dma:
            if eng == "pool":
                si = self.n_dma_sems - 8 + (self.dma_rr_sw % 8)
                self.dma_rr_sw += 1
            else:
                si = self.dma_rr % (self.n_dma_sems - 8)
                self.dma_rr += 1
            prev = self.dma_last.get(si)
            if prev is not None:
                deps[id(prev)] = prev
            self.dma_last[si] = op
            self.dma_cnt[si] = self.dma_cnt.get(si, 0) + 16
            op.sem = ("dma", si)
            op.val = self.dma_cnt[si]
        for d in deps.values():
            if d is op:
                continue
            if d.eng == "pe" and eng == "pe" and not d.is_dma and not is_dma:
                continue
            op.deps.append(d)
            d.needs_inc = True
        self.ops[eng].append(op)
        return op

    def emit(self):
        nc = self.nc
        sems = {}
        for e in ENGS:
            sems[("eng", e)] = nc.alloc_semaphore(f"s_{e}")
        for i in range(self.n_dma_sems):
            sems[("dma", i)] = nc.alloc_semaphore(f"s_dma{i}")
        for e in ENGS:
            c = 0
            for op in self.ops[e]:
                if op.is_dma:
                    continue
                if op.needs_inc:
                    c += 1
                    op.sem = ("eng", e)
                    op.val = c
            if os.environ.get("KDEBUG"):
                print("engine", e, "ops", len(self.ops[e]), "incs", c, flush=True)
        if os.environ.get("KDEBUG"):
            print("dma sem counts", self.dma_cnt, flush=True)
        engmap = {"pe": "tensor", "act": "scalar", "dve": "vector", "pool": "gpsimd", "sp": "sync"}
        with nc.Block() as block:
            for e in ENGS:
                ops = self.ops[e]
                if not ops:
                    continue

                def body(eng, ops=ops):
                    waited = {}
                    for op in ops:
                        for d in op.deps:
                            if waited.get(d.sem, 0) >= d.val:
                                continue
                            eng.wait_ge(sems[d.sem], d.val)
                            waited[d.sem] = d.val
                        ins = op.fn(eng)
                        if op.needs_inc:
                            ins.then_inc(sems[op.sem], 16 if op.is_dma else 1)
                    for op in ops:
                        if op.is_dma and waited.get(op.sem, 0) < op.val:
                            eng.wait_ge(sems[op.sem], op.val)
                            waited[op.sem] = op.val

                getattr(block, engmap[e])(body)


SLAB = 4096
def slab_plan():
    plan = []
    for h in range(H):
        for cg in ("qk", "v", "g"):
            plan.append(("win", h, cg))
    for mp in range(4):
        plan.append(("wout", mp))
    for jp in range(NJ // 2):
        plan.append(("up0", jp))
    for m in range(8):
        plan.append(("dn0", m))
    for half in range(2):
        plan.append(("s5in", half))
    for nm in ("Br", "Bi", "Cr", "Cin"):
        plan.append(("s5" + nm,))
    for mp in range(4):
        plan.append(("glu", mp))
    for jp in range(NJ // 2):
        plan.append(("up1", jp))
    for m in range(8):
        plan.append(("dn1", m))
    return plan


PLAN = slab_plan()
NSLAB = len(PLAN)
SLAB_IDX = {k: i for i, k in enumerate(PLAN)}
CIN_IDX = SLAB_IDX[("s5Cin",)]


def host_slabs(inp):
    W = np.zeros((NSLAB, 128, SLAB), np.float32)
    w_in = inp["ret_w_in"][0].reshape(8, 128, 6144)
    for h in range(H):
        cols = {
            "qk": np.concatenate([w_in[:, :, h * 256:(h + 1) * 256], w_in[:, :, 1024 + h * 256:1024 + (h + 1) * 256]], axis=2),
            "v": w_in[:, :, 2048 + h * 512:2048 + (h + 1) * 512],
            "g": w_in[:, :, 4096 + h * 512:4096 + (h + 1) * 512],
        }
        for cg in ("qk", "v", "g"):
            W[SLAB_IDX[("win", h, cg)]] = cols[cg].transpose(1, 0, 2).reshape(128, SLAB)
    w_out = inp["ret_w_out"][0].reshape(16, 128, 1024)
    for mp in range(4):
        W[SLAB_IDX[("wout", mp)]] = w_out[:, :, mp * 256:(mp + 1) * 256].transpose(1, 0, 2).reshape(128, SLAB)
    for l in range(2):
        w_up = inp["ffn_w_up"][l].reshape(8, 128, 2 * DFF)
        for jp in range(NJ // 2):
            blk = np.zeros((128, 2, 8, 256), np.float32)
            for jj in range(2):
                j = jp * 2 + jj
                blk[:, jj, :, 0:128] = w_up[:, :, j * 128:(j + 1) * 128].transpose(1, 0, 2)
                blk[:, jj, :, 128:256] = w_up[:, :, DFF + j * 128:DFF + (j + 1) * 128].transpose(1, 0, 2)
            W[SLAB_IDX[(f"up{l}", jp)]] = blk.reshape(128, SLAB)
        w_dn = inp["ffn_w_down"][l].reshape(NJ, 128, 1024)
        for m in range(8):
            W[SLAB_IDX[(f"dn{l}", m)], :, 0:NJ * 128] = w_dn[:, :, m * 128:(m + 1) * 128].transpose(1, 0, 2).reshape(128, NJ * 128)
    s5in = inp["s5_w_in"][0].reshape(8, 128, 1024)
    for half in range(2):
        W[SLAB_IDX[("s5in", half)]] = s5in[:, :, half * 512:(half + 1) * 512].transpose(1, 0, 2).reshape(128, SLAB)
    for nm, key in (("Br", "s5_b_re"), ("Bi", "s5_b_im")):
        B = inp[key][0]
        blk = np.zeros((128, 32, 128), np.float32)
        for ct in range(32):
            for gp in range(2):
                g = 2 * ct + gp
                r0 = 32 * (ct % 4) + 16 * gp
                blk[r0:r0 + 16, ct, gp * 64:(gp + 1) * 64] = B[g].T
        W[SLAB_IDX[("s5" + nm,)]] = blk.reshape(128, SLAB)
    for nm, key in (("Cr", "s5_c_re"), ("Cin", "s5_c_im")):
        C = inp[key][0]
        blk = np.zeros((128, 32, 128), np.float32)
        for ct in range(32):
            for gp in range(2):
                g = 2 * ct + gp
                c0 = 32 * (ct % 4) + 16 * gp
                blk[gp * 64:(gp + 1) * 64, ct, c0:c0 + 16] = C[g].T
        W[SLAB_IDX[("s5" + nm,)]] = blk.reshape(128, SLAB)
    wg = inp["s5_w_glu"][0].reshape(8, 128, 2048)
    for mp in range(4):
        blk = np.zeros((128, 8, 2, 256), np.float32)
        for mm in range(2):
            m = mp * 2 + mm
            blk[:, :, mm, 0:128] = wg[:, :, m * 128:(m + 1) * 128].transpose(1, 0, 2)
            blk[:, :, mm, 128:256] = wg[:, :, 1024 + m * 128:1024 + (m + 1) * 128].transpose(1, 0, 2)
        W[SLAB_IDX[("glu", mp)]] = blk.reshape(128, SLAB)
    return W


def host_consts():
    half = 128
    inv_freq = np.power(np.float32(10000.0), -np.arange(half, dtype=np.float32) / np.float32(half)).astype(np.float32)
    c = {}
    c["ident"] = np.eye(128, dtype=np.float32)
    c["ones"] = np.ones((128, 128), np.float32)
    m = np.arange(128)
    c["mask01"] = (m[None, :] >= m[:, None]).astype(np.float32)
    c["invf"] = np.broadcast_to(inv_freq[None, :], (128, 128)).copy()
    lg = np.log1p(-np.exp2(-5.0 - np.arange(H, dtype=np.float64)))
    p1 = (np.arange(128, dtype=np.float64) + 1.0)[:, None]
    dq = np.exp(lg[None, :] * p1)
    dk = np.exp(-lg[None, :] * p1) * (256.0 ** -0.5)
    c["dqk"] = np.concatenate([dq, dk], axis=1).astype(np.float32)
    c["iota1"] = np.broadcast_to((np.arange(512, dtype=np.float32) + 1.0)[None, :], (128, 512)).copy()
    return c, [float(np.exp(lg[h] * 128.0)) for h in range(H)]


_, GCH = host_consts()


def build_program(NS, L, stop_after=None, gelu_mode=None):
    if gelu_mode is None:
        gelu_mode = os.environ.get("K_GELU", "act")
    NCH = L // 512
    NTT = L // 128
    nc = bass.Bass("TRN2", target_bir_lowering=False)
    P = Prog(nc)

    def din(name, shape, dt=F32):
        return nc.dram_tensor(name, list(shape), dt, kind="ExternalInput").ap()

    xT_in = din("xT", [NS, D, L])
    cT_in = din("cT", [128, 8, NS])
    pos_in = din("pos", [NS, 128, NTT], I32)
    adaw_in = din("ada_w", [2, 128, 8, 6144])
    adab_in = din("ada_b", [128, 2, 48])
    wslab_in = din("wslab", [NSLAB, 128, SLAB])
    s5p_in = din("s5p", [128, 3, 32])
    s5d_in = din("s5d", [128, 8])
    convw_in = din("conv_w", [128, 2, NJ, 3])
    convb_in = din("conv_b", [128, 2, NJ])
    fng_in = din("fng", [128, 8])
    ident_in = din("ident", [128, 128])
    ones_in = din("ones", [128, 128])
    mask_in = din("mask01", [128, 128])
    invf_in = din("invf", [128, 128])
    dqk_in = din("dqk", [128, 8])
    iota_in = din("iota1", [128, 512])
    outT = nc.dram_tensor("outT", [NS, D, L], F32, kind="ExternalOutput").ap()
    wb = nc.dram_tensor("wb", [NSLAB, 128, SLAB], BF16).ap()
    tabs = nc.dram_tensor("tabs", [32, 128, 4, 512], F32).ap()

    def sb(name, shape, dt=F32):
        return nc.alloc_sbuf_tensor("sb_" + name, list(shape), dt)

    banks = [nc.alloc_psum_tensor(f"pb{i}", [128, 512], F32) for i in range(8)]
    b7 = banks[7][:].bitcast(BF16)
    b1 = banks[1][:].bitcast(BF16)

    xT = sb("xT", [128, 8, 512])
    hT = sb("hT", [128, 8, 512], BF16)
    rstd = sb("rstd", [128, 512])
    sq = [sb(f"sq{i}", [128, 512], BF16) for i in range(2)]
    tmpf = [sb(f"tmpf{i}", [128, 512]) for i in range(2)]
    NB = 6
    ring = [sb(f"ring{i}", [128, SLAB], BF16) for i in range(NB)]
    modT = sb("modT", [128, 2, 48, NS])
    cond = sb("cond", [128, 8, NS], BF16)
    cin = sb("cin", [128, 8, NS])
    adab = sb("adab", [128, 2, 48])
    ident = sb("ident", [128, 128], BF16)
    ones = sb("ones", [128, 128], BF16)
    mask01 = sb("mask01", [128, 128])
    invf = sb("invf", [128, 128])
    dqk = sb("dqk", [128, 8])
    posi = sb("posi", [128, NS, NTT], I32)
    posf = sb("posf", [128, NS, NTT])
    convw = sb("convw", [128, 2, NJ, 3])
    convb = sb("convb", [128, 2, NJ])
    fng = sb("fng", [128, 8])
    s5d = sb("s5d", [128, 8])
    s5p = sb("s5p", [128, 3, 32])
    rho = sb("rho", [128, 32])
    rABCD = sb("rABCD", [128, 4, 2, 128])
    ang = rABCD
    cs = sb("cs", [128, 4, 2, 128])
    mm4 = sb("mm4", [128, 4, 512])
    m1 = mm4[:, 0, :]; m2 = mm4[:, 1, :]; m3 = mm4[:, 2, :]; m4 = mm4[:, 3, :]
    oraw = mm4[:].rearrange("p a c -> p (a c)").bitcast(BF16).rearrange("p (a c) -> p a c", a=8)
    angm = mm4[:, 0:2, :]
    angi = mm4[:, 2:4, :].bitcast(I32)
    rA = rABCD[:, 0]; rB = rABCD[:, 1]; rC = rABCD[:, 2]; rD = rABCD[:, 3]
    RK = ["rA", "rB", "rC", "rD"]
    rot = sb("rot", [128, 2, 2, 128])
    qk_tm = [sb(f"qk_tm{i}", [128, 2, 256], BF16) for i in range(2)]
    qkT = [sb(f"qkT{i}", [128, 4, 128], BF16) for i in range(2)]
    v_sb = [sb(f"v_sb{i}", [128, 512], BF16) for i in range(2)]
    g_sb4 = sb("g_sb4", [128, 4, 512], BF16)
    st6x = sb("st6x", [128, 2, 4, 6]); mv4 = sb("mv4", [128, 2, 4, 2])
    gn = sb("gn", [128, 2, 6, 4])
    sT_sb = sb("sT_sb", [128, 128], BF16)
    Rst = sb("Rst", [128, H, 2, 512])
    Sbs = [sb(f"Sb{i}", [128, 2, 512], BF16) for i in range(2)]
    st6 = sb("st6", [128, 6]); mv = sb("mv", [128, 4])
    yn = sb("yn", [128, 512], BF16)
    y_tm = sb("y_tm", [128, 512], BF16)
    aT = sb("aT", [128, NJ, 512], BF16)
    yT = aT[:, 0:16, :]
    gbuf = [sb(f"gbuf{i}", [128, 514]) for i in range(2)]
    acc = [sb(f"acc{i}", [128, 512]) for i in range(2)]
    sil = [sb(f"sil{i}", [128, 512]) for i in range(2)]
    tails = sb("tails", [128, 2, NJ, 2])
    uT_bf = aT[:, 0:8, :]
    tab = [sb(f"tab{i}", [128, 4, 512]) for i in range(2)]
    wri = sb("wri", [128, 2, 512])
    zri = sb("zri", [128, 2, 512])
    s_bf = [sb(f"s_bf{i}", [128, 4, 512], BF16) for i in range(2)]
    cwk = sb("cwk", [128, 4])
    carry = sb("carry", [128, 2, 32])
    zl_all = sb("zl_all", [128, 2, 32])
    cslast = sb("cslast", [128, 2, 32])
    cw6 = sb("cw6", [128, 4, 32])
    yg = hT

    PL = os.environ.get("K_POOL", "dve")

    def MM(out, lhsT, rhs, start, stop, r, w):
        P.add("pe", lambda e: e.matmul(out, lhsT, rhs, start=start, stop=stop), reads=r, writes=w)

    def TR(out, in_, r, w):
        P.add("pe", lambda e: e.transpose(out, in_, ident[:]), reads=list(r) + ["ident"], writes=w)

    def ACT(out, in_, func, r, w, bias=0.0, scale=1.0):
        P.add("act", lambda e: e.activation(out=out, in_=in_, func=func, bias=bias, scale=scale), reads=r, writes=w)

    def TT(eng, out, in0, in1, op, r, w):
        P.add(eng, lambda e: e.tensor_tensor(out=out, in0=in0, in1=in1, op=op), reads=r, writes=w)

    def TS(eng, out, in0, s1, s2, op0, op1, r, w):
        if s2 is None:
            P.add(eng, lambda e: e.tensor_scalar(out=out, in0=in0, scalar1=s1, scalar2=None, op0=op0), reads=r, writes=w)
        else:
            P.add(eng, lambda e: e.tensor_scalar(out=out, in0=in0, scalar1=s1, scalar2=s2, op0=op0, op1=op1), reads=r, writes=w)

    def STT(eng, out, in0, scalar, in1, op0, op1, r, w):
        P.add(eng, lambda e: e.scalar_tensor_tensor(out=out, in0=in0, scalar=scalar, in1=in1, op0=op0, op1=op1), reads=r, writes=w)

    def CP(eng, out, in_, r, w):
        P.add(eng, lambda e: e.tensor_copy(out=out, in_=in_), reads=r, writes=w)

    def DMA(eng, out, in_, r, w):
        P.add(eng, lambda e: e.dma_start(out=out, in_=in_), reads=r, writes=w, is_dma=True)

    def range_reduce_sin(eng, x, xi, xm, out, kx, ki, km, kout):
        TS(eng, xi, x, float(1.0 / TWO_PI), None, ALU.mult, None, kx, ki)
        STT(eng, x, xi, -TWO_PI, x, ALU.mult, ALU.add, ki + kx, kx)
        TS(eng, xm, x, PI, -TWO_PI, ALU.is_gt, ALU.mult, kx, km)
        TT(eng, x, x, xm, ALU.add, kx + km, kx)
        TS(eng, xm, x, -PI, TWO_PI, ALU.is_lt, ALU.mult, kx, km)
        TT(eng, x, x, xm, ALU.add, kx + km, kx)
        ACT(out, x, AF.Sin, kx, kout)

    for (dst, src, nm) in ((mask01, mask_in, "mask01"), (invf, invf_in, "invf"), (dqk, dqk_in, "dqk"),
                           (adab, adab_in, "adab"), (cin, cT_in, "cin"), (convw, convw_in, "convw"),
                           (convb, convb_in, "convb"), (fng, fng_in, "fng"), (s5d, s5d_in, "s5d"),
                           (s5p, s5p_in, "s5p"), (posi, pos_in.rearrange("s p n -> p s n"), "posi")):
        DMA("sp", dst[:], src, [], [nm])
    DMA("pool", ident[:], ident_in, [], ["ident"])
    DMA("pool", ones[:], ones_in, [], ["ones"])
    GRP = 4
    for i0 in range(0, NSLAB, GRP):
        i1 = min(NSLAB, i0 + GRP)
        idxs = [i for i in range(i0, i1) if i != CIN_IDX]
        runs = []
        for i in idxs:
            if runs and runs[-1][1] == i:
                runs[-1][1] = i + 1
            else:
                runs.append([i, i + 1])
        for a, b in runs:
            DMA("pool", wb[a:b].rearrange("s p f -> (s p) f"), wslab_in[a:b].rearrange("s p f -> (s p) f"), [], [f"wb{i}" for i in range(a, b)])
    cst = aT[:].rearrange("p j c -> p (j c)").bitcast(F32)[:, 0:SLAB]
    DMA("sp", cst, wslab_in[CIN_IDX], [], ["aT"])
    TS("dve", ring[0][:], cst, -1.0, None, ALU.mult, None, ["aT"], ["ring0"])
    DMA("sp", wb[CIN_IDX], ring[0][:], ["ring0"], [f"wb{CIN_IDX}"])

    CP("dve", posf[:], posi[:], ["posi"], ["posf"])
    ACT(cond[:], cin[:], AF.Silu, ["cin"], ["cond"])
    for l in range(2):
        for cgp in range(12):
            slot = ring[1 + (cgp % 2)]
            key = f"ring{1 + (cgp % 2)}"
            DMA("pool", slot[:].rearrange("p (k c) -> p k c", k=8), adaw_in[l][:, :, cgp * 512:(cgp + 1) * 512], [], [key])
            sv = slot[:].rearrange("p (k c) -> p k c", k=8)
            for mm in range(4):
                m = cgp * 4 + mm
                for k in range(8):
                    MM(banks[0][:, m * NS:(m + 1) * NS], sv[:, k, mm * 128:(mm + 1) * 128], cond[:, k, :], k == 0, k == 7,
                       [key, "cond"], ["bank0"])
        bv = banks[0][:, 0:48 * NS].rearrange("p (m s) -> p m s", s=NS)
        for s in range(NS):
            TT("dve", modT[:, l, :, s], bv[:, :, s], adab[:, l, :], ALU.add, ["bank0", "adab"], ["modT"])
    for l in range(2):
        for c0 in (8, 32):
            TS("dve", modT[:, l, c0:c0 + 8, :], modT[:, l, c0:c0 + 8, :], 1.0, None, ALU.add, None, ["modT"], ["modT"])

    def mod(l, which, k, s):
        base = {"sh1": 0, "sc1": 8, "g1": 16, "sh2": 24, "sc2": 32, "g2": 40}[which]
        return modT[:, l, base + k, s:s + 1]

    lr = s5p[:, 0, :]; li = s5p[:, 1, :]; ldt = s5p[:, 2, :]
    sw = sb("s5w", [128, 16, 32])
    swi = sb("s5wi", [128, 2, 32], I32)
    dtp = sw[:, 0, :]; xx = sw[:, 1, :]; th = sw[:, 2, :]
    ACT(dtp, ldt, AF.Exp, ["s5p"], ["s5w"])
    TT("dve", xx, lr, dtp, ALU.mult, ["s5p", "s5w"], ["s5w"])
    TT("dve", th, li, dtp, ALU.mult, ["s5p", "s5w"], ["s5w"])
    ACT(rho[:], xx, AF.Exp, ["s5w"], ["rho"])
    TS("dve", sw[:, 3, :], th, 1.0, None, ALU.mult, None, ["s5w"], ["s5w"])
    TS("dve", sw[:, 4, :], th, float(PI / 2), None, ALU.add, None, ["s5w"], ["s5w"])
    range_reduce_sin("dve", sw[:, 3:5, :], swi[:, :, :], sw[:, 5:7, :], sw[:, 7:9, :], ["s5w"], ["s5wi"], ["s5w"], ["s5w"])
    sn0 = sw[:, 7, :]; cs0 = sw[:, 8, :]
    ar = sw[:, 9, :]; ai = sw[:, 10, :]
    TT("dve", ar, rho[:], cs0, ALU.mult, ["rho", "s5w"], ["s5w"])
    TT("dve", ai, rho[:], sn0, ALU.mult, ["rho", "s5w"], ["s5w"])
    nr = sw[:, 11, :]
    TS("dve", nr, ar, -1.0, None, ALU.add, None, ["s5w"], ["s5w"])
    den = sw[:, 12, :]; t0 = sw[:, 13, :]
    TT("dve", den, lr, lr, ALU.mult, ["s5p", "s5w"], ["s5w"])
    TT("dve", t0, li, li, ALU.mult, ["s5p", "s5w"], ["s5w"])
    TT("dve", den, den, t0, ALU.add, ["s5w"], ["s5w"])
    P.add("dve", lambda e: e.reciprocal(out=den, in_=den), reads=["s5w"], writes=["s5w"])
    fcoef = sb("fcoef", [128, 3, 32])
    TT("dve", t0, nr, lr, ALU.mult, ["s5w", "s5p"], ["s5w"])
    TT("dve", sw[:, 14, :], ai, li, ALU.mult, ["s5w", "s5p"], ["s5w"])
    TT("dve", t0, t0, sw[:, 14, :], ALU.add, ["s5w"], ["s5w"])
    TT("dve", fcoef[:, 0, :], t0, den, ALU.mult, ["s5w"], ["fcoef"])
    TT("dve", t0, ai, lr, ALU.mult, ["s5w", "s5p"], ["s5w"])
    TT("dve", sw[:, 14, :], nr, li, ALU.mult, ["s5w", "s5p"], ["s5w"])
    TT("dve", t0, t0, sw[:, 14, :], ALU.subtract, ["s5w"], ["s5w"])
    TT("dve", fcoef[:, 1, :], t0, den, ALU.mult, ["s5w"], ["fcoef"])
    TS("dve", fcoef[:, 2, :], fcoef[:, 0, :], -1.0, None, ALU.mult, None, ["fcoef"], ["fcoef"])
    iota1 = rstd
    DMA("sp", iota1[:], iota_in, [], ["rstd"])
    phx = zri; phi_ = angi
    phm = s_bf[0][:].rearrange("p a c -> p (a c)").bitcast(F32).rearrange("p (a c) -> p a c", a=2)
    thv = sb("thv", [128, 32])
    CP("dve", thv[:], th, ["s5w"], ["thv"])
    for ct in range(32):
        tb = tab[ct % 2]
        tk = f"tab{ct % 2}"
        TS("dve", phx[:, 0, :], iota1[:], thv[:, ct:ct + 1], None, ALU.mult, None, ["rstd", "thv"], ["zri"])
        TS("dve", phx[:, 1, :], phx[:, 0, :], float(PI / 2), None, ALU.add, None, ["zri"], ["zri"])
        range_reduce_sin("dve", phx[:], phi_, phm, wri[:], ["zri"], ["m3", "m4"], ["s_bf0"], ["wri"])
        CP(PL, tb[:, 3, :], wri[:, 0, :], ["wri"], [tk])
        CP(PL, tb[:, 2, :], wri[:, 1, :], ["wri"], [tk])
        CP(PL, cslast[:, 0, ct:ct + 1], wri[:, 1, 511:512], ["wri"], ["cslast"])
        CP(PL, cslast[:, 1, ct:ct + 1], wri[:, 0, 511:512], ["wri"], ["cslast"])
        TS("dve", m1, wri[:, 1, :], fcoef[:, 0, ct:ct + 1], None, ALU.mult, None, ["wri", "fcoef"], ["m1"])
        STT("dve", tb[:, 0, :], wri[:, 0, :], fcoef[:, 1, ct:ct + 1], m1, ALU.mult, ALU.add, ["wri", "fcoef", "m1"], [tk])
        TS("dve", m2, wri[:, 1, :], fcoef[:, 1, ct:ct + 1], None, ALU.mult, None, ["wri", "fcoef"], ["m2"])
        STT("dve", tb[:, 1, :], wri[:, 0, :], fcoef[:, 2, ct:ct + 1], m2, ALU.mult, ALU.add, ["wri", "fcoef", "m2"], [tk])
        DMA("sp", tabs[ct], tb[:], [tk], [f"tabs{ct}"])

    rs = {"n": 0, "loaded": 0, "total": NS * NCH * NSLAB}

    def slab_cols(key):
        return NJ * 128 if key[0] in ("dn0", "dn1") else SLAB

    def ring_next(expect, live=1):
        n = rs["n"]
        assert PLAN[n % NSLAB] == expect, (PLAN[n % NSLAB], expect)
        lim = min(rs["total"], n + NB - live + 1)
        while rs["loaded"] < lim:
            j = rs["loaded"]
            si = j % NSLAB
            nc_ = slab_cols(PLAN[si])
            DMA("sp", ring[j % NB][:, 0:nc_], wb[si][:, 0:nc_], [f"wb{si}"], [f"ring{j % NB}"])
            rs["loaded"] += 1
        rs["n"] = n + 1
        return ring[n % NB], f"ring{n % NB}"

    def norm_mod(l, which_sc, which_sh, s):
        for k in range(8):
            if k % 2 == 0:
                ACT(sq[0][:], xT[:, k, :], AF.Square, [f"xT{k}"], ["sq0"])
            else:
                TT("dve", sq[1][:], xT[:, k, :], xT[:, k, :], ALU.mult, [f"xT{k}"], ["sq1"])
            MM(banks[5][:], ones[:], sq[k % 2][:], k == 0, k == 7, ["ones", f"sq{k % 2}"], ["bank5"])
        ACT(rstd[:], banks[5][:], AF.Sqrt, ["bank5"], ["rstd"], bias=EPS, scale=1.0 / D)
        P.add("dve", lambda e: e.reciprocal(out=rstd[:], in_=rstd[:]), reads=["rstd"], writes=["rstd"])
        for k in range(8):
            t = tmpf[k % 2]
            TT("dve", t[:], xT[:, k, :], rstd[:], ALU.mult, [f"xT{k}", "rstd"], [f"tmpf{k % 2}"])
            ACT(hT[:, k, :], t[:], AF.Identity, [f"tmpf{k % 2}", "modT"], ["hT"],
                bias=mod(l, which_sh, k, s), scale=mod(l, which_sc, k, s))

    def ffn(l, s):
        norm_mod(l, "sc2", "sh2", s)
        for jp in range(NJ // 2):
            slot, key = ring_next((f"up{l}", jp))
            wv = slot[:].rearrange("p (j k c) -> p j k c", j=2, k=8)
            for jj in range(2):
                j = jp * 2 + jj
                pv = banks[(j % 3) * 2]; pg = banks[(j % 3) * 2 + 1]
                kv = f"bank{(j % 3) * 2}"; kg = f"bank{(j % 3) * 2 + 1}"
                for k in range(8):
                    MM(pv[:], wv[:, jj, k, 0:128], hT[:, k, :], k == 0, k == 7, [key, "hT"], [kv])
                for k in range(8):
                    MM(pg[:], wv[:, jj, k, 128:256], hT[:, k, :], k == 0, k == 7, [key, "hT"], [kg])
                gb = gbuf[j % 2]; gk = f"gbuf{j % 2}"
                ac = acc[j % 2]; ak = f"acc{j % 2}"
                sl = sil[j % 2]; sk = f"sil{j % 2}"
                ACT(gb[:, 0:2], tails[:, l, j, :], AF.Copy, ["tails"], [gk])
                ACT(gb[:, 2:514], pg[:], AF.Copy, [kg], [gk])
                ACT(ac[:], pg[:], AF.Identity, [kg, "convw", "convb"], [ak], bias=convb[:, l, j:j + 1], scale=convw[:, l, j, 2:3])
                ACT(tails[:, l, j, :], gb[:, 512:514], AF.Copy, [gk], ["tails"])
                STT("dve", ac[:], gb[:, 1:513], convw[:, l, j, 1:2], ac[:], ALU.mult, ALU.add, [gk, "convw", ak], [ak])
                STT("dve", ac[:], gb[:, 0:512], convw[:, l, j, 0:1], ac[:], ALU.mult, ALU.add, [gk, "convw", ak], [ak])
                ACT(sl[:], ac[:], AF.Silu, [ak], [sk])
                TT("dve", aT[:, j, :], sl[:], pv[:], ALU.mult, [sk, kv], ["aT"])
        for m in range(8):
            slot, key = ring_next((f"dn{l}", m))
            wv = slot[:, 0:NJ * 128].rearrange("p (j c) -> p j c", j=NJ)
            pb = banks[6 + (m % 2)]; pk = f"bank{6 + (m % 2)}" if m % 2 == 0 else "bank7"
            for j in range(NJ):
                MM(pb[:], wv[:, j, :], aT[:, j, :], j == 0, j == NJ - 1, [key, "aT"], [pk])
            STT("dve", xT[:, m, :], pb[:], mod(l, "g2", m, s), xT[:, m, :], ALU.mult, ALU.add, [pk, "modT", f"xT{m}"], [f"xT{m}"])

    def retention(s, c):
        l = 0
        norm_mod(l, "sc1", "sh1", s)
        for nt in range(4):
            ntg = c * 4 + nt
            TS("dve", ang[:, nt, 0, :], invf[:], posf[:, s, ntg:ntg + 1], None, ALU.mult, None, ["invf", "posf"], RK)
            TS("dve", ang[:, nt, 1, :], ang[:, nt, 0, :], float(PI / 2), None, ALU.add, None, RK, RK)
        range_reduce_sin("dve", ang[:].rearrange("p a b c -> p (a b c)"), angi.rearrange("p a c -> p (a c)"),
                         angm.rearrange("p a c -> p (a c)"), cs[:].rearrange("p a b c -> p (a b c)"),
                         RK, ["m3", "m4"], ["m1", "m2"], ["cs"])
        slabs_h = {}

        def stage_A(h, i):
            if i == 0:
                slabs_h[h] = [ring_next(("win", h, cg), live=li + 1) for li, cg in enumerate(("qk", "v", "g"))]
            slabs = slabs_h[h]
            par = i % 2
            for ci in range(3):
                slot, key = slabs[ci]
                wv = slot[:].rearrange("p (k c) -> p k c", k=8)
                for k in range(8):
                    MM(banks[2 + ci][:], hT[:, k, i * 128:(i + 1) * 128], wv[:, k, :], k == 0, k == 7, ["hT", key], [f"bank{2 + ci}"])
            qv = banks[2][:].rearrange("p (a b c) -> p a b c", a=2, b=2)
            t1 = qv[:, :, 0, :]; t2 = qv[:, :, 1, :]
            cosb = cs[:, i, 1, :].unsqueeze(1).to_broadcast([128, 2, 128])
            sinb = cs[:, i, 0, :].unsqueeze(1).to_broadcast([128, 2, 128])
            TT("dve", rA, t1, cosb, ALU.mult, ["bank2", "cs"], ["rA"])
            TT("dve", rB, t2, sinb, ALU.mult, ["bank2", "cs"], ["rB"])
            TT("dve", rC, t1, sinb, ALU.mult, ["bank2", "cs"], ["rC"])
            TT("dve", rD, t2, cosb, ALU.mult, ["bank2", "cs"], ["rD"])
            TT("dve", rot[:, :, 0, :], rA, rB, ALU.subtract, ["rA", "rB"], ["rot"])
            TT("dve", rot[:, :, 1, :], rC, rD, ALU.add, ["rC", "rD"], ["rot"])
            qt = qk_tm[par]; qtk = f"qk_tm{par}"
            ACT(qt[:, 0, :], rot[:, 0, :, :].rearrange("p b c -> p (b c)"), AF.Identity, ["rot", "dqk"], [qtk], scale=dqk[:, h:h + 1])
            ACT(qt[:, 1, :], rot[:, 1, :, :].rearrange("p b c -> p (b c)"), AF.Identity, ["rot", "dqk"], [qtk], scale=dqk[:, 4 + h:5 + h])
            ACT(v_sb[par][:], banks[3][:], AF.Copy, ["bank3"], [f"v_sb{par}"])
            gbuf_h = g_sb4 if h % 2 == 0 else s_bf[1]
            ACT(gbuf_h[:, i, :], banks[4][:], AF.Silu, ["bank4"], [f"g{h % 2}_{i}"])
            for a in range(2):
                for dd in range(2):
                    TR(b7[:, (a * 2 + dd) * 128:(a * 2 + dd + 1) * 128], qt[:, a, dd * 128:(dd + 1) * 128], [qtk], ["bank7"])
            qT = qkT[par]; qTk = f"qkT{par}"
            CP("dve", qT[:].rearrange("p a c -> p (a c)"), b7[:, 0:512], ["bank7"], [qTk])

        def stage_B(h, i):
            par = i % 2
            qt = qk_tm[par]; qtk = f"qk_tm{par}"
            qT = qkT[par]; qTk = f"qkT{par}"
            Sin = Sbs[i % 2]; Sink = f"Sb{i % 2}"
            Sout = Sbs[(i + 1) % 2]; Soutk = f"Sb{(i + 1) % 2}"
            if i == 0:
                for dd in range(2):
                    ACT(Sin[:, dd, :], Rst[:, h, dd, :], AF.Copy, [f"Rst{dd}"], [Sink], scale=GCH[h])
            def dkv(dd):
                MM(banks[0][:], qt[:, 1, dd * 128:(dd + 1) * 128], v_sb[par][:], True, True, [qtk, f"v_sb{par}"], ["bank0"])
                STT("dve", Rst[:, h, dd, :], Rst[:, h, dd, :], GCH[h], banks[0][:], ALU.mult, ALU.add, [f"Rst{dd}", "bank0"], [f"Rst{dd}"])
                if i < 3:
                    ACT(Sout[:, dd, :], Rst[:, h, dd, :], AF.Copy, [f"Rst{dd}"], [Soutk], scale=GCH[h])
            for dd in range(2):
                MM(banks[5][:, 0:128], qT[:, 2 + dd, :], qT[:, dd, :], dd == 0, dd == 1, [qTk], ["bank5"])
            TT("dve", sT_sb[:], banks[5][:, 0:128], mask01[:], ALU.mult, ["bank5", "mask01"], ["sT_sb"])
            dkv(0)
            for dd in range(2):
                MM(banks[6][:], qT[:, dd, :], Sin[:, dd, :], dd == 0, False, [qTk, Sink], ["bank6"])
            MM(banks[6][:], sT_sb[:], v_sb[par][:], False, True, ["sT_sb", f"v_sb{par}"], ["bank6"])
            dkv(1)
            hp = h % 2
            P.add("dve", lambda e: e.bn_stats(out=st6x[:, hp, i, :], in_=banks[6][:]), reads=["bank6"], writes=[f"st6x{hp}"])
            ACT(oraw[:, hp * 4 + i, :], banks[6][:], AF.Copy, ["bank6"], [f"or{hp}_{i}"])

        def stage_Chead(h):
            hp = h % 2
            var = gn[:, hp, 2, :]; rs_ = gn[:, hp, 3, :]; nb = gn[:, hp, 4, :]
            for i in range(4):
                P.add("dve", lambda e, i=i: e.bn_aggr(out=mv4[:, hp, i, :], in_=st6x[:, hp, i, :]), reads=[f"st6x{hp}"], writes=[f"mv4{hp}"])
            ACT(var, mv4[:, hp, :, 1], AF.Sqrt, [f"mv4{hp}"], [f"gn{hp}"], bias=EPS, scale=1.0)
            P.add("dve", lambda e: e.reciprocal(out=rs_, in_=var), reads=[f"gn{hp}"], writes=[f"gn{hp}"])
            STT("dve", nb, mv4[:, hp, :, 0], -1.0, rs_, ALU.mult, ALU.mult, [f"gn{hp}", f"mv4{hp}"], [f"gn{hp}"])

        def stage_Ctile_pre(h, i):
            hp = h % 2
            gbuf_h = g_sb4 if hp == 0 else s_bf[1]
            ACT(yn[:], oraw[:, hp * 4 + i, :], AF.Identity, [f"or{hp}_{i}", f"gn{hp}"], ["yn"], bias=gn[:, hp, 4, i:i + 1], scale=gn[:, hp, 3, i:i + 1])
            TT("dve", y_tm[:], yn[:], gbuf_h[:, i, :], ALU.mult, ["yn", f"g{hp}_{i}"], ["y_tm"])

        def stage_Ctile_post(h, i):
            for vt in range(4):
                TR(b1[:, vt * 128:(vt + 1) * 128], y_tm[:, vt * 128:(vt + 1) * 128], ["y_tm"], ["bank1"])
            ACT(yT[:, h * 4:(h + 1) * 4, i * 128:(i + 1) * 128], b1[:, 0:512].rearrange("p (a c) -> p a c", a=4), AF.Copy, ["bank1"], ["aT"])

        order = [(h, i) for h in range(H) for i in range(4)]
        stage_A(*order[0])
        for idx, (h, i) in enumerate(order):
            if h > 0:
                stage_Ctile_pre(h - 1, i)
            if idx + 1 < len(order):
                stage_A(*order[idx + 1])
            if h > 0:
                stage_Ctile_post(h - 1, i)
            stage_B(h, i)
            if i == 3:
                stage_Chead(h)
        for i in range(4):
            stage_Ctile_pre(H - 1, i)
            stage_Ctile_post(H - 1, i)
        for mp in range(4):
            slot, key = ring_next(("wout", mp))
            wv = slot[:].rearrange("p (k c) -> p k c", k=16)
            for mm in range(2):
                m = mp * 2 + mm
                pb = banks[m % 2]; pk = f"bank{m % 2}"
                for kt in range(16):
                    MM(pb[:], wv[:, kt, mm * 128:(mm + 1) * 128], yT[:, kt, :], kt == 0, kt == 15, [key, "aT"], [pk])
                STT("dve", xT[:, m, :], pb[:], mod(l, "g1", m, s), xT[:, m, :], ALU.mult, ALU.add, [pk, "modT", f"xT{m}"], [f"xT{m}"])

    S5LV = int(os.environ.get("S5LV", "9"))

    def s5(s):
        l = 1
        norm_mod(l, "sc1", "sh1", s)
        for half in range(2):
            slot, key = ring_next(("s5in", half))
            wv = slot[:].rearrange("p (k c) -> p k c", k=8)
            for mm in range(4):
                m = half * 4 + mm
                pb = banks[m % 2]; pk = f"bank{m % 2}"
                for k in range(8):
                    MM(pb[:], wv[:, k, mm * 128:(mm + 1) * 128], hT[:, k, :], k == 0, k == 7, [key, "hT"], [pk])
                ACT(uT_bf[:, m, :], pb[:], AF.Copy, [pk], ["aT"])
        sBr, kBr = ring_next(("s5Br",), live=1); sBi, kBi = ring_next(("s5Bi",), live=2)
        sCr, kCr = ring_next(("s5Cr",), live=3); sCi, kCi = ring_next(("s5Cin",), live=4)
        vBr = sBr[:].rearrange("p (t c) -> p t c", t=32); vBi = sBi[:].rearrange("p (t c) -> p t c", t=32)
        vCr = sCr[:].rearrange("p (t c) -> p t c", t=32); vCi = sCi[:].rearrange("p (t c) -> p t c", t=32)
        def s5_B(ct):
            cht = ct // 4
            tb = tab[ct % 2]; tk = f"tab{ct % 2}"
            DMA("sp", tb[:], tabs[ct], [f"tabs{ct}"], [tk])
            pr = banks[2 + 2 * (ct % 2)]; pi = banks[3 + 2 * (ct % 2)]
            kr = f"bank{2 + 2 * (ct % 2)}"; ki = f"bank{3 + 2 * (ct % 2)}"
            MM(pr[:], vBr[:, ct, :], uT_bf[:, cht, :], True, True, [kBr, "aT"], [kr])
            MM(pi[:], vBi[:, ct, :], uT_bf[:, cht, :], True, True, [kBi, "aT"], [ki])

        s5_B(0)
        for ct in range(32):
            cht = ct // 4
            tb = tab[ct % 2]; tk = f"tab{ct % 2}"
            pr = banks[2 + 2 * (ct % 2)]; pi = banks[3 + 2 * (ct % 2)]
            kr = f"bank{2 + 2 * (ct % 2)}"; ki = f"bank{3 + 2 * (ct % 2)}"
            if ct + 1 < 32:
                s5_B(ct + 1)
            X = mm4[:, 0:2, :]; Y = mm4[:, 2:4, :]
            TT("dve", X, pr[:].unsqueeze(1).to_broadcast([128, 2, 512]), tb[:, 0:2, :], ALU.mult, [kr, tk], ["m1", "m2"])
            TT("dve", Y, pi[:].unsqueeze(1).to_broadcast([128, 2, 512]), tb[:, 0:2, :], ALU.mult, [ki, tk], ["m3", "m4"])
            TT("dve", wri[:, 0, :], mm4[:, 0, :], mm4[:, 3, :], ALU.subtract, ["m1", "m2", "m3", "m4"], ["wri"])
            TT("dve", wri[:, 1, :], mm4[:, 2, :], mm4[:, 1, :], ALU.add, ["m1", "m2", "m3", "m4"], ["wri"])
            rb = rho[:, ct:ct + 1].to_broadcast([128, 512])
            for ri in range(2):
                P.add("dve", lambda e, ri=ri, rb=rb, ct=ct: e.tensor_tensor_scan(out=zri[:, ri, :], data0=rb, data1=wri[:, ri, :],
                                                                         initial=carry[:, ri, ct:ct + 1], op0=ALU.mult, op1=ALU.add),
                      reads=["rho", "wri", "carry"], writes=["zri"])
            pp = s_bf[ct % 2]; ppk = f"s_bf{ct % 2}"
            TT("dve", pp[:, 0:2, :], zri[:, 0, :].unsqueeze(1).to_broadcast([128, 2, 512]), tb[:, 2:4, :], ALU.mult, ["zri", tk], [ppk])
            STT("dve", pp[:, 2, :], zri[:, 1, :], -1.0, tb[:, 3, :], ALU.mult, ALU.mult, ["zri", tk], [ppk])
            TT("dve", pp[:, 3, :], zri[:, 1, :], tb[:, 2, :], ALU.mult, ["zri", tk], [ppk])
            ACT(zl_all[:, 0, ct:ct + 1], zri[:, 0, 511:512], AF.Copy, ["zri"], ["zl_all"])
            ACT(zl_all[:, 1, ct:ct + 1], zri[:, 1, 511:512], AF.Copy, ["zri"], ["zl_all"])
            MM(banks[6][:], vCr[:, ct, :], pp[:, 0, :], ct % 4 == 0, False, [kCr, ppk], ["bank6"])
            MM(banks[6][:], vCr[:, ct, :], pp[:, 2, :], False, False, [kCr, ppk], ["bank6"])
            MM(banks[6][:], vCi[:, ct, :], pp[:, 1, :], False, False, [kCi, ppk], ["bank6"])
            MM(banks[6][:], vCi[:, ct, :], pp[:, 3, :], False, ct % 4 == 3, [kCi, ppk], ["bank6"])
            if ct % 4 == 3:
                t = tmpf[0]
                STT("dve", t[:], uT_bf[:, cht, :], s5d[:, cht:cht + 1], banks[6][:], ALU.mult, ALU.add, ["aT", "s5d", "bank6"], ["tmpf0"])
                ACT(yg[:, cht, :], t[:], AF.Gelu_apprx_tanh, ["tmpf0"], ["hT"])
        zr_ = zl_all[:, 0, :]; zi_ = zl_all[:, 1, :]; cl = cslast[:, 0, :]; sl_ = cslast[:, 1, :]
        TT("dve", cw6[:, 0, :], zr_, cl, ALU.mult, ["zl_all", "cslast"], ["cw6"])
        TT("dve", cw6[:, 1, :], zi_, sl_, ALU.mult, ["zl_all", "cslast"], ["cw6"])
        TT("dve", cw6[:, 2, :], zr_, sl_, ALU.mult, ["zl_all", "cslast"], ["cw6"])
        TT("dve", cw6[:, 3, :], zi_, cl, ALU.mult, ["zl_all", "cslast"], ["cw6"])
        TT("dve", carry[:, 0, :], cw6[:, 0, :], cw6[:, 1, :], ALU.subtract, ["cw6"], ["carry"])
        TT("dve", carry[:, 1, :], cw6[:, 2, :], cw6[:, 3, :], ALU.add, ["cw6"], ["carry"])
        for mp in range(4):
            slot, key = ring_next(("glu", mp))
            wv = slot[:].rearrange("p (k m c) -> p k m c", k=8, m=2)
            for mm in range(2 if S5LV >= 6 else 0):
                m = mp * 2 + mm
                pa = banks[(m % 2) * 2]; pbb = banks[(m % 2) * 2 + 1]
                ka = f"bank{(m % 2) * 2}"; kb = f"bank{(m % 2) * 2 + 1}"
                for k in range(8):
                    MM(pa[:], wv[:, k, mm, 0:128], yg[:, k, :], k == 0, k == 7, [key, "hT"], [ka])
                for k in range(8):
                    MM(pbb[:], wv[:, k, mm, 128:256], yg[:, k, :], k == 0, k == 7, [key, "hT"], [kb])
                t = tmpf[m % 2]; tkk = f"tmpf{m % 2}"
                ACT(t[:], pbb[:], AF.Sigmoid, [kb], [tkk])
                TT("dve", t[:], pa[:], t[:], ALU.mult, [ka, tkk], [tkk])
                STT("dve", xT[:, m, :], t[:], mod(l, "g1", m, s), xT[:, m, :], ALU.mult, ALU.add, [tkk, "modT", f"xT{m}"], [f"xT{m}"])

    def final_norm_out(s, c):
        for k in range(8):
            ACT(sq[k % 2][:], xT[:, k, :], AF.Square, [f"xT{k}"], [f"sq{k % 2}"])
            MM(banks[5][:], ones[:], sq[k % 2][:], k == 0, k == 7, ["ones", f"sq{k % 2}"], ["bank5"])
        ACT(rstd[:], banks[5][:], AF.Sqrt, ["bank5"], ["rstd"], bias=EPS, scale=1.0 / D)
        P.add("dve", lambda e: e.reciprocal(out=rstd[:], in_=rstd[:]), reads=["rstd"], writes=["rstd"])
        for k in range(8):
            t = tmpf[k % 2]; tk = f"tmpf{k % 2}"
            STT("dve", t[:], xT[:, k, :], fng[:, k:k + 1], rstd[:], ALU.mult, ALU.mult, [f"xT{k}", "fng", "rstd"], [tk])
            DMA("sp", outT[s, k * 128:(k + 1) * 128, c * 512:(c + 1) * 512], t[:], [tk], [])

    for s in range(NS):
        P.add("dve", lambda e: e.memset(Rst[:], 0.0), reads=[], writes=["Rst0", "Rst1"])
        P.add("dve", lambda e: e.memset(carry[:], 0.0), reads=[], writes=["carry"])
        P.add(PL, lambda e: e.memset(tails[:], 0.0), reads=[], writes=["tails"])
        for c in range(NCH):
            for k in range(8):
                DMA("sp", xT[:, k, :], xT_in[s, k * 128:(k + 1) * 128, c * 512:(c + 1) * 512], [], [f"xT{k}"])
            retention(s, c)
            stages = [("ret", None), ("ffn0", lambda: ffn(0, s)), ("s5", lambda: s5(s)), ("ffn1", lambda: ffn(1, s))]
            done = False
            for nm, fn in stages:
                if fn is not None and not done:
                    fn()
                if stop_after == nm and not done:
                    done = True
                    DMA("sp", outT[s].rearrange("(k p) t -> p k t", p=128)[:, :, c * 512:(c + 1) * 512], xT[:], [f"xT{k}" for k in range(8)], [])
                    while rs["n"] % NSLAB != 0:
                        ring_next(PLAN[rs["n"] % NSLAB])
            if not done:
                final_norm_out(s, c)
    P.emit()
    return nc


def prep_core_inputs(inp, seqs, W, consts):
    L = inp["x"].shape[1]
    NS = len(seqs)
    d = {}
    d["xT"] = np.ascontiguousarray(inp["x"][seqs].transpose(0, 2, 1))
    d["cT"] = np.ascontiguousarray(inp["c"][seqs].reshape(NS, 8, 128).transpose(2, 1, 0))
    d["pos"] = np.ascontiguousarray(inp["pos"][seqs].reshape(NS, L // 128, 128).transpose(0, 2, 1)).astype(np.int32)
    d["ada_w"] = np.ascontiguousarray(inp["ada_w"].reshape(2, 8, 128, 6144).transpose(0, 2, 1, 3))
    d["ada_b"] = np.ascontiguousarray(inp["ada_b"].reshape(2, 48, 128).transpose(2, 0, 1))
    d["wslab"] = W
    lam = np.stack([inp["s5_lam_re"][0].reshape(32, 128).T, inp["s5_lam_im"][0].reshape(32, 128).T,
                    np.repeat(inp["s5_log_dt"][0], 64).reshape(32, 128).T], axis=1)
    d["s5p"] = np.ascontiguousarray(lam.astype(np.float32))
    d["s5d"] = np.ascontiguousarray(inp["s5_d"][0].reshape(8, 128).T)
    d["conv_w"] = np.ascontiguousarray(inp["ffn_conv_w"][:, :, 0, :].reshape(2, 3, NJ, 128).transpose(3, 0, 2, 1))
    d["conv_b"] = np.ascontiguousarray(inp["ffn_conv_b"].reshape(2, NJ, 128).transpose(2, 0, 1))
    d["fng"] = np.ascontiguousarray(inp["final_norm_g"].reshape(8, 128).T)
    for k, v in consts.items():
        d[k] = v
    return {k: np.ascontiguousarray(v) for k, v in d.items()}


def run(inp, n_cores=8, stop_after=None, trace=False):
    inp = {k: np.asarray(v) for k, v in inp.items()}
    B, L, _ = inp["x"].shape
    NS = B // n_cores
    W = host_slabs(inp)
    consts, _ = host_consts()
    nc = build_program(NS, L, stop_after=stop_after)
    in_maps = [prep_core_inputs(inp, list(range(c * NS, (c + 1) * NS)), W, consts) for c in range(n_cores)]
    res = run_bass_kernel_spmd(nc, in_maps, core_ids=list(range(n_cores)), trace=trace)
    out = np.empty((B, L, D), np.float32)
    for c in range(n_cores):
        out[c * NS:(c + 1) * NS] = res.results[c]["outT"].transpose(0, 2, 1)
    return out, res


def kernel(**inputs):
    out, _ = run(inputs, n_cores=8)
    return out
```
